# Optimizing a Trainium2 kernel written in Bass

```python
import math
import jax
import jax.numpy as jnp
from jax import lax
import numpy as np

D_MODEL = 1024
BATCH = 2
SEQ = 16384
DEPTH = 1

MEM_LEN = 256
POOL_WIDTH = D_MODEL // 2
POOL_WINDOWS = (2, 4, 8, 16)
POOL_GROUPS = len(POOL_WINDOWS)
POOL_GROUP_WIDTH = POOL_WIDTH // POOL_GROUPS
DIFF_HEADS = 4
DIFF_HEAD_DIM = 64
DIFF_V_DIM = 2 * DIFF_HEAD_DIM
Q_WIDTH = DIFF_HEADS * 2 * DIFF_HEAD_DIM
ATTN_WIDTH = DIFF_HEADS * DIFF_V_DIM
MIX_WIDTH = POOL_WIDTH + ATTN_WIDTH
IN_PROJ_WIDTH = POOL_WIDTH + 2 * Q_WIDTH + ATTN_WIDTH + MIX_WIDTH
Q_BLOCK = 128
ROPE_THETA = 10000.0
CROSS_HEADS = 4
CROSS_HEAD_DIM = D_MODEL // CROSS_HEADS
D_FF = 4 * D_MODEL
NORM_EPS = 1e-6
NEG_BIG = -1e30

kernel_name = "hybrid_pool_diffattn_gated_block"


def rms_norm(x, g):
    xf = x.astype(jnp.float32)
    y = xf * lax.rsqrt(jnp.mean(xf * xf, axis=-1, keepdims=True) + NORM_EPS)
    return (y * g.astype(jnp.float32)).astype(x.dtype)


def lambda_init_fn(layer_idx):
    return 0.8 - 0.6 * math.exp(-0.3 * layer_idx)


def apply_rope(t):
    S = t.shape[1]
    dh = t.shape[-1]
    inv_freq = ROPE_THETA ** (-jnp.arange(0, dh, 2, dtype=jnp.float32) / dh)
    ang = jnp.arange(S, dtype=jnp.float32)[:, None] * inv_freq[None, :]
    cos = jnp.cos(ang)[None, :, None, None, :].astype(t.dtype)
    sin = jnp.sin(ang)[None, :, None, None, :].astype(t.dtype)
    t1, t2 = jnp.split(t, 2, axis=-1)
    return jnp.concatenate([t1 * cos - t2 * sin, t2 * cos + t1 * sin], axis=-1)


def causal_multiscale_pool(u, w_pool, pool_scale):
    B, S, _ = u.shape
    ug = u.reshape(B, S, POOL_GROUPS, POOL_GROUP_WIDTH)
    pos = jnp.arange(S)
    outs = []
    for g, w in enumerate(POOL_WINDOWS):
        xg = ug[:, :, g].astype(jnp.float32)
        c = jnp.cumsum(xg, axis=1)
        lag = jnp.pad(c, ((0, 0), (w, 0), (0, 0)))[:, :S]
        cnt = jnp.minimum(pos + 1, w).astype(jnp.float32)[None, :, None]
        outs.append((c - lag) / cnt - xg)
    z = jnp.stack(outs, axis=2).astype(u.dtype)
    y = jnp.einsum('bsgc,gcd->bsgd', z, w_pool).reshape(B, S, POOL_WIDTH)
    return y * pool_scale


def diff_attention(q, k, v, lam):
    B, S, H, _, Dh = q.shape
    E = v.shape[-1]
    nb = S // Q_BLOCK
    scale = Dh ** -0.5
    qb = q.reshape(B, nb, Q_BLOCK, H, 2, Dh).transpose(1, 0, 2, 3, 4, 5)
    kpos = jnp.arange(S)

    def one_block(args):
        q_blk, i = args
        qpos = i * Q_BLOCK + jnp.arange(Q_BLOCK)
        s = jnp.einsum('bqhcd,bkhcd->bhcqk', q_blk, k,
                       preferred_element_type=jnp.float32) * scale
        mask = kpos[None, :] <= qpos[:, None]
        p = jax.nn.softmax(jnp.where(mask, s, NEG_BIG), axis=-1)
        a = p[:, :, 0] - lam * p[:, :, 1]
        return jnp.einsum('bhqk,bkhe->bqhe', a.astype(v.dtype), v)

    o = lax.map(one_block, (qb, jnp.arange(nb)))
    return o.transpose(1, 0, 2, 3, 4).reshape(B, S, H, E)


def memory_cross_attention(h, mem_n, w_cq, w_ckv, w_co):
    B, S, _ = h.shape
    M = mem_n.shape[1]
    q = (h @ w_cq).reshape(B, S, CROSS_HEADS, CROSS_HEAD_DIM)
    kv = (mem_n @ w_ckv).reshape(B, M, 2, CROSS_HEADS, CROSS_HEAD_DIM)
    k, v = kv[:, :, 0], kv[:, :, 1]
    s = jnp.einsum('bshd,bmhd->bhsm', q, k,
                   preferred_element_type=jnp.float32) * CROSS_HEAD_DIM ** -0.5
    p = jax.nn.softmax(s, axis=-1).astype(v.dtype)
    o = jnp.einsum('bhsm,bmhd->bshd', p, v).reshape(B, S, D_MODEL)
    return o @ w_co


def setup_inputs(seed: int = 0) -> dict:
    key = jax.random.key(seed)
    ks = jax.random.split(key, 24)
    f32 = jnp.float32

    def nrm(k, shape, scale):
        return jax.random.normal(k, shape, f32) * scale

    def gain(k, shape):
        return 1.0 + 0.02 * jax.random.normal(k, shape, f32)

    L = DEPTH
    return {
        "x": jax.random.normal(ks[0], (BATCH, SEQ, D_MODEL), f32),
        "mem": jax.random.normal(ks[1], (BATCH, MEM_LEN, D_MODEL), f32),
        "g_mix": gain(ks[2], (L, D_MODEL)),
        "w_in": nrm(ks[3], (L, D_MODEL, IN_PROJ_WIDTH), D_MODEL ** -0.5),
        "w_pool": nrm(ks[4], (L, POOL_GROUPS, POOL_GROUP_WIDTH, POOL_GROUP_WIDTH), POOL_GROUP_WIDTH ** -0.5),
        "pool_scale": gain(ks[5], (L, POOL_WIDTH)),
        "lambda_q1": nrm(ks[6], (L, DIFF_HEAD_DIM), 0.1),
        "lambda_k1": nrm(ks[7], (L, DIFF_HEAD_DIM), 0.1),
        "lambda_q2": nrm(ks[8], (L, DIFF_HEAD_DIM), 0.1),
        "lambda_k2": nrm(ks[9], (L, DIFF_HEAD_DIM), 0.1),
        "g_subln": gain(ks[10], (L, DIFF_V_DIM)),
        "w_out": nrm(ks[11], (L, MIX_WIDTH, D_MODEL), MIX_WIDTH ** -0.5),
        "g_cross": gain(ks[12], (L, D_MODEL)),
        "g_mem": gain(ks[13], (L, D_MODEL)),
        "w_cq": nrm(ks[14], (L, D_MODEL, D_MODEL), D_MODEL ** -0.5),
        "w_ckv": nrm(ks[15], (L, D_MODEL, 2 * D_MODEL), D_MODEL ** -0.5),
        "w_co": nrm(ks[16], (L, D_MODEL, D_MODEL), D_MODEL ** -0.5),
        "g_mlp": gain(ks[17], (L, D_MODEL)),
        "w_up": nrm(ks[18], (L, D_MODEL, D_FF), D_MODEL ** -0.5),
        "w_down": nrm(ks[19], (L, D_FF, D_MODEL), D_FF ** -0.5),
        "g_final": gain(ks[20], (D_MODEL,)),
    }


def reference(x, mem, g_mix, w_in, w_pool, pool_scale, lambda_q1, lambda_k1,
              lambda_q2, lambda_k2, g_subln, w_out, g_cross, g_mem, w_cq, w_ckv,
              w_co, g_mlp, w_up, w_down, g_final):
    B, S, _ = x.shape
    splits = [POOL_WIDTH, POOL_WIDTH + Q_WIDTH, POOL_WIDTH + 2 * Q_WIDTH,
              POOL_WIDTH + 2 * Q_WIDTH + ATTN_WIDTH]
    for l in range(DEPTH):
        lam_init = lambda_init_fn(l)
        h = rms_norm(x, g_mix[l])
        proj = h @ w_in[l]
        u, q, k, v, gate_logits = jnp.split(proj, splits, axis=-1)
        y_pool = causal_multiscale_pool(u, w_pool[l], pool_scale[l])
        q = apply_rope(q.reshape(B, S, DIFF_HEADS, 2, DIFF_HEAD_DIM))
        k = apply_rope(k.reshape(B, S, DIFF_HEADS, 2, DIFF_HEAD_DIM))
        v = v.reshape(B, S, DIFF_HEADS, DIFF_V_DIM)
        lam = (jnp.exp(jnp.sum(lambda_q1[l].astype(jnp.float32) * lambda_k1[l].astype(jnp.float32)))
               - jnp.exp(jnp.sum(lambda_q2[l].astype(jnp.float32) * lambda_k2[l].astype(jnp.float32)))
               + lam_init)
        o = diff_attention(q, k, v, lam)
        o = rms_norm(o, g_subln[l]) * (1.0 - lam_init)
        y_attn = o.reshape(B, S, ATTN_WIDTH)
        gates = jax.nn.sigmoid(gate_logits.astype(jnp.float32)).astype(x.dtype)
        mixed = jnp.concatenate([y_pool, y_attn], axis=-1) * gates
        x = x + mixed @ w_out[l]
        hc = rms_norm(x, g_cross[l])
        mem_n = rms_norm(mem, g_mem[l])
        x = x + memory_cross_attention(hc, mem_n, w_cq[l], w_ckv[l], w_co[l])
        hm = rms_norm(x, g_mlp[l])
        x = x + jnp.square(jax.nn.relu(hm @ w_up[l])) @ w_down[l]
    return rms_norm(x, g_final)
```

```python
import bisect
import math
import numpy as np
import concourse.bass as bass
import concourse.mybir as mybir
from concourse.bass_utils import run_bass_kernel_spmd

F32 = mybir.dt.float32
BF16 = mybir.dt.bfloat16
I32 = mybir.dt.int32
U8 = mybir.dt.uint8
ALU = mybir.AluOpType
AF = mybir.ActivationFunctionType
AX = mybir.AxisListType

D = 1024
CH = 512
NCORE = 8
CPB = 4
HALO = 16
MEM = 256
DFF = 4096
EPS = 1e-6
LAM_INIT = 0.8 - 0.6 * math.exp(0.0)
TWO_PI = 2.0 * math.pi
CW1 = 6.28125
CW2 = TWO_PI - CW1
PI_LO = 3.1415925
NDMA = 48
NSWDMA = 8
ESZ = {F32: 4, BF16: 2, I32: 4, U8: 1}


class Arena:
    def __init__(self, size):
        self.b = [0, size]
        self.w = [None]
        self.r = [{}]

    def _split(self, x):
        i = bisect.bisect_right(self.b, x) - 1
        if self.b[i] == x:
            return i
        self.b.insert(i + 1, x)
        self.w.insert(i + 1, self.w[i])
        self.r.insert(i + 1, dict(self.r[i]))
        return i + 1

    def rng(self, lo, hi):
        i = self._split(lo)
        j = self._split(hi)
        return range(i, j)


class Buf:
    def __init__(self, arena, lo, hi, ap, esz=1):
        self.arena, self.lo, self.hi, self.ap, self.esz = arena, lo, hi, ap, esz

    def c(self, c0, c1):
        return Buf(self.arena, self.lo + c0 * self.esz, self.lo + c1 * self.esz,
                   self.ap[:, c0:c1], self.esz)

    def i(self, k):
        n = self.ap.shape[2]
        return Buf(self.arena, self.lo + k * n * self.esz, self.lo + (k + 1) * n * self.esz,
                   self.ap[:, k, :], self.esz)


class Op:
    __slots__ = ("eng", "fn", "deps", "idx", "inc", "val", "is_dma", "sem_id", "key")


class Sched:
    ENGS = ("pe", "act", "dve", "pool", "sp")

    def __init__(self):
        self.ops = {e: [] for e in self.ENGS}
        self.dma_rr = 0
        self.sw_rr = 0
        self.dma_last = [None] * NDMA
        self.dma_cnt = [0] * NDMA
        self.nuid = 0

    def add(self, eng, fn, reads=(), writes=(), dma=False):
        op = Op()
        op.eng, op.fn, op.is_dma, op.inc, op.val, op.sem_id = eng, fn, dma, False, 0, -1
        op.idx = len(self.ops[eng])
        if dma:
            self.nuid += 1
            op.key = ("d", self.nuid)
        else:
            op.key = eng
        deps = {}

        def dep(o, raw):
            if o is None or o is op:
                return
            if (not dma) and (not o.is_dma) and o.eng == eng:
                if eng == "pe":
                    return
            p = deps.get(o.key)
            if p is None or o.idx > p.idx:
                deps[o.key] = o

        for r in reads:
            ar = r.arena
            for s in ar.rng(r.lo, r.hi):
                dep(ar.w[s], True)
        for r in writes:
            ar = r.arena
            for s in ar.rng(r.lo, r.hi):
                dep(ar.w[s], False)
                for o in ar.r[s].values():
                    dep(o, False)
        for r in reads:
            ar = r.arena
            for s in ar.rng(r.lo, r.hi):
                ar.r[s][op.key] = op
        for r in writes:
            ar = r.arena
            for s in ar.rng(r.lo, r.hi):
                ar.w[s] = op
                ar.r[s] = {}
        if dma:
            if eng == "pool":
                sid = NDMA - NSWDMA + self.sw_rr % NSWDMA
                self.sw_rr += 1
            else:
                sid = self.dma_rr % (NDMA - NSWDMA)
                self.dma_rr += 1
            prev = self.dma_last[sid]
            if prev is not None:
                deps[prev.key] = prev
            self.dma_last[sid] = op
            self.dma_cnt[sid] += 16
            op.sem_id = sid
            op.val = self.dma_cnt[sid]
        op.deps = deps
        self.ops[eng].append(op)
        return op

    def emit(self, nc, eng_sems, dma_sems, block):
        for e in self.ENGS:
            for op in self.ops[e]:
                for d in op.deps.values():
                    if not d.is_dma:
                        d.inc = True
        for e in self.ENGS:
            c = 0
            for op in self.ops[e]:
                if not op.is_dma and op.inc:
                    c += 1
                    op.val = c

        def run(e, h):
            seen = {}
            for op in self.ops[e]:
                waits = {}
                for d in op.deps.values():
                    k = ("d", d.sem_id) if d.is_dma else ("e", d.eng)
                    if waits.get(k, 0) < d.val:
                        waits[k] = d.val
                for k, v in waits.items():
                    if seen.get(k, 0) >= v:
                        continue
                    seen[k] = v
                    sem = dma_sems[k[1]] if k[0] == "d" else eng_sems[k[1]]
                    h.wait_ge(sem, v)
                ins = op.fn(h)
                if op.is_dma:
                    ins.then_inc(dma_sems[op.sem_id], 16)
                elif op.inc:
                    ins.then_inc(eng_sems[e], 1)
            if e == "sp":
                for sid in range(NDMA):
                    if self.dma_cnt[sid] > 0 and seen.get(("d", sid), 0) < self.dma_cnt[sid]:
                        h.wait_ge(dma_sems[sid], self.dma_cnt[sid])

        @block.tensor
        def _(h):
            run("pe", h)

        @block.scalar
        def _(h):
            run("act", h)

        @block.vector
        def _(h):
            run("dve", h)

        @block.gpsimd
        def _(h):
            run("pool", h)

        @block.sync
        def _(h):
            run("sp", h)


def build(S, debug=False):
    NCH = S // CH
    M = NCH // CPB
    NT = S // 128
    nc = bass.Bass("TRN2", target_bir_lowering=False)

    def din(name, shape):
        return nc.dram_tensor(name, list(shape), F32, kind="ExternalInput").ap()

    xb = din("xb", [S, D])
    xo = din("xo", [M, HALO + CH, D])
    posA = din("posA", [NCH, CH])
    posB = din("posB", [M, CH])
    consts = din("consts", [128, 4])
    thr_d = din("thr", [128, 16])
    qkb_d = din("qkbase", [128, CH])
    ident_d = din("ident", [128, 128])
    memb = din("memb", [MEM, D])
    w_in = din("w_in", [D, 3072])
    w_qsw = din("w_qsw", [D, 512])
    w_ksw = din("w_ksw", [D, 512])
    w_pool = din("w_pool", [4, 128, 128])
    w_out = din("w_out", [D, D])
    w_cq = din("w_cq", [D, D])
    w_ckv = din("w_ckv", [D, 2 * D])
    w_co = din("w_co", [D, D])
    w_up = din("w_up", [D, DFF])
    w_down = din("w_down", [DFF, D])
    gcols_d = din("gcols", [128, 32])
    pscale_d = din("pscale", [128, 4])
    gfin_d = din("gfin", [1, D])
    gsub_d = din("gsub", [1, 128])
    lam_d = din("lam", [1, 256])
    out_d = nc.dram_tensor("out", [M * CH, D], F32, kind="ExternalOutput").ap()
    skind = "ExternalOutput" if debug else "Internal"
    KTs = nc.dram_tensor("KTs", [4, 128, S], BF16, kind=skind).ap()
    Vs = nc.dram_tensor("Vs", [4, 128, NT, 129], BF16, kind=skind).ap()
    slab_defs = [
        ("qq", 1024, [(w_in, 0, 512, 512, 0), (w_qsw, 0, 0, 512, 512)]),
        ("ug", 1024, [(w_in, 0, 0, 512, 0), (w_in, 0, 2048, 512, 512)]),
        ("g2", 512, [(w_in, 0, 2560, 512, 0)]),
        ("out", 1024, [(w_out, 0, 0, 1024, 0)]),
        ("cq", 1024, [(w_cq, 0, 0, 1024, 0)]),
        ("co", 1024, [(w_co, 0, 0, 1024, 0)]),
    ]
    for i in range(4):
        slab_defs.append(("up%d" % i, 1024, [(w_up, 0, 1024 * i, 1024, 0)]))
    for i in range(4):
        slab_defs.append(("dn%d" % i, 1024, [(w_down, 1024 * i, 0, 1024, 0)]))
    NSLAB = len(slab_defs)
    wbf = [nc.dram_tensor("wbf_" + sd[0], [128, 8, sd[1]], BF16, kind="Internal").ap()
           for sd in slab_defs]
    dbg = {}
    if debug:
        dbg["x1"] = nc.dram_tensor("dbg_x1", [M * CH, D], F32, kind="ExternalOutput").ap()
        dbg["mixT"] = nc.dram_tensor("dbg_mixT", [M, 128, 8, CH], BF16, kind="ExternalOutput").ap()
        dbg["x2"] = nc.dram_tensor("dbg_x2", [M * CH, D], F32, kind="ExternalOutput").ap()
        dbg["acc"] = nc.dram_tensor("dbg_acc", [4, 128, 8, 129], F32, kind="ExternalOutput").ap()
        dbg["yt"] = nc.dram_tensor("dbg_yt", [4, 128, CH], BF16, kind="ExternalOutput").ap()
        dbg["q"] = nc.dram_tensor("dbg_q", [128, 4, CH], BF16, kind="ExternalOutput").ap()

    SB_BYTES = 175 * 1024
    sched = Sched()
    sb_arena = Arena(SB_BYTES)
    ps_arena = Arena(8 * 2048)
    dram_arenas = {}

    from contextlib import ExitStack
    with ExitStack() as es:
        big = es.enter_context(nc.sbuf_tensor("big", [128, SB_BYTES], U8))
        banks = [es.enter_context(nc.psum_tensor("bank%d" % i, [128, 512], F32)) for i in range(8)]
        eng_sems = {e: es.enter_context(nc.semaphore("sem_" + e)) for e in Sched.ENGS}
        dma_sems = [es.enter_context(nc.semaphore("dsem%d" % i)) for i in range(NDMA)]

        top = [0]

        def sb(shape, dt):
            esz = ESZ[dt]
            n = 1
            for v in shape[1:]:
                n *= v
            nb = (n * esz + 63) // 64 * 64
            lo = top[0]
            top[0] += nb
            assert top[0] <= SB_BYTES, ("SBUF overflow", top[0])
            ap = big[0:shape[0], lo:lo + n * esz].bitcast(dt)
            if len(shape) == 3:
                ap = ap.rearrange("p (a b) -> p a b", a=shape[1])
            elif len(shape) == 4:
                ap = ap.rearrange("p (a b c) -> p a b c", a=shape[1], b=shape[2])
            return Buf(sb_arena, lo, lo + n * esz, ap, esz)

        def ps(bank, c0=0, c1=512, dt=F32):
            esz = ESZ[dt]
            ap = banks[bank][:, :]
            if dt != F32:
                ap = ap.bitcast(dt)
            b_ = Buf(ps_arena, bank * 2048, (bank + 1) * 2048, ap[:, c0:c1], esz)
            b_.c = lambda a0, a1, b_=b_: Buf(ps_arena, b_.lo, b_.hi, b_.ap[:, a0:a1], esz)
            return b_

        def dr(name, lo, hi):
            if name not in dram_arenas:
                dram_arenas[name] = Arena(1 << 40)
            return Buf(dram_arenas[name], lo, hi, None)

        A = sched.add

        def dma(q, out, in_, reads, writes):
            return A(q, lambda h, o=out, i=in_: h.dma_start(out=o, in_=i), reads, writes, dma=True)

        ident = sb([128, 128], BF16)
        ones = sb([128, 128], BF16)
        cst = sb([128, 4], F32)
        thr = sb([128, 16], F32)
        qkb = sb([128, CH], F32)
        gcols = sb([128, 32], F32)
        pscale = sb([128, 4], F32)
        gfin = sb([128, D], F32)
        gsub = sb([128, 128], F32)
        lamt = sb([128, 256], F32)
        lamw = sb([128, 8], F32)
        neglam = sb([128, 1], F32)
        wpool = sb([128, 4, 128], BF16)
        KcT = sb([128, 8, MEM], BF16)
        Vc = sb([128, 2, D], BF16)
        ssn = sb([128, 8], F32)
        lnv = sb([128, 8], F32)
        rstd = sb([128, 8], F32)
        junks = [sb([128, D], BF16) for _ in range(4)]
        junk_n = [0]

        dma("pool", ident.ap, ident_d, [], [ident])
        dma("sp", cst.ap, consts, [], [cst])
        dma("sp", thr.ap, thr_d, [], [thr])
        dma("sp", qkb.ap, qkb_d, [], [qkb])
        dma("sp", gcols.ap, gcols_d, [], [gcols])
        dma("sp", pscale.ap, pscale_d, [], [pscale])
        dma("sp", gfin.ap, gfin_d[0:1, :].broadcast_to([128, D]), [], [gfin])
        dma("sp", gsub.ap, gsub_d[0:1, :].broadcast_to([128, 128]), [], [gsub])
        dma("sp", lamt.ap, lam_d[0:1, :].broadcast_to([128, 256]), [], [lamt])
        dma("pool", wpool.ap, w_pool.rearrange("g c d -> c g d"), [], [wpool])
        A("dve", lambda h: h.memset(ones.ap, 1.0), [], [ones])
        A("dve", lambda h: h.memset(ssn.ap, 1.0), [], [ssn])
        A("dve", lambda h: h.tensor_tensor(out=lamt.ap[:, 0:64], in0=lamt.ap[:, 0:64],
                                           in1=lamt.ap[:, 64:128], op=ALU.mult), [lamt], [lamt])
        A("dve", lambda h: h.tensor_tensor(out=lamt.ap[:, 128:192], in0=lamt.ap[:, 128:192],
                                           in1=lamt.ap[:, 192:256], op=ALU.mult), [lamt], [lamt])
        A("dve", lambda h: h.reduce_sum(out=lamw.ap[:, 0:1], in_=lamt.ap[:, 0:64], axis=AX.X), [lamt], [lamw])
        A("dve", lambda h: h.reduce_sum(out=lamw.ap[:, 1:2], in_=lamt.ap[:, 128:192], axis=AX.X), [lamt], [lamw])
        A("act", lambda h: h.activation(out=lamw.ap[:, 2:4], in_=lamw.ap[:, 0:2], func=AF.Exp), [lamw], [lamw])
        A("dve", lambda h: h.tensor_tensor(out=lamw.ap[:, 4:5], in0=lamw.ap[:, 3:4], in1=lamw.ap[:, 2:3],
                                           op=ALU.subtract), [lamw], [lamw])
        A("dve", lambda h: h.tensor_scalar(out=neglam.ap, in0=lamw.ap[:, 4:5], scalar1=-LAM_INIT, scalar2=None,
                                           op0=ALU.add), [lamw], [neglam])
        A("dve", lambda h: h.tensor_scalar(out=gsub.ap, in0=gsub.ap, scalar1=1.0 - LAM_INIT, scalar2=None,
                                           op0=ALU.mult), [gsub], [gsub])

        wbf_bufs = []
        for s, (nm, ncol, parts) in enumerate(slab_defs):
            b = dr("wbf%d" % s, 0, 1)
            wbf_bufs.append(b)
            for (src, r0, c0, n_, d0) in parts:
                dma("pool", wbf[s][:, :, d0:d0 + n_],
                    src[r0:r0 + 1024, c0:c0 + n_].rearrange("(k p) c -> p k c", p=128), [], [b])

        persist_top = top[0]

        def rms_stats(tiles, n_in):
            for k, (xt, P) in enumerate(tiles):
                junk = junks[junk_n[0] % 4]
                junk_n[0] += 1
                A("act", lambda h, xt=xt, P=P, k=k, junk=junk: h.activation(
                    out=junk.ap[0:P, 0:n_in], in_=xt.ap[0:P, :], func=AF.Square,
                    accum_out=ssn.ap[0:P, k:k + 1]), [xt], [junk, ssn.c(k, k + 1)])
            n = len(tiles)
            A("act", lambda h: h.activation(out=lnv.ap[:, 0:n], in_=ssn.ap[:, 0:n], func=AF.Ln,
                                            bias=epsc.ap[:, 0:1], scale=1.0 / n_in), [ssn, epsc], [lnv])
            A("act", lambda h: h.activation(out=rstd.ap[:, 0:n], in_=lnv.ap[:, 0:n], func=AF.Exp, scale=-0.5),
              [lnv], [rstd])

        epsc = sb([128, 1], F32)
        A("dve", lambda h: h.memset(epsc.ap, EPS), [], [epsc])
        halfpi = sb([128, 1], F32)
        A("dve", lambda h: h.memset(halfpi.ap, math.pi / 2.0), [], [halfpi])
        persist_top = top[0]

        def norm_T2(tiles, gofs, xn_bufs, tr_regions, hT, width, evac, halo=None):
            allt = [(t[0], t[1]) for t in tiles]
            if halo is not None:
                allt.append((halo[0], halo[1]))
            rms_stats(allt, D)
            for k, (xt, P) in enumerate(allt):
                xn = xn_bufs[k]
                A("dve", lambda h, xt=xt, P=P, k=k, xn=xn: h.tensor_scalar(
                    out=xn.ap[0:P, :], in0=xt.ap[0:P, :], scalar1=rstd.ap[0:P, k:k + 1], scalar2=None,
                    op0=ALU.mult), [xt, rstd.c(k, k + 1)], [xn])
            for fc in range(8):
                reg = tr_regions[fc % len(tr_regions)]
                for k, (xt, P, col0) in enumerate(tiles):
                    xn = xn_bufs[k]
                    dst = reg.c(col0, col0 + P)
                    A("pe", lambda h, xn=xn, P=P, fc=fc, dst=dst: h.transpose(
                        out=dst.ap, in_=xn.ap[0:P, fc * 128:(fc + 1) * 128], identity=ident.ap[0:P, 0:P]),
                      [xn, ident], [dst])
                src = reg.c(0, width)
                dstb = hT.i(fc)
                g = gcols.c(gofs + fc, gofs + fc + 1)
                e = evac[fc % len(evac)]
                if e == "act":
                    A("act", lambda h, src=src, dstb=dstb, g=g: h.activation(
                        out=dstb.ap, in_=src.ap, func=AF.Copy, scale=g.ap), [src, g], [dstb])
                else:
                    A("dve", lambda h, src=src, dstb=dstb, g=g: h.tensor_scalar(
                        out=dstb.ap, in0=src.ap, scalar1=g.ap, scalar2=None, op0=ALU.mult), [src, g], [dstb])
            if halo is not None:
                xt, P, trh, hTh = halo
                xn = xn_bufs[len(tiles)]
                for fc in range(8):
                    dst = trh.c(fc * HALO, (fc + 1) * HALO)
                    A("pe", lambda h, xn=xn, P=P, fc=fc, dst=dst: h.transpose(
                        out=dst.ap, in_=xn.ap[0:P, fc * 128:(fc + 1) * 128], identity=ident.ap[0:P, 0:P]),
                      [xn, ident], [dst])
                for fc in range(8):
                    src = trh.c(fc * HALO, (fc + 1) * HALO)
                    dstb = hTh.i(fc)
                    g = gcols.c(gofs + fc, gofs + fc + 1)
                    A("dve", lambda h, src=src, dstb=dstb, g=g: h.tensor_scalar(
                        out=dstb.ap, in0=src.ap, scalar1=g.ap, scalar2=None, op0=ALU.mult), [src, g], [dstb])

        def rope_tables(pos_row, tb):
            dma("sp", tb["pos"].ap, pos_row.broadcast_to([128, CH]), [], [tb["pos"]])
            A("pool", lambda h: h.tensor_scalar(out=tb["ang"].ap, in0=tb["pos"].ap, scalar1=cst.ap[:, 0:1],
                                                scalar2=None, op0=ALU.mult), [tb["pos"], cst], [tb["ang"]])
            A("pool", lambda h: h.tensor_scalar(out=tb["v"].ap, in0=tb["pos"].ap, scalar1=cst.ap[:, 1:2],
                                                scalar2=None, op0=ALU.mult), [tb["pos"], cst], [tb["v"]])
            A("dve", lambda h: h.tensor_copy(out=tb["vi"].ap, in_=tb["v"].ap), [tb["v"]], [tb["vi"]])
            A("dve", lambda h: h.tensor_copy(out=tb["v"].ap, in_=tb["vi"].ap), [tb["vi"]], [tb["v"]])
            A("dve", lambda h: h.scalar_tensor_tensor(out=tb["y"].ap, in0=tb["v"].ap, scalar=-CW1, in1=tb["ang"].ap,
                                                      op0=ALU.mult, op1=ALU.add), [tb["v"], tb["ang"]], [tb["y"]])
            A("dve", lambda h: h.scalar_tensor_tensor(out=tb["ang"].ap, in0=tb["v"].ap, scalar=-CW2, in1=tb["y"].ap,
                                                      op0=ALU.mult, op1=ALU.add), [tb["v"], tb["y"]], [tb["ang"]])
            A("pool", lambda h: h.tensor_scalar(out=tb["y"].ap, in0=tb["ang"].ap, scalar1=PI_LO, scalar2=-PI_LO,
                                                op0=ALU.min, op1=ALU.max), [tb["ang"]], [tb["y"]])
            A("act", lambda h: h.activation(out=tb["sin"].ap, in_=tb["y"].ap, func=AF.Sin, scale=cst.ap[:, 2:3]),
              [tb["y"], cst], [tb["sin"]])
            A("dve", lambda h: h.scalar_tensor_tensor(out=tb["v"].ap, in0=tb["y"].ap, scalar=-1.0, in1=tb["y"].ap,
                                                      op0=ALU.mult, op1=ALU.max),
              [tb["y"]], [tb["v"]])
            A("act", lambda h: h.activation(out=tb["cos"].ap, in_=tb["v"].ap, func=AF.Sin, scale=-1.0,
                                            bias=halfpi.ap[:, 0:1]), [tb["v"], halfpi], [tb["cos"]])

        def mm_group(out, pairs, tile_position=None, start=True, stop=True, skip=False):
            n = len(pairs)
            for k, (l, r) in enumerate(pairs):
                kw = {}
                if tile_position is not None:
                    kw["tile_position"] = tile_position
                if skip:
                    kw["skip_group_check"] = True
                A("pe", lambda h, l=l, r=r, k=k, kw=kw: h.matmul(
                    out.ap, lhsT=l.ap, rhs=r.ap, start=(start and k == 0), stop=(stop and k == n - 1), **kw),
                  [l, r], [out])

        def rope_apply(ps_a, ps_b, tb, tmpa, tmpb, dst):
            A("dve", lambda h: h.tensor_tensor(out=tmpa.ap, in0=ps_a.ap, in1=tb["cos"].ap, op=ALU.mult),
              [ps_a, tb["cos"]], [tmpa])
            A("dve", lambda h: h.tensor_tensor(out=tmpb.ap, in0=ps_b.ap, in1=tb["sin"].ap, op=ALU.mult),
              [ps_b, tb["sin"]], [tmpb])
            A("pool", lambda h: h.tensor_tensor(out=dst.ap, in0=tmpa.ap, in1=tmpb.ap, op=ALU.add),
              [tmpa, tmpb], [dst])

        def mk_tables(keys=("pos", "ang", "v", "vi", "y", "cos", "sin")):
            return {k: sb([128, CH], I32 if k == "vi" else F32) for k in keys}

        top[0] = persist_top
        wckv = sb([128, 8, 2 * D], BF16)
        memx = sb([128, 2, D], F32)
        memxn = [sb([128, D], BF16) for _ in range(2)]
        memT = sb([128, 8, MEM], BF16)
        dma("pool", wckv.ap[:, :, 0:1024], w_ckv[:, 0:1024].rearrange("(k p) c -> p k c", p=128), [], [wckv])
        dma("pool", wckv.ap[:, :, 1024:2048], w_ckv[:, 1024:2048].rearrange("(k p) c -> p k c", p=128), [], [wckv])
        dma("sp", memx.ap, memb.rearrange("(t p) d -> p t d", p=128), [], [memx])
        trm = [ps(0, 0, 512, BF16), ps(7, 0, 512, BF16)]
        norm_T2([(memx.i(0), 128, 0), (memx.i(1), 128, 128)], 24, memxn, trm, memT, MEM, ["act", "dve"])
        for oc in range(8):
            o = ps(1 + oc % 2, 0, MEM)
            mm_group(o, [(Buf(sb_arena, wckv.lo, wckv.hi, wckv.ap[:, kc, oc * 128:(oc + 1) * 128]), memT.i(kc))
                         for kc in range(8)])
            dst = KcT.i(oc)
            A("act", lambda h, o=o, dst=dst: h.activation(out=dst.ap, in_=o.ap, func=AF.Copy), [o], [dst])
        for mt in range(2):
            for hf in range(2):
                o = ps(3 + (mt * 2 + hf) % 2)
                mm_group(o, [(Buf(sb_arena, memT.lo, memT.hi, memT.ap[:, kc, mt * 128:(mt + 1) * 128]),
                              Buf(sb_arena, wckv.lo, wckv.hi, wckv.ap[:, kc, 1024 + hf * 512:1024 + (hf + 1) * 512]))
                             for kc in range(8)])
                dst = Buf(sb_arena, Vc.lo, Vc.hi, Vc.ap[:, mt, hf * 512:(hf + 1) * 512])
                A("dve", lambda h, o=o, dst=dst: h.tensor_copy(out=dst.ap, in_=o.ap), [o], [dst])

        top[0] = persist_top
        wkv = sb([128, 8, 1536], BF16)
        xa = [sb([128, 4, D], F32) for _ in range(2)]
        xnA = [sb([128, D], BF16) for _ in range(4)]
        hTA = [sb([128, 8, CH], BF16) for _ in range(2)]
        kst = [[sb([128, CH], BF16) for _ in range(4)] for _ in range(2)]
        vst = [sb([128, 4, 4, 129], BF16) for _ in range(2)]
        tmpA = [sb([128, CH], F32) for _ in range(2)]
        tmpB = [sb([128, CH], F32) for _ in range(2)]
        tabs = [mk_tables() for _ in range(2)]
        dma("pool", wkv.ap[:, :, 0:512], w_in[:, 1024:1536].rearrange("(k p) c -> p k c", p=128), [], [wkv])
        dma("pool", wkv.ap[:, :, 512:1024], w_ksw.rearrange("(k p) c -> p k c", p=128), [], [wkv])
        dma("pool", wkv.ap[:, :, 1024:1536], w_in[:, 1536:2048].rearrange("(k p) c -> p k c", p=128), [], [wkv])
        for v in vst:
            A("pool", lambda h, v=v: h.memset(v.ap, 1.0), [], [v])
        trA = [ps(0, 0, 512, BF16), ps(1, 0, 512, BF16)]

        def wsl(wb, kc, c0, c1):
            return Buf(sb_arena, wb.lo, wb.hi, wb.ap[:, kc, c0:c1])

        def load_xa(c):
            dma("sp", xa[c % 2].ap, xb[c * CH:(c + 1) * CH, :].rearrange("(t p) d -> p t d", p=128), [], [xa[c % 2]])

        load_xa(0)
        for c in range(NCH):
            sl = c % 2
            if c + 1 < NCH:
                load_xa(c + 1)
            tb = tabs[sl]
            rope_tables(posA[c:c + 1, :], tb)
            hT = hTA[sl]
            norm_T2([(xa[sl].i(t), 128, t * 128) for t in range(4)], 0, xnA, trA, hT, CH, ["act", "dve"])
            for hh in range(4):
                pk = ps(2 + 2 * (hh % 2))
                pks = ps(3 + 2 * (hh % 2))
                mm_group(pk, [(wsl(wkv, kc, hh * 128, (hh + 1) * 128), hT.i(kc)) for kc in range(8)])
                mm_group(pks, [(wsl(wkv, kc, 512 + hh * 128, 512 + (hh + 1) * 128), hT.i(kc)) for kc in range(8)])
                rope_apply(pk, pks, tb, tmpA[hh % 2], tmpB[hh % 2], kst[sl][hh])
                dma("sp", KTs[hh, :, c * CH:(c + 1) * CH], kst[sl][hh].ap, [kst[sl][hh]],
                    [dr("KT%d" % hh, c * CH, (c + 1) * CH)])
            for t in range(4):
                pv = ps(6 + t % 2)
                mm_group(pv, [(Buf(sb_arena, hT.lo, hT.hi, hT.ap[:, kc, t * 128:(t + 1) * 128]),
                               wsl(wkv, kc, 1024, 1536)) for kc in range(8)])
                dst = Buf(sb_arena, vst[sl].lo, vst[sl].hi, vst[sl].ap[:, t, :, 0:128])
                A("act", lambda h, pv=pv, dst=dst: h.activation(
                    out=dst.ap, in_=pv.ap.rearrange("p (a b) -> p a b", a=4), func=AF.Copy), [pv], [dst])
            for hh in range(4):
                dma("sp", Vs[hh, :, 4 * c:4 * c + 4, :], vst[sl].ap[:, :, hh, :], [vst[sl]],
                    [dr("V%d" % hh, 4 * c, 4 * c + 4)])

        top[0] = persist_top
        xB = [sb([128, 4, D], F32) for _ in range(2)]
        xH = [sb([HALO, D], F32) for _ in range(2)]
        xnB = [sb([128, D], BF16) for _ in range(5)]
        W2 = HALO + CH
        hTB = sb([128, 8, CH], BF16)
        hTh = sb([128, 8, HALO], BF16)
        gT = sb([128, 8, CH], BF16)
        NSL = 2
        slabs = [sb([128, 8, 1024], BF16) for _ in range(NSL)]
        stage_top = top[0]
        uT = sb([128, 4, W2], F32)
        zT = sb([128, 4, CH], BF16)
        qT = sb([128, 4, CH], BF16)
        tmpA2 = [sb([128, CH], F32)]
        tmpB2 = [sb([128, CH], F32)]
        tabB = mk_tables(("pos", "cos", "sin"))
        accS = sb([128, 8, 129], F32)
        rl = sb([128, 8], F32)
        rl2n = sb([128, 4], F32)
        ot2 = [sb([128, 128], F32) for _ in range(2)]
        oo = [sb([128, 128], F32) for _ in range(4)]
        yts = [sb([128, CH], BF16) for _ in range(2)]
        r1_top = top[0]
        NKV = 2
        ktb = [sb([128, 1024], BF16) for _ in range(NKV)]
        vtb = [sb([128, 8, 129], BF16) for _ in range(NKV)]
        NPT = 3
        ptb = [[sb([128, CH], BF16) for _ in range(2)] for _ in range(NPT)]
        r1_end = top[0]
        top[0] = r1_top
        pa = sb([128, W2], F32)
        pb = sb([128, W2], F32)
        invc = sb([128, CH], F32)
        tabB.update(mk_tables(("ang", "v", "vi", "y")))
        assert top[0] <= r1_end, (top[0], r1_end)
        top[0] = r1_end
        mix_top = top[0]
        top[0] = stage_top
        qcT = sb([128, 8, CH], BF16)
        pcT = [sb([128, 2, CH], BF16) for _ in range(2)]
        lnl = sb([128, CH], F32)
        rlc = [sb([128, CH], F32) for _ in range(2)]
        ocT = sb([128, 8, CH], BF16)
        cross_top = top[0]
        top[0] = stage_top
        aT = sb([128, 32, CH], BF16)
        rtmp = [sb([128, CH], BF16) for _ in range(3)]
        mlp_top = top[0]
        top[0] = max(mix_top, cross_top, mlp_top)
        print("SBUF bytes used (phase B):", top[0])

        slab_emitted = [0]
        slab_seq = [(i, s) for i in range(M) for s in range(NSLAB)]

        def slab_get(n):
            while slab_emitted[0] <= min(n + NSL - 1, len(slab_seq) - 1):
                k = slab_emitted[0]
                s = slab_seq[k][1]
                ncol = slab_defs[s][1]
                dstb = slabs[k % NSL]
                dma("sp", dstb.ap[:, :, 0:ncol], wbf[s], [wbf_bufs[s]], [dstb])
                slab_emitted[0] += 1
            return slabs[n % NSL]

        kv_seq = []
        for i in range(M):
            for hh in range(4):
                for kb in range(2 * (i + 1)):
                    kv_seq.append((i, hh, kb))
        kv_emitted = [0]

        def kv_fetch(k):
            if k >= len(kv_seq) or k < kv_emitted[0]:
                return
            assert k == kv_emitted[0]
            i, hh, kb = kv_seq[k]
            dma("sp", ktb[k % NKV].ap, KTs[hh, :, kb * 1024:(kb + 1) * 1024],
                [dr("KT%d" % hh, kb * 1024, (kb + 1) * 1024)], [ktb[k % NKV]])
            dma("sp", vtb[k % NKV].ap, Vs[hh, :, kb * 8:(kb + 1) * 8, :],
                [dr("V%d" % hh, kb * 8, (kb + 1) * 8)], [vtb[k % NKV]])
            kv_emitted[0] += 1

        def load_xB(i):
            dma("sp", xB[i % 2].ap, xo[i, HALO:HALO + CH, :].rearrange("(t p) d -> p t d", p=128), [], [xB[i % 2]])
            dma("sp", xH[i % 2].ap, xo[i, 0:HALO, :], [], [xH[i % 2]])

        def resid_proj(xcur, srcT, wslab, banks_ring):
            n = 0
            for t in range(4):
                for hf in range(2):
                    o = ps(banks_ring[n % len(banks_ring)])
                    n += 1
                    mm_group(o, [(Buf(sb_arena, srcT.lo, srcT.hi, srcT.ap[:, kc, t * 128:(t + 1) * 128]),
                                  wsl(wslab, kc, hf * 512, (hf + 1) * 512)) for kc in range(8)])
                    xs = Buf(sb_arena, xcur.lo + (t * D + hf * 512) * 4, xcur.lo + (t * D + (hf + 1) * 512) * 4,
                             xcur.ap[:, t, hf * 512:(hf + 1) * 512], 4)
                    A("dve", lambda h, o=o, xs=xs: h.tensor_tensor(out=xs.ap, in0=o.ap, in1=xs.ap, op=ALU.add),
                      [o, xs], [xs])

        trB = [ps(7, 0, 512, BF16), ps(0, 0, 512, BF16)]
        trH = ps(6, 0, 8 * HALO, BF16)
        GB = [3, 4, 5, 6]
        slab_n = 0
        kv_n = 0
        load_xB(0)
        for i in range(M):
            xs_ = xB[i % 2]
            xh_ = xH[i % 2]
            if i + 1 < M:
                load_xB(i + 1)
            rope_tables(posB[i:i + 1, :], tabB)
            norm_T2([(xs_.i(t), 128, t * 128) for t in range(4)], 0, xnB, trB, hTB, CH, ["act", "dve"],
                    halo=(xh_, HALO, trH, hTh))
            gb = 0
            w_qq = slab_get(slab_n); slab_n += 1
            for hh in range(4):
                pq = ps(GB[gb % 4]); gb += 1
                pqs = ps(GB[gb % 4]); gb += 1
                mm_group(pq, [(wsl(w_qq, kc, hh * 128, (hh + 1) * 128), hTB.i(kc)) for kc in range(8)])
                mm_group(pqs, [(wsl(w_qq, kc, 512 + hh * 128, 512 + (hh + 1) * 128), hTB.i(kc)) for kc in range(8)])
                rope_apply(pq, pqs, tabB, tmpA2[0], tmpB2[0], qT.i(hh))
            w_ug = slab_get(slab_n); slab_n += 1
            for g in range(4):
                o = ps(GB[gb % 4]); gb += 1
                mm_group(o, [(wsl(w_ug, kc, g * 128, (g + 1) * 128), hTB.i(kc)) for kc in range(8)])
                oh = ps(GB[gb % 4], 0, HALO); gb += 1
                mm_group(oh, [(wsl(w_ug, kc, g * 128, (g + 1) * 128), hTh.i(kc)) for kc in range(8)])
                um = Buf(sb_arena, uT.lo + (g * W2 + HALO) * 4, uT.lo + (g + 1) * W2 * 4, uT.ap[:, g, HALO:W2], 4)
                uh = Buf(sb_arena, uT.lo + g * W2 * 4, uT.lo + (g * W2 + HALO) * 4, uT.ap[:, g, 0:HALO], 4)
                A("act", lambda h, o=o, um=um: h.activation(out=um.ap, in_=o.ap, func=AF.Copy), [o], [um])
                A("dve", lambda h, oh=oh, uh=uh: h.tensor_copy(out=uh.ap, in_=oh.ap), [oh], [uh])
            for gc in range(8):
                if gc == 4:
                    w_g2 = slab_get(slab_n); slab_n += 1
                wg_, c0_ = (w_ug, 512 + gc * 128) if gc < 4 else (w_g2, (gc - 4) * 128)
                o = ps(GB[gb % 4]); gb += 1
                mm_group(o, [(wsl(wg_, kc, c0_, c0_ + 128), hTB.i(kc)) for kc in range(8)])
                dst = gT.i(gc)
                A("act", lambda h, o=o, dst=dst: h.activation(out=dst.ap, in_=o.ap, func=AF.Sigmoid), [o], [dst])
            for g, wdw in enumerate((2, 4, 8, 16)):
                U = uT.i(g)
                cur = U
                step = 1
                tmp_cycle = [pa, pb]
                ti = 0
                while step * 2 < wdw:
                    nxt = tmp_cycle[ti % 2]; ti += 1
                    A("pool", lambda h, cur=cur, nxt=nxt, step=step: h.tensor_tensor(
                        out=nxt.ap[:, 2 * step - 1:W2], in0=cur.ap[:, 2 * step - 1:W2],
                        in1=cur.ap[:, step - 1:W2 - step], op=ALU.add), [cur], [nxt])
                    cur = nxt
                    step *= 2
                nxt = tmp_cycle[ti % 2]; ti += 1
                A("pool", lambda h, cur=cur, nxt=nxt, step=step: h.tensor_tensor(
                    out=nxt.ap[:, HALO:W2], in0=cur.ap[:, HALO:W2], in1=cur.ap[:, HALO - step:W2 - step], op=ALU.add),
                  [cur], [nxt])
                A("dve", lambda h, wdw=wdw: h.tensor_scalar(out=invc.ap, in0=tabB["pos"].ap, scalar1=1.0,
                                                            scalar2=float(wdw), op0=ALU.add, op1=ALU.min),
                  [tabB["pos"]], [invc])
                A("act", lambda h: h.activation(out=invc.ap, in_=invc.ap, func=AF.Ln), [invc], [invc])
                A("act", lambda h: h.activation(out=invc.ap, in_=invc.ap, func=AF.Exp, scale=-1.0), [invc], [invc])
                A("dve", lambda h, nxt=nxt: h.tensor_tensor(out=nxt.ap[:, HALO:W2], in0=nxt.ap[:, HALO:W2],
                                                            in1=invc.ap, op=ALU.mult), [nxt, invc], [nxt])
                zg = zT.i(g)
                A("dve", lambda h, nxt=nxt, U=U, zg=zg: h.tensor_tensor(out=zg.ap, in0=nxt.ap[:, HALO:W2],
                                                                    in1=U.ap[:, HALO:W2], op=ALU.subtract),
                  [nxt, U], [zg])
                o = ps(GB[gb % 4]); gb += 1
                mm_group(o, [(Buf(sb_arena, wpool.lo, wpool.hi, wpool.ap[:, g, :]), zg)])
                dst = gT.i(g)
                A("dve", lambda h, o=o, dst=dst, g=g: h.scalar_tensor_tensor(
                    out=dst.ap, in0=o.ap, scalar=pscale.ap[:, g:g + 1], in1=dst.ap, op0=ALU.mult, op1=ALU.mult),
                  [o, pscale, dst], [dst])
            n_kt = 16 * (i + 1)
            pending_fin = []
            trF = ps(7, 0, 512, BF16)

            def fin_part2(hp):
                for qt in range(4):
                    yq = yts[hp % 2].c(qt * 128, (qt + 1) * 128)
                    dst = trF.c(qt * 128, (qt + 1) * 128)
                    A("pe", lambda h, yq=yq, dst=dst: h.transpose(out=dst.ap, in_=yq.ap, identity=ident.ap),
                      [yq, ident], [dst])
                dst = gT.i(4 + hp)
                A("dve", lambda h, dst=dst: h.tensor_tensor(out=dst.ap, in0=trF.ap, in1=dst.ap, op=ALU.mult),
                  [trF, dst], [dst])

            for hh in range(4):
                accs = []
                for a in range(8):
                    bk, off = a // 3, (a % 3) * 129
                    accs.append(ps(bk, off, off + 129))
                qh = qT.i(hh)
                q_lo = Buf(sb_arena, qh.lo, qh.hi, qh.ap[0:64, :])
                q_hi = Buf(sb_arena, qh.lo, qh.hi, qh.ap[64:128, :])
                sb_ = [[ps(3, 0, 512), ps(4, 0, 512)], [ps(5, 0, 512), ps(6, 0, 512)]]
                blocks = {}

                kv_base = kv_n
                kv_n += 2 * (i + 1)

                def qk(kt):
                    kb, k8 = kt // 8, kt % 8
                    n_ = kv_base + kb
                    kv_fetch(n_)
                    blocks[kb] = (ktb[n_ % NKV], vtb[n_ % NKV])
                    kt_b, _ = blocks[kb]
                    for cpt in range(2):
                        l = Buf(sb_arena, kt_b.lo, kt_b.hi, kt_b.ap[64 * cpt:64 * (cpt + 1), k8 * 128:(k8 + 1) * 128])
                        r = q_lo if cpt == 0 else q_hi
                        mm_group(sb_[kt % 2][cpt], [(l, r)], tile_position=(64 * cpt, 0))

                qk(0)
                for kt in range(n_kt):
                    if kt + 1 < n_kt:
                        qk(kt + 1)
                    if kt == 6 and pending_fin:
                        fin_part2(pending_fin.pop(0))
                    kb, k8 = kt // 8, kt % 8
                    if k8 == 0:
                        nxt_ = kv_base + kb + 1
                        if nxt_ < len(kv_seq) and kv_seq[nxt_][0] == i:
                            kv_fetch(nxt_)
                    _, v_b = blocks[kb]
                    pts = ptb[kt % NPT]
                    for cpt in range(2):
                        s_ = sb_[kt % 2][cpt]
                        p_ = pts[cpt]
                        A("act", lambda h, s_=s_, p_=p_: h.activation(out=p_.ap, in_=s_.ap, func=AF.Exp, scale=0.125),
                          [s_], [p_])
                        if kt >= n_kt - 16:
                            rr = kt - (n_kt - 16)
                            A("dve", lambda h, p_=p_, rr=rr: h.scalar_tensor_tensor(
                                out=p_.ap, in0=qkb.ap, scalar=thr.ap[:, rr:rr + 1], in1=p_.ap,
                                op0=ALU.is_ge, op1=ALU.mult), [qkb, thr, p_], [p_])
                    vk = Buf(sb_arena, v_b.lo, v_b.hi, v_b.ap[:, k8, :])
                    for cpt in range(2):
                        for qt in range(4):
                            a = cpt * 4 + qt
                            l = Buf(sb_arena, pts[cpt].lo, pts[cpt].hi, pts[cpt].ap[:, qt * 128:(qt + 1) * 128])
                            A("pe", lambda h, a=a, l=l, vk=vk, kt=kt: h.matmul(
                                accs[a].ap, lhsT=l.ap, rhs=vk.ap, start=(kt == 0 and a % 3 == 0),
                                stop=(kt == n_kt - 1), skip_group_check=True), [l, vk], [accs[a]])
                for bk in range(3):
                    na = 3 if bk < 2 else 2
                    src = ps(bk, 0, na * 129)
                    dst = Buf(sb_arena, accS.lo + bk * 3 * 129 * 4, accS.lo + (bk * 3 + na) * 129 * 4,
                              accS.ap[:, bk * 3:bk * 3 + na, :], 4)
                    eng = "act" if bk == 1 else "dve"
                    if eng == "act":
                        A("act", lambda h, src=src, dst=dst, na=na: h.activation(
                            out=dst.ap, in_=src.ap.rearrange("p (a b) -> p a b", a=na), func=AF.Copy), [src], [dst])
                    else:
                        A("dve", lambda h, src=src, dst=dst, na=na: h.tensor_copy(
                            out=dst.ap, in_=src.ap.rearrange("p (a b) -> p a b", a=na)), [src], [dst])
                if debug and i == 0:
                    dma("sp", dbg["acc"][hh], accS.ap, [accS], [])
                    if hh == 0:
                        dma("sp", dbg["q"], qT.ap, [qT], [])
                A("dve", lambda h: h.reciprocal(out=rl.ap, in_=accS.ap[:, :, 128]), [accS], [rl])
                A("dve", lambda h: h.tensor_scalar(out=rl2n.ap, in0=rl.ap[:, 4:8], scalar1=neglam.ap[:, 0:1],
                                                   scalar2=None, op0=ALU.mult), [rl, neglam], [rl2n])
                for qt in range(4):
                    t2 = ot2[qt % 2]
                    o_ = oo[qt]
                    A("dve", lambda h, qt=qt, t2=t2: h.tensor_scalar(
                        out=t2.ap, in0=accS.ap[:, 4 + qt, 0:128], scalar1=rl2n.ap[:, qt:qt + 1], scalar2=None,
                        op0=ALU.mult), [accS, rl2n], [t2])
                    A("dve", lambda h, qt=qt, t2=t2, o_=o_: h.scalar_tensor_tensor(
                        out=o_.ap, in0=accS.ap[:, qt, 0:128], scalar=rl.ap[:, qt:qt + 1], in1=t2.ap,
                        op0=ALU.mult, op1=ALU.add), [accS, rl, t2], [o_])
                rms_stats([(oo[qt], 128) for qt in range(4)], 128)
                for qt in range(4):
                    o_ = oo[qt]
                    yq = yts[hh % 2].c(qt * 128, (qt + 1) * 128)
                    A("dve", lambda h, qt=qt, o_=o_, yq=yq: h.scalar_tensor_tensor(
                        out=yq.ap, in0=o_.ap, scalar=rstd.ap[:, qt:qt + 1], in1=gsub.ap, op0=ALU.mult, op1=ALU.mult),
                      [o_, rstd.c(qt, qt + 1), gsub], [yq])

                if debug and i == 0:
                    dma("sp", dbg["yt"][hh], yts[hh % 2].ap, [yts[hh % 2]], [])
                pending_fin.append(hh)
            while pending_fin:
                fin_part2(pending_fin.pop(0))
            if debug:
                dma("sp", dbg["mixT"][i], gT.ap, [gT], [])
            w_o = slab_get(slab_n); slab_n += 1
            resid_proj(xs_, gT, w_o, GB)
            if debug:
                dma("sp", dbg["x1"][i * CH:(i + 1) * CH, :].rearrange("(t p) d -> p t d", p=128), xs_.ap, [xs_], [])
            norm_T2([(xs_.i(t), 128, t * 128) for t in range(4)], 8, xnB, trB, hTB, CH, ["act", "dve"])
            w_q = slab_get(slab_n); slab_n += 1
            gb = 0
            for oc in range(8):
                o = ps(GB[gb % 4]); gb += 1
                mm_group(o, [(wsl(w_q, kc, oc * 128, (oc + 1) * 128), hTB.i(kc)) for kc in range(8)])
                dst = qcT.i(oc)
                if oc % 2 == 0:
                    A("act", lambda h, o=o, dst=dst: h.activation(out=dst.ap, in_=o.ap, func=AF.Copy), [o], [dst])
                else:
                    A("dve", lambda h, o=o, dst=dst: h.tensor_copy(out=dst.ap, in_=o.ap), [o], [dst])
            for hh in range(4):
                pc = pcT[hh % 2]
                for mt in range(2):
                    o = ps(GB[gb % 4]); gb += 1
                    mm_group(o, [(Buf(sb_arena, KcT.lo, KcT.hi, KcT.ap[:, 2 * hh + dc, mt * 128:(mt + 1) * 128]),
                                  qcT.i(2 * hh + dc)) for dc in range(2)])
                    dst = pc.i(mt)
                    A("act", lambda h, o=o, dst=dst: h.activation(out=dst.ap, in_=o.ap, func=AF.Exp, scale=1.0 / 16.0),
                      [o], [dst])
                ol = ps(GB[gb % 4]); gb += 1
                mm_group(ol, [(ones, pc.i(mt)) for mt in range(2)])
                rc = rlc[hh % 2]
                A("act", lambda h, ol=ol: h.activation(out=lnl.ap, in_=ol.ap, func=AF.Ln), [ol], [lnl])
                A("act", lambda h, rc=rc: h.activation(out=rc.ap, in_=lnl.ap, func=AF.Exp, scale=-1.0), [lnl], [rc])
                for dc in range(2):
                    o = ps(GB[gb % 4]); gb += 1
                    mm_group(o, [(Buf(sb_arena, Vc.lo, Vc.hi, Vc.ap[:, mt, (2 * hh + dc) * 128:(2 * hh + dc + 1) * 128]),
                                  pc.i(mt)) for mt in range(2)])
                    dst = ocT.i(2 * hh + dc)
                    A("dve", lambda h, o=o, dst=dst, rc=rc: h.tensor_tensor(out=dst.ap, in0=o.ap, in1=rc.ap, op=ALU.mult),
                      [o, rc], [dst])
            w_c = slab_get(slab_n); slab_n += 1
            resid_proj(xs_, ocT, w_c, GB)
            if debug:
                dma("sp", dbg["x2"][i * CH:(i + 1) * CH, :].rearrange("(t p) d -> p t d", p=128), xs_.ap, [xs_], [])
            norm_T2([(xs_.i(t), 128, t * 128) for t in range(4)], 16, xnB, trB, hTB, CH, ["act", "dve"])
            gb = 0
            for su in range(4):
                w_u = slab_get(slab_n); slab_n += 1
                for o8 in range(8):
                    oc = su * 8 + o8
                    o = ps(GB[gb % 4]); gb += 1
                    mm_group(o, [(wsl(w_u, kc, o8 * 128, (o8 + 1) * 128), hTB.i(kc)) for kc in range(8)])
                    rt = rtmp[oc % 3]
                    A("act", lambda h, o=o, rt=rt: h.activation(out=rt.ap, in_=o.ap, func=AF.Relu), [o], [rt])
                    dst = aT.i(oc)
                    A("pool", lambda h, rt=rt, dst=dst: h.tensor_tensor(out=dst.ap, in0=rt.ap, in1=rt.ap, op=ALU.mult),
                      [rt], [dst])
            dacc = [ps(b_) for b_ in range(8)]
            for sd in range(4):
                w_d = slab_get(slab_n); slab_n += 1
                for t in range(4):
                    for hf in range(2):
                        o = dacc[t * 2 + hf]
                        for k8 in range(8):
                            kc = sd * 8 + k8
                            l = Buf(sb_arena, aT.lo, aT.hi, aT.ap[:, kc, t * 128:(t + 1) * 128])
                            r = wsl(w_d, k8, hf * 512, (hf + 1) * 512)
                            A("pe", lambda h, o=o, l=l, r=r, kc=kc: h.matmul(
                                o.ap, lhsT=l.ap, rhs=r.ap, start=(kc == 0), stop=(kc == 31)), [l, r], [o])
            for t in range(4):
                for hf in range(2):
                    o = dacc[t * 2 + hf]
                    xsl = Buf(sb_arena, xs_.lo + (t * D + hf * 512) * 4, xs_.lo + (t * D + (hf + 1) * 512) * 4,
                              xs_.ap[:, t, hf * 512:(hf + 1) * 512], 4)
                    A("dve", lambda h, o=o, xsl=xsl: h.tensor_tensor(out=xsl.ap, in0=o.ap, in1=xsl.ap, op=ALU.add),
                      [o, xsl], [xsl])
            rms_stats([(xs_.i(t), 128) for t in range(4)], D)
            for t in range(4):
                xt = xs_.i(t)
                A("dve", lambda h, xt=xt, t=t: h.scalar_tensor_tensor(
                    out=xt.ap, in0=xt.ap, scalar=rstd.ap[:, t:t + 1], in1=gfin.ap, op0=ALU.mult, op1=ALU.mult),
                  [xt, rstd.c(t, t + 1), gfin], [xt])
            dma("sp", out_d[i * CH:(i + 1) * CH, :].rearrange("(t p) d -> p t d", p=128), xs_.ap, [xs_],
                [dr("out", i, i + 1)])

        with nc.Block() as block:
            sched.emit(nc, eng_sems, dma_sems, block)
    return nc


def make_in_maps(S, x, mem, g_mix, w_in, w_pool, pool_scale, lambda_q1, lambda_k1, lambda_q2, lambda_k2,
                 g_subln, w_out, g_cross, g_mem, w_cq, w_ckv, w_co, g_mlp, w_up, w_down, g_final):
    f = lambda a: np.ascontiguousarray(np.asarray(a, dtype=np.float32))
    NCH = S // CH
    M = NCH // CPB
    w_in0 = f(w_in[0])
    perm = np.arange(512).reshape(8, 2, 32)[:, ::-1, :].reshape(512)
    w_qsw = f(w_in0[:, 512:1024][:, perm])
    w_ksw = f(w_in0[:, 1024:1536][:, perm])
    col = lambda g: np.asarray(g, np.float32).reshape(8, 128).T
    gcols = f(np.concatenate([col(g_mix[0]), col(g_cross[0]), col(g_mlp[0]), col(g_mem[0])], axis=1))
    pscale = f(np.asarray(pool_scale[0], np.float32).reshape(4, 128).T)
    lam = f(np.concatenate([lambda_q1[0], lambda_k1[0], lambda_q2[0], lambda_k2[0]]).reshape(1, 256))
    p = np.arange(128)
    inv_freq = (10000.0 ** (-(np.arange(0, 64, 2, dtype=np.float32)) / np.float32(64))).astype(np.float32)
    invf = inv_freq[p % 32]
    sgn = np.where((p % 64) < 32, -1.0, 1.0).astype(np.float32)
    ident = np.eye(128, dtype=np.float32)
    qkbase = f(np.arange(CH)[None, :] - np.arange(128)[:, None])
    posA = f(np.arange(S).reshape(NCH, CH))
    shared = {
        "posA": posA, "qkbase": qkbase, "ident": ident, "w_in": w_in0, "w_qsw": w_qsw, "w_ksw": w_ksw,
        "w_pool": f(w_pool[0]), "w_out": f(w_out[0]), "w_cq": f(w_cq[0]), "w_ckv": f(w_ckv[0]),
        "w_co": f(w_co[0]), "w_up": f(w_up[0]), "w_down": f(w_down[0]), "gcols": gcols, "pscale": pscale,
        "gfin": f(np.asarray(g_final).reshape(1, D)), "gsub": f(np.asarray(g_subln[0]).reshape(1, 128)), "lam": lam,
    }
    x = np.asarray(x, np.float32)
    mem = np.asarray(mem, np.float32)
    in_maps = []
    for c in range(NCORE):
        b, j = c // CPB, c % CPB
        xo = np.zeros((M, HALO + CH, D), np.float32)
        posB = np.zeros((M, CH), np.float32)
        for m in range(M):
            t0 = (CPB * m + j) * CH
            lo = max(t0 - HALO, 0)
            xo[m, HALO - (t0 - lo):] = x[b, lo:t0 + CH]
            posB[m] = np.arange(t0, t0 + CH)
        consts = np.zeros((128, 4), np.float32)
        consts[:, 0] = invf
        consts[:, 1] = invf / np.float32(TWO_PI)
        consts[:, 2] = sgn
        thr = np.zeros((128, 16), np.float32)
        thr[:, :] = (128.0 * np.arange(16) - 512.0 * j)[None, :]
        d = dict(shared)
        d.update({"xb": f(x[b]), "xo": xo, "posB": posB, "consts": consts, "thr": thr, "memb": f(mem[b])})
        in_maps.append(d)
    return in_maps


_NC_CACHE = {}


def run(S, inputs, debug=False):
    key = (S, debug)
    if key not in _NC_CACHE:
        _NC_CACHE[key] = build(S, debug)
    nc = _NC_CACHE[key]
    in_maps = make_in_maps(S, **inputs)
    res = run_bass_kernel_spmd(nc, in_maps, core_ids=list(range(NCORE)))
    return res


def assemble(S, results, key="out"):
    NCH = S // CH
    M = NCH // CPB
    out = np.zeros((2, S, D), np.float32)
    for c in range(NCORE):
        b, j = c // CPB, c % CPB
        o = np.asarray(results[c][key]).reshape(M, CH, D)
        for m in range(M):
            t0 = (CPB * m + j) * CH
            out[b, t0:t0 + CH] = o[m]
    return out


def kernel(**inputs):
    S = int(np.asarray(inputs["x"]).shape[1])
    res = run(S, inputs)
    return assemble(S, res.results)
```

```python
import bisect
import math
import os
import numpy as np
import concourse.bass as bass
import concourse.mybir as mybir
from concourse.bass_utils import run_bass_kernel_spmd

F32 = mybir.dt.float32
BF16 = mybir.dt.bfloat16
I32 = mybir.dt.int32
U8 = mybir.dt.uint8
ALU = mybir.AluOpType
AF = mybir.ActivationFunctionType
AX = mybir.AxisListType

D = 1024
CH = 512
NCORE = 8
CPB = 4
HALO = 16
MEM = 256
DFF = 4096
EPS = 1e-6
LAM_INIT = 0.8 - 0.6 * math.exp(0.0)
TWO_PI = 2.0 * math.pi
CW1 = 6.28125
CW2 = TWO_PI - CW1
PI_LO = 3.1415925
NDMA = 48
NSWDMA = 8
ESZ = {F32: 4, BF16: 2, I32: 4, U8: 1}


class Arena:
    def __init__(self, size):
        self.b = [0, size]
        self.w = [None]
        self.r = [{}]

    def _split(self, x):
        i = bisect.bisect_right(self.b, x) - 1
        if self.b[i] == x:
            return i
        self.b.insert(i + 1, x)
        self.w.insert(i + 1, self.w[i])
        self.r.insert(i + 1, dict(self.r[i]))
        return i + 1

    def rng(self, lo, hi):
        i = self._split(lo)
        j = self._split(hi)
        return range(i, j)


class Buf:
    def __init__(self, arena, lo, hi, ap, esz=1):
        self.arena, self.lo, self.hi, self.ap, self.esz = arena, lo, hi, ap, esz

    def c(self, c0, c1):
        return Buf(self.arena, self.lo + c0 * self.esz, self.lo + c1 * self.esz,
                   self.ap[:, c0:c1], self.esz)

    def i(self, k):
        n = self.ap.shape[2]
        return Buf(self.arena, self.lo + k * n * self.esz, self.lo + (k + 1) * n * self.esz,
                   self.ap[:, k, :], self.esz)


class Op:
    __slots__ = ("eng", "fn", "deps", "idx", "inc", "val", "is_dma", "sem_id", "key")


class Sched:
    ENGS = ("pe", "act", "dve", "pool", "sp")

    def __init__(self):
        self.ops = {e: [] for e in self.ENGS}
        self.dma_rr = 0
        self.sw_rr = 0
        self.dma_last = [None] * NDMA
        self.dma_cnt = [0] * NDMA
        self.nuid = 0

    def add(self, eng, fn, reads=(), writes=(), dma=False):
        op = Op()
        op.eng, op.fn, op.is_dma, op.inc, op.val, op.sem_id = eng, fn, dma, False, 0, -1
        op.idx = len(self.ops[eng])
        if dma:
            self.nuid += 1
            op.key = ("d", self.nuid)
        else:
            op.key = eng
        deps = {}

        def dep(o, raw):
            if o is None or o is op:
                return
            if (not dma) and (not o.is_dma) and o.eng == eng:
                if eng == "pe":
                    return
            p = deps.get(o.key)
            if p is None or o.idx > p.idx:
                deps[o.key] = o

        for r in reads:
            ar = r.arena
            for s in ar.rng(r.lo, r.hi):
                dep(ar.w[s], True)
        for r in writes:
            ar = r.arena
            for s in ar.rng(r.lo, r.hi):
                dep(ar.w[s], False)
                for o in ar.r[s].values():
                    dep(o, False)
        for r in reads:
            ar = r.arena
            for s in ar.rng(r.lo, r.hi):
                ar.r[s][op.key] = op
        for r in writes:
            ar = r.arena
            for s in ar.rng(r.lo, r.hi):
                ar.w[s] = op
                ar.r[s] = {}
        if dma:
            if eng == "pool":
                sid = NDMA - NSWDMA + self.sw_rr % NSWDMA
                self.sw_rr += 1
            else:
                sid = self.dma_rr % (NDMA - NSWDMA)
                self.dma_rr += 1
            prev = self.dma_last[sid]
            if prev is not None:
                deps[prev.key] = prev
            self.dma_last[sid] = op
            self.dma_cnt[sid] += 16
            op.sem_id = sid
            op.val = self.dma_cnt[sid]
        op.deps = deps
        self.ops[eng].append(op)
        return op

    def emit(self, nc, eng_sems, dma_sems, block):
        for e in self.ENGS:
            for op in self.ops[e]:
                for d in op.deps.values():
                    if not d.is_dma:
                        d.inc = True
        for e in self.ENGS:
            c = 0
            for op in self.ops[e]:
                if not op.is_dma and op.inc:
                    c += 1
                    op.val = c

        def run(e, h):
            seen = {}
            for op in self.ops[e]:
                waits = {}
                for d in op.deps.values():
                    k = ("d", d.sem_id) if d.is_dma else ("e", d.eng)
                    if waits.get(k, 0) < d.val:
                        waits[k] = d.val
                for k, v in waits.items():
                    if seen.get(k, 0) >= v:
                        continue
                    seen[k] = v
                    sem = dma_sems[k[1]] if k[0] == "d" else eng_sems[k[1]]
                    h.wait_ge(sem, v)
                ins = op.fn(h)
                if op.is_dma:
                    ins.then_inc(dma_sems[op.sem_id], 16)
                elif op.inc:
                    ins.then_inc(eng_sems[e], 1)
            if e == "sp":
                for sid in range(NDMA):
                    if self.dma_cnt[sid] > 0 and seen.get(("d", sid), 0) < self.dma_cnt[sid]:
                        h.wait_ge(dma_sems[sid], self.dma_cnt[sid])

        @block.tensor
        def _(h):
            run("pe", h)

        @block.scalar
        def _(h):
            run("act", h)

        @block.vector
        def _(h):
            run("dve", h)

        @block.gpsimd
        def _(h):
            run("pool", h)

        @block.sync
        def _(h):
            run("sp", h)


def build(S, debug=False):
    NCH = S // CH
    M = NCH // CPB
    NT = S // 128
    nc = bass.Bass("TRN2", target_bir_lowering=False)

    def din(name, shape):
        return nc.dram_tensor(name, list(shape), F32, kind="ExternalInput").ap()

    xb = din("xb", [S, D])
    xo = din("xo", [M, HALO + CH, D])
    posA = din("posA", [NCH, CH])
    posB = din("posB", [M, CH])
    consts = din("consts", [128, 4])
    thr_d = din("thr", [128, 16])
    qkb_d = din("qkbase", [128, CH])
    ident_d = din("ident", [128, 128])
    memb = din("memb", [MEM, D])
    w_in = din("w_in", [D, 3072])
    w_qsw = din("w_qsw", [D, 512])
    w_ksw = din("w_ksw", [D, 512])
    w_pool = din("w_pool", [4, 128, 128])
    w_out = din("w_out", [D, D])
    w_cq = din("w_cq", [D, D])
    w_ckv = din("w_ckv", [D, 2 * D])
    w_co = din("w_co", [D, D])
    w_up = din("w_up", [D, DFF])
    w_down = din("w_down", [DFF, D])
    gcols_d = din("gcols", [128, 32])
    pscale_d = din("pscale", [128, 4])
    gfin_d = din("gfin", [1, D])
    gsub_d = din("gsub", [1, 128])
    lam_d = din("lam", [1, 256])
    out_d = nc.dram_tensor("out", [M * CH, D], F32, kind="ExternalOutput").ap()
    skind = "ExternalOutput" if debug else "Internal"
    KTs = nc.dram_tensor("KTs", [4, 128, S], BF16, kind=skind).ap()
    Vs = nc.dram_tensor("Vs", [4, 128, NT, 129], BF16, kind=skind).ap()
    slab_defs = [
        ("qq", 1024, [(w_in, 0, 512, 512, 0), (w_qsw, 0, 0, 512, 512)]),
        ("ug", 1024, [(w_in, 0, 0, 512, 0), (w_in, 0, 2048, 512, 512)]),
        ("g2", 512, [(w_in, 0, 2560, 512, 0)]),
        ("out", 1024, [(w_out, 0, 0, 1024, 0)]),
        ("cq", 1024, [(w_cq, 0, 0, 1024, 0)]),
        ("co", 1024, [(w_co, 0, 0, 1024, 0)]),
    ]
    for i in range(4):
        slab_defs.append(("up%d" % i, 1024, [(w_up, 0, 1024 * i, 1024, 0)]))
    for i in range(4):
        slab_defs.append(("dn%d" % i, 1024, [(w_down, 1024 * i, 0, 1024, 0)]))
    NSLAB = len(slab_defs)
    wbf = [nc.dram_tensor("wbf_" + sd[0], [128, 8, sd[1]], BF16, kind="Internal").ap()
           for sd in slab_defs]
    dbg = {}
    if debug:
        dbg["x1"] = nc.dram_tensor("dbg_x1", [M * CH, D], F32, kind="ExternalOutput").ap()
        dbg["mixT"] = nc.dram_tensor("dbg_mixT", [M, 128, 8, CH], BF16, kind="ExternalOutput").ap()
        dbg["x2"] = nc.dram_tensor("dbg_x2", [M * CH, D], F32, kind="ExternalOutput").ap()
        dbg["acc"] = nc.dram_tensor("dbg_acc", [4, 128, 8, 129], F32, kind="ExternalOutput").ap()
        dbg["yt"] = nc.dram_tensor("dbg_yt", [4, 128, CH], BF16, kind="ExternalOutput").ap()
        dbg["q"] = nc.dram_tensor("dbg_q", [128, 4, CH], BF16, kind="ExternalOutput").ap()

    SB_BYTES = 175 * 1024
    sched = Sched()
    sb_arena = Arena(SB_BYTES)
    ps_arena = Arena(8 * 2048)
    dram_arenas = {}

    from contextlib import ExitStack
    with ExitStack() as es:
        big = es.enter_context(nc.sbuf_tensor("big", [128, SB_BYTES], U8))
        banks = [es.enter_context(nc.psum_tensor("bank%d" % i, [128, 512], F32)) for i in range(8)]
        eng_sems = {e: es.enter_context(nc.semaphore("sem_" + e)) for e in Sched.ENGS}
        dma_sems = [es.enter_context(nc.semaphore("dsem%d" % i)) for i in range(NDMA)]

        top = [0]

        def sb(shape, dt):
            esz = ESZ[dt]
            n = 1
            for v in shape[1:]:
                n *= v
            nb = (n * esz + 63) // 64 * 64
            lo = top[0]
            top[0] += nb
            assert top[0] <= SB_BYTES, ("SBUF overflow", top[0])
            ap = big[0:shape[0], lo:lo + n * esz].bitcast(dt)
            if len(shape) == 3:
                ap = ap.rearrange("p (a b) -> p a b", a=shape[1])
            elif len(shape) == 4:
                ap = ap.rearrange("p (a b c) -> p a b c", a=shape[1], b=shape[2])
            return Buf(sb_arena, lo, lo + n * esz, ap, esz)

        def ps(bank, c0=0, c1=512, dt=F32):
            esz = ESZ[dt]
            ap = banks[bank][:, :]
            if dt != F32:
                ap = ap.bitcast(dt)
            b_ = Buf(ps_arena, bank * 2048, (bank + 1) * 2048, ap[:, c0:c1], esz)
            b_.c = lambda a0, a1, b_=b_: Buf(ps_arena, b_.lo, b_.hi, b_.ap[:, a0:a1], esz)
            return b_

        def dr(name, lo, hi):
            if name not in dram_arenas:
                dram_arenas[name] = Arena(1 << 40)
            return Buf(dram_arenas[name], lo, hi, None)

        A = sched.add

        def dma(q, out, in_, reads, writes):
            return A(q, lambda h, o=out, i=in_: h.dma_start(out=o, in_=i), reads, writes, dma=True)

        ident = sb([128, 128], BF16)
        ones = sb([128, 128], BF16)
        cst = sb([128, 4], F32)
        thr = sb([128, 16], F32)
        qkb = sb([128, CH], F32)
        gcols = sb([128, 32], F32)
        pscale = sb([128, 4], F32)
        gfin = sb([128, D], F32)
        gsub = sb([128, 128], F32)
        lamt = sb([128, 256], F32)
        lamw = sb([128, 8], F32)
        neglam = sb([128, 1], F32)
        wpool = sb([128, 4, 128], BF16)
        KcT = sb([128, 8, MEM], BF16)
        Vc = sb([128, 2, D], BF16)
        ssn = sb([128, 16], F32)
        lnv = sb([128, 16], F32)
        rstd = sb([128, 16], F32)

        dma("pool", ident.ap, ident_d, [], [ident])
        dma("sp", cst.ap, consts, [], [cst])
        dma("sp", thr.ap, thr_d, [], [thr])
        dma("sp", qkb.ap, qkb_d, [], [qkb])
        dma("sp", gcols.ap, gcols_d, [], [gcols])
        dma("sp", pscale.ap, pscale_d, [], [pscale])
        dma("sp", gfin.ap, gfin_d[0:1, :].broadcast_to([128, D]), [], [gfin])
        dma("sp", gsub.ap, gsub_d[0:1, :].broadcast_to([128, 128]), [], [gsub])
        dma("sp", lamt.ap, lam_d[0:1, :].broadcast_to([128, 256]), [], [lamt])
        dma("pool", wpool.ap, w_pool.rearrange("g c d -> c g d"), [], [wpool])
        A("dve", lambda h: h.memset(ones.ap, 1.0), [], [ones])
        A("dve", lambda h: h.memset(ssn.ap, 1.0), [], [ssn])
        A("dve", lambda h: h.tensor_tensor(out=lamt.ap[:, 0:64], in0=lamt.ap[:, 0:64],
                                           in1=lamt.ap[:, 64:128], op=ALU.mult), [lamt], [lamt])
        A("dve", lambda h: h.tensor_tensor(out=lamt.ap[:, 128:192], in0=lamt.ap[:, 128:192],
                                           in1=lamt.ap[:, 192:256], op=ALU.mult), [lamt], [lamt])
        A("dve", lambda h: h.reduce_sum(out=lamw.ap[:, 0:1], in_=lamt.ap[:, 0:64], axis=AX.X), [lamt], [lamw])
        A("dve", lambda h: h.reduce_sum(out=lamw.ap[:, 1:2], in_=lamt.ap[:, 128:192], axis=AX.X), [lamt], [lamw])
        A("act", lambda h: h.activation(out=lamw.ap[:, 2:4], in_=lamw.ap[:, 0:2], func=AF.Exp), [lamw], [lamw])
        A("dve", lambda h: h.tensor_tensor(out=lamw.ap[:, 4:5], in0=lamw.ap[:, 3:4], in1=lamw.ap[:, 2:3],
                                           op=ALU.subtract), [lamw], [lamw])
        A("dve", lambda h: h.tensor_scalar(out=neglam.ap, in0=lamw.ap[:, 4:5], scalar1=-LAM_INIT, scalar2=None,
                                           op0=ALU.add), [lamw], [neglam])
        A("dve", lambda h: h.tensor_scalar(out=gsub.ap, in0=gsub.ap, scalar1=1.0 - LAM_INIT, scalar2=None,
                                           op0=ALU.mult), [gsub], [gsub])

        wbf_bufs = []
        for s, (nm, ncol, parts) in enumerate(slab_defs):
            b = dr("wbf%d" % s, 0, 1)
            wbf_bufs.append(b)
            for (src, r0, c0, n_, d0) in parts:
                dma("pool", wbf[s][:, :, d0:d0 + n_],
                    src[r0:r0 + 1024, c0:c0 + n_].rearrange("(k p) c -> p k c", p=128), [], [b])

        persist_top = top[0]

        def rms_stats(tiles, n_in, jlist):
            for k, (xt, P) in enumerate(tiles):
                junk = jlist[k % len(jlist)]
                A("act", lambda h, xt=xt, P=P, k=k, junk=junk: h.activation(
                    out=junk.ap[0:P, 0:n_in], in_=xt.ap[0:P, :], func=AF.Square,
                    accum_out=ssn.ap[0:P, k:k + 1]), [xt], [junk, ssn.c(k, k + 1)])
            n = len(tiles)
            A("act", lambda h: h.activation(out=lnv.ap[:, 0:n], in_=ssn.ap[:, 0:n], func=AF.Ln,
                                            bias=epsc.ap[:, 0:1], scale=1.0 / n_in), [ssn, epsc], [lnv])
            A("act", lambda h: h.activation(out=rstd.ap[:, 0:n], in_=lnv.ap[:, 0:n], func=AF.Exp, scale=-0.5),
              [lnv], [rstd])

        epsc = sb([128, 1], F32)
        A("dve", lambda h: h.memset(epsc.ap, EPS), [], [epsc])
        halfpi = sb([128, 1], F32)
        A("dve", lambda h: h.memset(halfpi.ap, math.pi / 2.0), [], [halfpi])
        junk_s = [sb([128, 128], BF16) for _ in range(2)]
        persist_top = top[0]

        def norm_pre(xt, P, k, xn, junk):
            sk, lk, rk = ssn.c(2 * k, 2 * k + 2), lnv.c(2 * k, 2 * k + 2), rstd.c(2 * k, 2 * k + 2)
            A("act", lambda h: h.activation(out=junk.ap[0:P, 0:D], in_=xt.ap[0:P, :], func=AF.Square,
                                            accum_out=sk.ap[0:P, 0:1]), [xt], [junk, sk])
            A("act", lambda h: h.activation(out=lk.ap[0:P, :], in_=sk.ap[0:P, :], func=AF.Ln,
                                            bias=epsc.ap[0:P, 0:1], scale=1.0 / D), [sk, epsc], [lk])
            A("act", lambda h: h.activation(out=rk.ap[0:P, :], in_=lk.ap[0:P, :], func=AF.Exp, scale=-0.5),
              [lk], [rk])
            A("dve", lambda h: h.tensor_scalar(out=xn.ap[0:P, :], in0=xt.ap[0:P, :], scalar1=rk.ap[0:P, 0:1],
                                               scalar2=None, op0=ALU.mult), [xt, rk], [xn])

        def norm_post(xn, P, bank, gofs, dsts, evac):
            for fc in range(8):
                dst = bank.c(fc * 128, fc * 128 + P)
                A("pe", lambda h, fc=fc, dst=dst: h.transpose(
                    out=dst.ap, in_=xn.ap[0:P, fc * 128:(fc + 1) * 128], identity=ident.ap[0:P, 0:P]),
                  [xn, ident], [dst])
            for fc in range(8):
                src = bank.c(fc * 128, fc * 128 + P)
                dstb = dsts[fc]
                g = gcols.c(gofs + fc, gofs + fc + 1)
                if evac[fc % len(evac)] == "act":
                    A("act", lambda h, src=src, dstb=dstb, g=g: h.activation(
                        out=dstb.ap, in_=src.ap, func=AF.Copy, scale=g.ap), [src, g], [dstb])
                else:
                    A("dve", lambda h, src=src, dstb=dstb, g=g: h.tensor_scalar(
                        out=dstb.ap, in0=src.ap, scalar1=g.ap, scalar2=None, op0=ALU.mult), [src, g], [dstb])

        def hT_dsts(hT, W, col0, P):
            return [Buf(sb_arena, hT.lo + (fc * W + col0) * 2, hT.lo + (fc * W + col0 + P) * 2,
                        hT.ap[:, fc, col0:col0 + P], 2) for fc in range(8)]

        def norm_run(tiles, gofs, xn_bufs, tr_banks, dst_list, jlist, evac=("act", "dve")):
            prev = None
            for k, (xt, P) in enumerate(tiles):
                norm_pre(xt, P, k, xn_bufs[k], jlist[k % len(jlist)])
                if prev is not None:
                    norm_post(*prev)
                prev = (xn_bufs[k], P, tr_banks[k % len(tr_banks)], gofs, dst_list[k], (evac[k % len(evac)],))
            norm_post(*prev)

        def rope_tables(pos_row, tb):
            dma("sp", tb["pos"].ap, pos_row.broadcast_to([128, CH]), [], [tb["pos"]])
            A("dve", lambda h: h.tensor_scalar(out=tb["ang"].ap, in0=tb["pos"].ap, scalar1=cst.ap[:, 0:1],
                                                scalar2=None, op0=ALU.mult), [tb["pos"], cst], [tb["ang"]])
            A("dve", lambda h: h.tensor_scalar(out=tb["v"].ap, in0=tb["pos"].ap, scalar1=cst.ap[:, 1:2],
                                                scalar2=None, op0=ALU.mult), [tb["pos"], cst], [tb["v"]])
            A("dve", lambda h: h.tensor_copy(out=tb["vi"].ap, in_=tb["v"].ap), [tb["v"]], [tb["vi"]])
            A("dve", lambda h: h.tensor_copy(out=tb["v"].ap, in_=tb["vi"].ap), [tb["vi"]], [tb["v"]])
            A("dve", lambda h: h.scalar_tensor_tensor(out=tb["y"].ap, in0=tb["v"].ap, scalar=-CW1, in1=tb["ang"].ap,
                                                      op0=ALU.mult, op1=ALU.add), [tb["v"], tb["ang"]], [tb["y"]])
            A("dve", lambda h: h.scalar_tensor_tensor(out=tb["ang"].ap, in0=tb["v"].ap, scalar=-CW2, in1=tb["y"].ap,
                                                      op0=ALU.mult, op1=ALU.add), [tb["v"], tb["y"]], [tb["ang"]])
            A("pool", lambda h: h.tensor_scalar(out=tb["y"].ap, in0=tb["ang"].ap, scalar1=PI_LO, scalar2=-PI_LO,
                                                op0=ALU.min, op1=ALU.max), [tb["ang"]], [tb["y"]])
            A("act", lambda h: h.activation(out=tb["sin"].ap, in_=tb["y"].ap, func=AF.Sin, scale=cst.ap[:, 2:3]),
              [tb["y"], cst], [tb["sin"]])
            A("dve", lambda h: h.scalar_tensor_tensor(out=tb["v"].ap, in0=tb["y"].ap, scalar=-1.0, in1=tb["y"].ap,
                                                      op0=ALU.mult, op1=ALU.max),
              [tb["y"]], [tb["v"]])
            A("act", lambda h: h.activation(out=tb["cos"].ap, in_=tb["v"].ap, func=AF.Sin, scale=-1.0,
                                            bias=halfpi.ap[:, 0:1]), [tb["v"], halfpi], [tb["cos"]])

        def mm_group(out, pairs, tile_position=None, start=True, stop=True, skip=False):
            n = len(pairs)
            for k, (l, r) in enumerate(pairs):
                kw = {}
                if tile_position is not None:
                    kw["tile_position"] = tile_position
                if skip:
                    kw["skip_group_check"] = True
                A("pe", lambda h, l=l, r=r, k=k, kw=kw: h.matmul(
                    out.ap, lhsT=l.ap, rhs=r.ap, start=(start and k == 0), stop=(stop and k == n - 1), **kw),
                  [l, r], [out])

        def rope_apply(ps_a, ps_b, tb, tmpa, tmpb, dst):
            A("dve", lambda h: h.tensor_tensor(out=tmpa.ap, in0=ps_a.ap, in1=tb["cos"].ap, op=ALU.mult),
              [ps_a, tb["cos"]], [tmpa])
            A("dve", lambda h: h.tensor_tensor(out=tmpb.ap, in0=ps_b.ap, in1=tb["sin"].ap, op=ALU.mult),
              [ps_b, tb["sin"]], [tmpb])
            A("pool", lambda h: h.tensor_tensor(out=dst.ap, in0=tmpa.ap, in1=tmpb.ap, op=ALU.add),
              [tmpa, tmpb], [dst])

        def mk_tables(keys=("pos", "ang", "v", "vi", "y", "cos", "sin")):
            return {k: sb([128, CH], I32 if k == "vi" else F32) for k in keys}

        top[0] = persist_top
        wckv = sb([128, 8, 2 * D], BF16)
        memx = sb([128, 2, D], F32)
        memxn = [sb([128, D], BF16) for _ in range(2)]
        memT = sb([128, 8, MEM], BF16)
        dma("pool", wckv.ap[:, :, 0:1024], w_ckv[:, 0:1024].rearrange("(k p) c -> p k c", p=128), [], [wckv])
        dma("pool", wckv.ap[:, :, 1024:2048], w_ckv[:, 1024:2048].rearrange("(k p) c -> p k c", p=128), [], [wckv])
        dma("sp", memx.ap, memb.rearrange("(t p) d -> p t d", p=128), [], [memx])
        junksA = [sb([128, D], BF16) for _ in range(2)]
        trm = [ps(0, 0, 1024, BF16), ps(7, 0, 1024, BF16)]
        norm_run([(memx.i(0), 128), (memx.i(1), 128)], 24, memxn, trm,
                 [hT_dsts(memT, MEM, 0, 128), hT_dsts(memT, MEM, 128, 128)], junksA)
        for oc in range(8):
            o = ps(1 + oc % 2, 0, MEM)
            mm_group(o, [(Buf(sb_arena, wckv.lo, wckv.hi, wckv.ap[:, kc, oc * 128:(oc + 1) * 128]), memT.i(kc))
                         for kc in range(8)])
            dst = KcT.i(oc)
            A("act", lambda h, o=o, dst=dst: h.activation(out=dst.ap, in_=o.ap, func=AF.Copy), [o], [dst])
        for mt in range(2):
            for hf in range(2):
                o = ps(3 + (mt * 2 + hf) % 2)
                mm_group(o, [(Buf(sb_arena, memT.lo, memT.hi, memT.ap[:, kc, mt * 128:(mt + 1) * 128]),
                              Buf(sb_arena, wckv.lo, wckv.hi, wckv.ap[:, kc, 1024 + hf * 512:1024 + (hf + 1) * 512]))
                             for kc in range(8)])
                dst = Buf(sb_arena, Vc.lo, Vc.hi, Vc.ap[:, mt, hf * 512:(hf + 1) * 512])
                A("dve", lambda h, o=o, dst=dst: h.tensor_copy(out=dst.ap, in_=o.ap), [o], [dst])

        top[0] = persist_top
        wkv = sb([128, 8, 1536], BF16)
        xa = [sb([128, 4, D], F32) for _ in range(2)]
        xnA = [sb([128, D], BF16) for _ in range(4)]
        hTA = [sb([128, 8, CH], BF16) for _ in range(2)]
        kst = [[sb([128, CH], BF16) for _ in range(4)] for _ in range(2)]
        vst = [sb([128, 4, 4, 129], BF16) for _ in range(2)]
        tmpA = [sb([128, CH], F32) for _ in range(2)]
        tmpB = [sb([128, CH], F32) for _ in range(2)]
        tabs = [mk_tables() for _ in range(2)]
        dma("pool", wkv.ap[:, :, 0:512], w_in[:, 1024:1536].rearrange("(k p) c -> p k c", p=128), [], [wkv])
        dma("pool", wkv.ap[:, :, 512:1024], w_ksw.rearrange("(k p) c -> p k c", p=128), [], [wkv])
        dma("pool", wkv.ap[:, :, 1024:1536], w_in[:, 1536:2048].rearrange("(k p) c -> p k c", p=128), [], [wkv])
        for v in vst:
            A("pool", lambda h, v=v: h.memset(v.ap, 1.0), [], [v])
        junksA = [sb([128, D], BF16) for _ in range(4)]
        trA = [ps(0, 0, 1024, BF16), ps(1, 0, 1024, BF16)]

        def wsl(wb, kc, c0, c1):
            return Buf(sb_arena, wb.lo, wb.hi, wb.ap[:, kc, c0:c1])

        def load_xa(c):
            dma("sp", xa[c % 2].ap, xb[c * CH:(c + 1) * CH, :].rearrange("(t p) d -> p t d", p=128), [], [xa[c % 2]])

        def frontA(c):
            sl = c % 2
            rope_tables(posA[c:c + 1, :], tabs[sl])
            norm_run([(xa[sl].i(t), 128) for t in range(4)], 0, xnA, trA,
                     [hT_dsts(hTA[sl], CH, t * 128, 128) for t in range(4)], junksA)

        def backA(c):
            sl = c % 2
            tb = tabs[sl]
            hT = hTA[sl]
            for hh in range(4):
                pk = ps(2 + 2 * (hh % 2))
                pks = ps(3 + 2 * (hh % 2))
                mm_group(pk, [(wsl(wkv, kc, hh * 128, (hh + 1) * 128), hT.i(kc)) for kc in range(8)])
                mm_group(pks, [(wsl(wkv, kc, 512 + hh * 128, 512 + (hh + 1) * 128), hT.i(kc)) for kc in range(8)])
                rope_apply(pk, pks, tb, tmpA[hh % 2], tmpB[hh % 2], kst[sl][hh])
                dma("sp", KTs[hh, :, c * CH:(c + 1) * CH], kst[sl][hh].ap, [kst[sl][hh]],
                    [dr("KT%d" % hh, c * CH, (c + 1) * CH)])
            for t in range(4):
                pv = ps(6 + t % 2)
                mm_group(pv, [(Buf(sb_arena, hT.lo, hT.hi, hT.ap[:, kc, t * 128:(t + 1) * 128]),
                               wsl(wkv, kc, 1024, 1536)) for kc in range(8)])
                dst = Buf(sb_arena, vst[sl].lo, vst[sl].hi, vst[sl].ap[:, t, :, 0:128])
                A("act", lambda h, pv=pv, dst=dst: h.activation(
                    out=dst.ap, in_=pv.ap.rearrange("p (a b) -> p a b", a=4), func=AF.Copy), [pv], [dst])
            for hh in range(4):
                dma("sp", Vs[hh, :, 4 * c:4 * c + 4, :], vst[sl].ap[:, :, hh, :], [vst[sl]],
                    [dr("V%d" % hh, 4 * c, 4 * c + 4)])

        STOP = int(os.environ.get("K_STOP", "0"))
        if STOP == 1:
            pass
        elif os.environ.get("K_NOPIPE_A"):
            load_xa(0)
            for c in range(NCH):
                if c + 1 < NCH:
                    load_xa(c + 1)
                frontA(c)
                backA(c)
        else:
            load_xa(0)
            if NCH > 1:
                load_xa(1)
            frontA(0)
            for c in range(NCH):
                if c + 1 < NCH:
                    frontA(c + 1)
                if c + 2 < NCH:
                    load_xa(c + 2)
                backA(c)

        top[0] = persist_top
        xB = [sb([128, 4, D], F32) for _ in range(2)]
        xH = [sb([HALO, D], F32) for _ in range(2)]
        xnB = [sb([128, D], BF16) for _ in range(5)]
        W2 = HALO + CH
        hTB = sb([128, 8, CH], BF16)
        hTh = sb([128, 8, HALO], BF16)
        gT = sb([128, 8, CH], BF16)
        junkP = [sb([128, D], BF16) for _ in range(2)]
        NSL = 2
        slabs = [sb([128, 8, 1024], BF16) for _ in range(NSL)]
        stage_top = top[0]
        uT = sb([128, 4, W2], F32)
        zT = sb([128, 4, CH], BF16)
        qT = sb([128, 4, CH], BF16)
        tabB = mk_tables(("pos", "cos", "sin"))
        pa = sb([128, W2], F32)
        pb = sb([128, W2], F32)
        invc = sb([128, CH], F32)
        accS = sb([128, 8, 129], F32)
        rl = sb([128, 8], F32)
        rl2n = sb([128, 4], F32)
        ot2 = [sb([128, 128], F32) for _ in range(2)]
        oo = [sb([128, 128], F32) for _ in range(4)]
        yts = [sb([128, CH], BF16) for _ in range(2)]
        r1_top = top[0]
        NKV = 2
        ktb = [sb([128, 1024], BF16) for _ in range(NKV)]
        vtb = [sb([128, 8, 129], BF16) for _ in range(NKV)]
        NPT = 3
        ptb = [[sb([128, CH], BF16) for _ in range(2)] for _ in range(NPT)]
        r1_end = top[0]
        top[0] = r1_top
        tmpA2 = [sb([128, CH], F32)]
        tmpB2 = [sb([128, CH], F32)]
        tabB.update(mk_tables(("ang", "v", "vi", "y")))
        assert top[0] <= r1_end, (top[0], r1_end)
        top[0] = r1_end
        mix_top = top[0]
        top[0] = stage_top
        qcT = sb([128, 8, CH], BF16)
        pcT = [sb([128, 2, CH], BF16) for _ in range(2)]
        lnl = sb([128, CH], F32)
        rlc = [sb([128, CH], F32) for _ in range(2)]
        ocT = sb([128, 8, CH], BF16)
        cross_top = top[0]
        top[0] = stage_top
        aT = sb([128, 32, CH], BF16)
        rtmp = [sb([128, CH], BF16) for _ in range(3)]
        mlp_top = top[0]
        top[0] = max(mix_top, cross_top, mlp_top)
        print("SBUF bytes used (phase B):", top[0])

        slab_emitted = [0]
        slab_seq = [(i, s) for i in range(M) for s in range(NSLAB)]

        def slab_get(n):
            while slab_emitted[0] <= min(n + NSL - 1, len(slab_seq) - 1):
                k = slab_emitted[0]
                s = slab_seq[k][1]
                ncol = slab_defs[s][1]
                dstb = slabs[k % NSL]
                dma("sp", dstb.ap[:, :, 0:ncol], wbf[s], [wbf_bufs[s]], [dstb])
                slab_emitted[0] += 1
            return slabs[n % NSL]

        kv_seq = []
        for i in range(M):
            for hh in range(4):
                for kb in range(2 * (i + 1)):
                    kv_seq.append((i, hh, kb))
        kv_emitted = [0]

        def kv_fetch(k):
            if k >= len(kv_seq) or k < kv_emitted[0]:
                return
            assert k == kv_emitted[0]
            i, hh, kb = kv_seq[k]
            dma("sp", ktb[k % NKV].ap, KTs[hh, :, kb * 1024:(kb + 1) * 1024],
                [dr("KT%d" % hh, kb * 1024, (kb + 1) * 1024)], [ktb[k % NKV]])
            dma("sp", vtb[k % NKV].ap, Vs[hh, :, kb * 8:(kb + 1) * 8, :],
                [dr("V%d" % hh, kb * 8, (kb + 1) * 8)], [vtb[k % NKV]])
            kv_emitted[0] += 1

        def load_xB(i):
            dma("sp", xB[i % 2].ap, xo[i, HALO:HALO + CH, :].rearrange("(t p) d -> p t d", p=128), [], [xB[i % 2]])
            dma("sp", xH[i % 2].ap, xo[i, 0:HALO, :], [], [xH[i % 2]])

        def resid_proj(xcur, srcT, wslab, banks_ring):
            n = 0
            for t in range(4):
                for hf in range(2):
                    o = ps(banks_ring[n % len(banks_ring)])
                    n += 1
                    mm_group(o, [(Buf(sb_arena, srcT.lo, srcT.hi, srcT.ap[:, kc, t * 128:(t + 1) * 128]),
                                  wsl(wslab, kc, hf * 512, (hf + 1) * 512)) for kc in range(8)])
                    xs = Buf(sb_arena, xcur.lo + (t * D + hf * 512) * 4, xcur.lo + (t * D + (hf + 1) * 512) * 4,
                             xcur.ap[:, t, hf * 512:(hf + 1) * 512], 4)
                    A("dve", lambda h, o=o, xs=xs: h.tensor_tensor(out=xs.ap, in0=o.ap, in1=xs.ap, op=ALU.add),
                      [o, xs], [xs])

        trB = [ps(7, 0, 1024, BF16), ps(0, 0, 1024, BF16)]
        GB = [3, 4, 5, 6]

        def resid_norm(xcur, srcT, wslab, banks_ring, gofs):
            prev = None
            n = 0
            for t in range(4):
                for hf in range(2):
                    o = ps(banks_ring[n % len(banks_ring)])
                    n += 1
                    mm_group(o, [(Buf(sb_arena, srcT.lo, srcT.hi, srcT.ap[:, kc, t * 128:(t + 1) * 128]),
                                  wsl(wslab, kc, hf * 512, (hf + 1) * 512)) for kc in range(8)])
                    xs = Buf(sb_arena, xcur.lo + (t * D + hf * 512) * 4, xcur.lo + (t * D + (hf + 1) * 512) * 4,
                             xcur.ap[:, t, hf * 512:(hf + 1) * 512], 4)
                    A("dve", lambda h, o=o, xs=xs: h.tensor_tensor(out=xs.ap, in0=o.ap, in1=xs.ap, op=ALU.add),
                      [o, xs], [xs])
                norm_pre(xcur.i(t), 128, t, xnB[t], junkP[t % 2])
                if prev is not None:
                    norm_post(*prev)
                prev = (xnB[t], 128, trB[t % 2], gofs, hT_dsts(hTB, CH, t * 128, 128), (("act", "dve")[t % 2],))
            norm_post(*prev)
        slab_n = 0
        kv_n = 0
        if STOP == 0:
            load_xB(0)
        STOPB = int(os.environ.get("K_STOPB", "0"))
        for i in range(M if STOP == 0 else 0):
            xs_ = xB[i % 2]
            xh_ = xH[i % 2]
            if i + 1 < M:
                load_xB(i + 1)
            rope_tables(posB[i:i + 1, :], tabB)
            norm_run([(xs_.i(t), 128) for t in range(4)] + [(xh_, HALO)], 0, xnB, trB,
                     [hT_dsts(hTB, CH, t * 128, 128) for t in range(4)] + [[hTh.i(fc) for fc in range(8)]], junkP)
            if STOPB == 3:
                continue
            gb = 0
            w_qq = slab_get(slab_n); slab_n += 1
            for hh in range(4):
                pq = ps(GB[gb % 4]); gb += 1
                pqs = ps(GB[gb % 4]); gb += 1
                mm_group(pq, [(wsl(w_qq, kc, hh * 128, (hh + 1) * 128), hTB.i(kc)) for kc in range(8)])
                mm_group(pqs, [(wsl(w_qq, kc, 512 + hh * 128, 512 + (hh + 1) * 128), hTB.i(kc)) for kc in range(8)])
                rope_apply(pq, pqs, tabB, tmpA2[0], tmpB2[0], qT.i(hh))
            w_ug = slab_get(slab_n); slab_n += 1
            for g in range(4):
                o = ps(GB[gb % 4]); gb += 1
                mm_group(o, [(wsl(w_ug, kc, g * 128, (g + 1) * 128), hTB.i(kc)) for kc in range(8)])
                oh = ps(GB[gb % 4], 0, HALO); gb += 1
                mm_group(oh, [(wsl(w_ug, kc, g * 128, (g + 1) * 128), hTh.i(kc)) for kc in range(8)])
                um = Buf(sb_arena, uT.lo + (g * W2 + HALO) * 4, uT.lo + (g + 1) * W2 * 4, uT.ap[:, g, HALO:W2], 4)
                uh = Buf(sb_arena, uT.lo + g * W2 * 4, uT.lo + (g * W2 + HALO) * 4, uT.ap[:, g, 0:HALO], 4)
                A("act", lambda h, o=o, um=um: h.activation(out=um.ap, in_=o.ap, func=AF.Copy), [o], [um])
                A("dve", lambda h, oh=oh, uh=uh: h.tensor_copy(out=uh.ap, in_=oh.ap), [oh], [uh])
            for gc in range(8):
                if gc == 4:
                    w_g2 = slab_get(slab_n); slab_n += 1
                wg_, c0_ = (w_ug, 512 + gc * 128) if gc < 4 else (w_g2, (gc - 4) * 128)
                o = ps(GB[gb % 4]); gb += 1
                mm_group(o, [(wsl(wg_, kc, c0_, c0_ + 128), hTB.i(kc)) for kc in range(8)])
                dst = gT.i(gc)
                A("act", lambda h, o=o, dst=dst: h.activation(out=dst.ap, in_=o.ap, func=AF.Sigmoid), [o], [dst])
            for g, wdw in enumerate((2, 4, 8, 16)):
                U = uT.i(g)
                cur = U
                step = 1
                tmp_cycle = [pa, pb]
                ti = 0
                while step * 2 < wdw:
                    nxt = tmp_cycle[ti % 2]; ti += 1
                    A("pool", lambda h, cur=cur, nxt=nxt, step=step: h.tensor_tensor(
                        out=nxt.ap[:, 2 * step - 1:W2], in0=cur.ap[:, 2 * step - 1:W2],
                        in1=cur.ap[:, step - 1:W2 - step], op=ALU.add), [cur], [nxt])
                    cur = nxt
                    step *= 2
                nxt = tmp_cycle[ti % 2]; ti += 1
                A("pool", lambda h, cur=cur, nxt=nxt, step=step: h.tensor_tensor(
                    out=nxt.ap[:, HALO:W2], in0=cur.ap[:, HALO:W2], in1=cur.ap[:, HALO - step:W2 - step], op=ALU.add),
                  [cur], [nxt])
                A("dve", lambda h, wdw=wdw: h.tensor_scalar(out=invc.ap, in0=tabB["pos"].ap, scalar1=1.0,
                                                            scalar2=float(wdw), op0=ALU.add, op1=ALU.min),
                  [tabB["pos"]], [invc])
                A("act", lambda h: h.activation(out=invc.ap, in_=invc.ap, func=AF.Ln), [invc], [invc])
                A("act", lambda h: h.activation(out=invc.ap, in_=invc.ap, func=AF.Exp, scale=-1.0), [invc], [invc])
                A("dve", lambda h, nxt=nxt: h.tensor_tensor(out=nxt.ap[:, HALO:W2], in0=nxt.ap[:, HALO:W2],
                                                            in1=invc.ap, op=ALU.mult), [nxt, invc], [nxt])
                zg = zT.i(g)
                A("dve", lambda h, nxt=nxt, U=U, zg=zg: h.tensor_tensor(out=zg.ap, in0=nxt.ap[:, HALO:W2],
                                                                    in1=U.ap[:, HALO:W2], op=ALU.subtract),
                  [nxt, U], [zg])
            pending_pool = [0, 1, 2, 3]
            defer_pool = not os.environ.get("K_NODEFER")
            if STOPB == 4:
                continue

            def pool_out(g):
                o = ps(7)
                zg = zT.i(g)
                mm_group(o, [(Buf(sb_arena, wpool.lo, wpool.hi, wpool.ap[:, g, :]), zg)])
                dst = gT.i(g)
                A("dve", lambda h, o=o, dst=dst, g=g: h.scalar_tensor_tensor(
                    out=dst.ap, in0=o.ap, scalar=pscale.ap[:, g:g + 1], in1=dst.ap, op0=ALU.mult, op1=ALU.mult),
                  [o, pscale, dst], [dst])

            if not defer_pool:
                while pending_pool:
                    pool_out(pending_pool.pop(0))
            n_kt = 16 * (i + 1)
            pending_fin = []
            trF = ps(7, 0, 512, BF16)

            def fin_part2(hp):
                for qt in range(4):
                    yq = yts[hp % 2].c(qt * 128, (qt + 1) * 128)
                    dst = trF.c(qt * 128, (qt + 1) * 128)
                    A("pe", lambda h, yq=yq, dst=dst: h.transpose(out=dst.ap, in_=yq.ap, identity=ident.ap),
                      [yq, ident], [dst])
                dst = gT.i(4 + hp)
                A("dve", lambda h, dst=dst: h.tensor_tensor(out=dst.ap, in0=trF.ap, in1=dst.ap, op=ALU.mult),
                  [trF, dst], [dst])

            for hh in range(4):
                accs = []
                for a in range(8):
                    bk, off = a // 3, (a % 3) * 129
                    accs.append(ps(bk, off, off + 129))
                qh = qT.i(hh)
                q_lo = Buf(sb_arena, qh.lo, qh.hi, qh.ap[0:64, :])
                q_hi = Buf(sb_arena, qh.lo, qh.hi, qh.ap[64:128, :])
                sb_ = [[ps(3, 0, 512), ps(4, 0, 512)], [ps(5, 0, 512), ps(6, 0, 512)]]
                blocks = {}

                kv_base = kv_n
                kv_n += 2 * (i + 1)

                def qk(kt):
                    kb, k8 = kt // 8, kt % 8
                    n_ = kv_base + kb
                    kv_fetch(n_)
                    blocks[kb] = (ktb[n_ % NKV], vtb[n_ % NKV])
                    kt_b, _ = blocks[kb]
                    for cpt in range(2):
                        l = Buf(sb_arena, kt_b.lo, kt_b.hi, kt_b.ap[64 * cpt:64 * (cpt + 1), k8 * 128:(k8 + 1) * 128])
                        r = q_lo if cpt == 0 else q_hi
                        mm_group(sb_[kt % 2][cpt], [(l, r)], tile_position=(64 * cpt, 0))

                qk(0)
                for kt in range(n_kt):
                    if kt + 1 < n_kt:
                        qk(kt + 1)
                    if kt == 6 and pending_fin:
                        fin_part2(pending_fin.pop(0))
                    if kt in (8, 9, 10, 11) and pending_pool:
                        pool_out(pending_pool.pop(0))
                    kb, k8 = kt // 8, kt % 8
                    if k8 == 0:
                        nxt_ = kv_base + kb + 1
                        if nxt_ < len(kv_seq) and kv_seq[nxt_][0] == i:
                            kv_fetch(nxt_)
                    _, v_b = blocks[kb]
                    pts = ptb[kt % NPT]
                    for cpt in range(2):
                        s_ = sb_[kt % 2][cpt]
                        p_ = pts[cpt]
                        A("act", lambda h, s_=s_, p_=p_: h.activation(out=p_.ap, in_=s_.ap, func=AF.Exp, scale=0.125),
                          [s_], [p_])
                        if kt >= n_kt - 16:
                            rr = kt - (n_kt - 16)
                            A("dve", lambda h, p_=p_, rr=rr: h.scalar_tensor_tensor(
                                out=p_.ap, in0=qkb.ap, scalar=thr.ap[:, rr:rr + 1], in1=p_.ap,
                                op0=ALU.is_ge, op1=ALU.mult), [qkb, thr, p_], [p_])
                    vk = Buf(sb_arena, v_b.lo, v_b.hi, v_b.ap[:, k8, :])
                    for cpt in range(2):
                        for qt in range(4):
                            a = cpt * 4 + qt
                            l = Buf(sb_arena, pts[cpt].lo, pts[cpt].hi, pts[cpt].ap[:, qt * 128:(qt + 1) * 128])
                            A("pe", lambda h, a=a, l=l, vk=vk, kt=kt: h.matmul(
                                accs[a].ap, lhsT=l.ap, rhs=vk.ap, start=(kt == 0 and a % 3 == 0),
                                stop=(kt == n_kt - 1), skip_group_check=True), [l, vk], [accs[a]])
                for bk in range(3):
                    na = 3 if bk < 2 else 2
                    src = ps(bk, 0, na * 129)
                    dst = Buf(sb_arena, accS.lo + bk * 3 * 129 * 4, accS.lo + (bk * 3 + na) * 129 * 4,
                              accS.ap[:, bk * 3:bk * 3 + na, :], 4)
                    eng = "act" if bk == 1 else "dve"
                    if eng == "act":
                        A("act", lambda h, src=src, dst=dst, na=na: h.activation(
                            out=dst.ap, in_=src.ap.rearrange("p (a b) -> p a b", a=na), func=AF.Copy), [src], [dst])
                    else:
                        A("dve", lambda h, src=src, dst=dst, na=na: h.tensor_copy(
                            out=dst.ap, in_=src.ap.rearrange("p (a b) -> p a b", a=na)), [src], [dst])
                if debug and i == 0:
                    dma("sp", dbg["acc"][hh], accS.ap, [accS], [])
                    if hh == 0:
                        dma("sp", dbg["q"], qT.ap, [qT], [])
                A("dve", lambda h: h.reciprocal(out=rl.ap, in_=accS.ap[:, :, 128]), [accS], [rl])
                A("dve", lambda h: h.tensor_scalar(out=rl2n.ap, in0=rl.ap[:, 4:8], scalar1=neglam.ap[:, 0:1],
                                                   scalar2=None, op0=ALU.mult), [rl, neglam], [rl2n])
                for qt in range(4):
                    t2 = ot2[qt % 2]
                    o_ = oo[qt]
                    A("dve", lambda h, qt=qt, t2=t2: h.tensor_scalar(
                        out=t2.ap, in0=accS.ap[:, 4 + qt, 0:128], scalar1=rl2n.ap[:, qt:qt + 1], scalar2=None,
                        op0=ALU.mult), [accS, rl2n], [t2])
                    A("dve", lambda h, qt=qt, t2=t2, o_=o_: h.scalar_tensor_tensor(
                        out=o_.ap, in0=accS.ap[:, qt, 0:128], scalar=rl.ap[:, qt:qt + 1], in1=t2.ap,
                        op0=ALU.mult, op1=ALU.add), [accS, rl, t2], [o_])
                rms_stats([(oo[qt], 128) for qt in range(4)], 128, junk_s)
                for qt in range(4):
                    o_ = oo[qt]
                    yq = yts[hh % 2].c(qt * 128, (qt + 1) * 128)
                    A("dve", lambda h, qt=qt, o_=o_, yq=yq: h.scalar_tensor_tensor(
                        out=yq.ap, in0=o_.ap, scalar=rstd.ap[:, qt:qt + 1], in1=gsub.ap, op0=ALU.mult, op1=ALU.mult),
                      [o_, rstd.c(qt, qt + 1), gsub], [yq])

                if debug and i == 0:
                    dma("sp", dbg["yt"][hh], yts[hh % 2].ap, [yts[hh % 2]], [])
                pending_fin.append(hh)
            while pending_pool:
                pool_out(pending_pool.pop(0))
            while pending_fin:
                fin_part2(pending_fin.pop(0))
            if STOPB == 5:
                continue
            if debug:
                dma("sp", dbg["mixT"][i], gT.ap, [gT], [])
            w_o = slab_get(slab_n); slab_n += 1
            resid_norm(xs_, gT, w_o, GB, 8)
            if STOPB == 6:
                continue
            if debug:
                dma("sp", dbg["x1"][i * CH:(i + 1) * CH, :].rearrange("(t p) d -> p t d", p=128), xs_.ap, [xs_], [])
            w_q = slab_get(slab_n); slab_n += 1
            gb = 0
            for oc in range(8):
                o = ps(GB[gb % 4]); gb += 1
                mm_group(o, [(wsl(w_q, kc, oc * 128, (oc + 1) * 128), hTB.i(kc)) for kc in range(8)])
                dst = qcT.i(oc)
                if oc % 2 == 0:
                    A("act", lambda h, o=o, dst=dst: h.activation(out=dst.ap, in_=o.ap, func=AF.Copy), [o], [dst])
                else:
                    A("dve", lambda h, o=o, dst=dst: h.tensor_copy(out=dst.ap, in_=o.ap), [o], [dst])
            for hh in range(4):
                pc = pcT[hh % 2]
                for mt in range(2):
                    o = ps(GB[gb % 4]); gb += 1
                    mm_group(o, [(Buf(sb_arena, KcT.lo, KcT.hi, KcT.ap[:, 2 * hh + dc, mt * 128:(mt + 1) * 128]),
                                  qcT.i(2 * hh + dc)) for dc in range(2)])
                    dst = pc.i(mt)
                    A("act", lambda h, o=o, dst=dst: h.activation(out=dst.ap, in_=o.ap, func=AF.Exp, scale=1.0 / 16.0),
                      [o], [dst])
                ol = ps(GB[gb % 4]); gb += 1
                mm_group(ol, [(ones, pc.i(mt)) for mt in range(2)])
                rc = rlc[hh % 2]
                A("act", lambda h, ol=ol: h.activation(out=lnl.ap, in_=ol.ap, func=AF.Ln), [ol], [lnl])
                A("act", lambda h, rc=rc: h.activation(out=rc.ap, in_=lnl.ap, func=AF.Exp, scale=-1.0), [lnl], [rc])
                for dc in range(2):
                    o = ps(GB[gb % 4]); gb += 1
                    mm_group(o, [(Buf(sb_arena, Vc.lo, Vc.hi, Vc.ap[:, mt, (2 * hh + dc) * 128:(2 * hh + dc + 1) * 128]),
                                  pc.i(mt)) for mt in range(2)])
                    dst = ocT.i(2 * hh + dc)
                    A("dve", lambda h, o=o, dst=dst, rc=rc: h.tensor_tensor(out=dst.ap, in0=o.ap, in1=rc.ap, op=ALU.mult),
                      [o, rc], [dst])
            w_c = slab_get(slab_n); slab_n += 1
            resid_norm(xs_, ocT, w_c, GB, 16)
            if STOPB == 7:
                continue
            if debug:
                dma("sp", dbg["x2"][i * CH:(i + 1) * CH, :].rearrange("(t p) d -> p t d", p=128), xs_.ap, [xs_], [])
            gb = 0
            for su in range(4):
                w_u = slab_get(slab_n); slab_n += 1
                for o8 in range(8):
                    oc = su * 8 + o8
                    o = ps(GB[gb % 4]); gb += 1
                    mm_group(o, [(wsl(w_u, kc, o8 * 128, (o8 + 1) * 128), hTB.i(kc)) for kc in range(8)])
                    rt = rtmp[oc % 3]
                    A("act", lambda h, o=o, rt=rt: h.activation(out=rt.ap, in_=o.ap, func=AF.Relu), [o], [rt])
                    dst = aT.i(oc)
                    A("dve", lambda h, o=o, rt=rt, dst=dst: h.tensor_tensor(out=dst.ap, in0=o.ap, in1=rt.ap, op=ALU.mult),
                      [o, rt], [dst])
            if STOPB == 8:
                continue
            dacc = [ps(b_) for b_ in range(8)]
            for sd in range(4):
                w_d = slab_get(slab_n); slab_n += 1
                for t in range(4):
                    for hf in range(2):
                        o = dacc[t * 2 + hf]
                        for k8 in range(8):
                            kc = sd * 8 + k8
                            l = Buf(sb_arena, aT.lo, aT.hi, aT.ap[:, kc, t * 128:(t + 1) * 128])
                            r = wsl(w_d, k8, hf * 512, (hf + 1) * 512)
                            A("pe", lambda h, o=o, l=l, r=r, kc=kc: h.matmul(
                                o.ap, lhsT=l.ap, rhs=r.ap, start=(kc == 0), stop=(kc == 31)), [l, r], [o])
            for t in range(4):
                for hf in range(2):
                    o = dacc[t * 2 + hf]
                    xsl = Buf(sb_arena, xs_.lo + (t * D + hf * 512) * 4, xs_.lo + (t * D + (hf + 1) * 512) * 4,
                              xs_.ap[:, t, hf * 512:(hf + 1) * 512], 4)
                    A("dve", lambda h, o=o, xsl=xsl: h.tensor_tensor(out=xsl.ap, in0=o.ap, in1=xsl.ap, op=ALU.add),
                      [o, xsl], [xsl])
            if STOPB == 9:
                continue
            rms_stats([(xs_.i(t), 128) for t in range(4)], D, junkP)
            if STOPB == 10:
                continue
            for t in range(4):
                xt = xs_.i(t)
                A("dve", lambda h, xt=xt, t=t: h.scalar_tensor_tensor(
                    out=xt.ap, in0=xt.ap, scalar=rstd.ap[:, t:t + 1], in1=gfin.ap, op0=ALU.mult, op1=ALU.mult),
                  [xt, rstd.c(t, t + 1), gfin], [xt])
            if STOPB == 11:
                continue
            dma("sp", out_d[i * CH:(i + 1) * CH, :].rearrange("(t p) d -> p t d", p=128), xs_.ap, [xs_],
                [dr("out", i, i + 1)])

        with nc.Block() as block:
            sched.emit(nc, eng_sems, dma_sems, block)
    return nc


def make_in_maps(S, x, mem, g_mix, w_in, w_pool, pool_scale, lambda_q1, lambda_k1, lambda_q2, lambda_k2,
                 g_subln, w_out, g_cross, g_mem, w_cq, w_ckv, w_co, g_mlp, w_up, w_down, g_final):
    f = lambda a: np.ascontiguousarray(np.asarray(a, dtype=np.float32))
    NCH = S // CH
    M = NCH // CPB
    w_in0 = f(w_in[0])
    perm = np.arange(512).reshape(8, 2, 32)[:, ::-1, :].reshape(512)
    w_qsw = f(w_in0[:, 512:1024][:, perm])
    w_ksw = f(w_in0[:, 1024:1536][:, perm])
    col = lambda g: np.asarray(g, np.float32).reshape(8, 128).T
    gcols = f(np.concatenate([col(g_mix[0]), col(g_cross[0]), col(g_mlp[0]), col(g_mem[0])], axis=1))
    pscale = f(np.asarray(pool_scale[0], np.float32).reshape(4, 128).T)
    lam = f(np.concatenate([lambda_q1[0], lambda_k1[0], lambda_q2[0], lambda_k2[0]]).reshape(1, 256))
    p = np.arange(128)
    inv_freq = (10000.0 ** (-(np.arange(0, 64, 2, dtype=np.float32)) / np.float32(64))).astype(np.float32)
    invf = inv_freq[p % 32]
    sgn = np.where((p % 64) < 32, -1.0, 1.0).astype(np.float32)
    ident = np.eye(128, dtype=np.float32)
    qkbase = f(np.arange(CH)[None, :] - np.arange(128)[:, None])
    posA = f(np.arange(S).reshape(NCH, CH))
    shared = {
        "posA": posA, "qkbase": qkbase, "ident": ident, "w_in": w_in0, "w_qsw": w_qsw, "w_ksw": w_ksw,
        "w_pool": f(w_pool[0]), "w_out": f(w_out[0]), "w_cq": f(w_cq[0]), "w_ckv": f(w_ckv[0]),
        "w_co": f(w_co[0]), "w_up": f(w_up[0]), "w_down": f(w_down[0]), "gcols": gcols, "pscale": pscale,
        "gfin": f(np.asarray(g_final).reshape(1, D)), "gsub": f(np.asarray(g_subln[0]).reshape(1, 128)), "lam": lam,
    }
    x = np.asarray(x, np.float32)
    mem = np.asarray(mem, np.float32)
    in_maps = []
    for c in range(NCORE):
        b, j = c // CPB, c % CPB
        xo = np.zeros((M, HALO + CH, D), np.float32)
        posB = np.zeros((M, CH), np.float32)
        for m in range(M):
            t0 = (CPB * m + j) * CH
            lo = max(t0 - HALO, 0)
            xo[m, HALO - (t0 - lo):] = x[b, lo:t0 + CH]
            posB[m] = np.arange(t0, t0 + CH)
        consts = np.zeros((128, 4), np.float32)
        consts[:, 0] = invf
        consts[:, 1] = invf / np.float32(TWO_PI)
        consts[:, 2] = sgn
        thr = np.zeros((128, 16), np.float32)
        thr[:, :] = (128.0 * np.arange(16) - 512.0 * j)[None, :]
        d = dict(shared)
        d.update({"xb": f(x[b]), "xo": xo, "posB": posB, "consts": consts, "thr": thr, "memb": f(mem[b])})
        in_maps.append(d)
    return in_maps


_NC_CACHE = {}


def run(S, inputs, debug=False):
    key = (S, debug)
    if key not in _NC_CACHE:
        _NC_CACHE[key] = build(S, debug)
    nc = _NC_CACHE[key]
    in_maps = make_in_maps(S, **inputs)
    res = run_bass_kernel_spmd(nc, in_maps, core_ids=list(range(NCORE)))
    return res


def assemble(S, results, key="out"):
    NCH = S // CH
    M = NCH // CPB
    out = np.zeros((2, S, D), np.float32)
    for c in range(NCORE):
        b, j = c // CPB, c % CPB
        o = np.asarray(results[c][key]).reshape(M, CH, D)
        for m in range(M):
            t0 = (CPB * m + j) * CH
            out[b, t0:t0 + CH] = o[m]
    return out


def kernel(**inputs):
    S = int(np.asarray(inputs["x"]).shape[1])
    res = run(S, inputs)
    return assemble(S, res.results)
```

```python
import bisect
import math
import os
import numpy as np
import concourse.bass as bass
import concourse.mybir as mybir
from concourse.bass_utils import run_bass_kernel_spmd

F32 = mybir.dt.float32
BF16 = mybir.dt.bfloat16
I32 = mybir.dt.int32
U8 = mybir.dt.uint8
ALU = mybir.AluOpType
AF = mybir.ActivationFunctionType
AX = mybir.AxisListType

D = 1024
CH = 512
NCORE = 8
CPB = 4
HALO = 16
MEM = 256
DFF = 4096
EPS = 1e-6
LAM_INIT = 0.8 - 0.6 * math.exp(0.0)
TWO_PI = 2.0 * math.pi
CW1 = 6.28125
CW2 = TWO_PI - CW1
PI_LO = 3.1415925
NDMA = 48
NSWDMA = 8
ESZ = {F32: 4, BF16: 2, I32: 4, U8: 1}


class Arena:
    def __init__(self, size):
        self.b = [0, size]
        self.w = [None]
        self.r = [{}]

    def _split(self, x):
        i = bisect.bisect_right(self.b, x) - 1
        if self.b[i] == x:
            return i
        self.b.insert(i + 1, x)
        self.w.insert(i + 1, self.w[i])
        self.r.insert(i + 1, dict(self.r[i]))
        return i + 1

    def rng(self, lo, hi):
        i = self._split(lo)
        j = self._split(hi)
        return range(i, j)


class Buf:
    def __init__(self, arena, lo, hi, ap, esz=1):
        self.arena, self.lo, self.hi, self.ap, self.esz = arena, lo, hi, ap, esz

    def c(self, c0, c1):
        return Buf(self.arena, self.lo + c0 * self.esz, self.lo + c1 * self.esz,
                   self.ap[:, c0:c1], self.esz)

    def i(self, k):
        n = self.ap.shape[2]
        return Buf(self.arena, self.lo + k * n * self.esz, self.lo + (k + 1) * n * self.esz,
                   self.ap[:, k, :], self.esz)


class Op:
    __slots__ = ("eng", "fn", "deps", "idx", "inc", "val", "is_dma", "sem_id", "key")


class Sched:
    ENGS = ("pe", "act", "dve", "pool", "sp")

    def __init__(self):
        self.ops = {e: [] for e in self.ENGS}
        self.dma_rr = 0
        self.sw_rr = 0
        self.dma_last = [None] * NDMA
        self.dma_cnt = [0] * NDMA
        self.nuid = 0

    def add(self, eng, fn, reads=(), writes=(), dma=False):
        op = Op()
        op.eng, op.fn, op.is_dma, op.inc, op.val, op.sem_id = eng, fn, dma, False, 0, -1
        op.idx = len(self.ops[eng])
        if dma:
            self.nuid += 1
            op.key = ("d", self.nuid)
        else:
            op.key = eng
        deps = {}

        def dep(o, raw):
            if o is None or o is op:
                return
            if (not dma) and (not o.is_dma) and o.eng == eng:
                if eng == "pe":
                    return
            p = deps.get(o.key)
            if p is None or o.idx > p.idx:
                deps[o.key] = o

        for r in reads:
            ar = r.arena
            for s in ar.rng(r.lo, r.hi):
                dep(ar.w[s], True)
        for r in writes:
            ar = r.arena
            for s in ar.rng(r.lo, r.hi):
                dep(ar.w[s], False)
                for o in ar.r[s].values():
                    dep(o, False)
        for r in reads:
            ar = r.arena
            for s in ar.rng(r.lo, r.hi):
                ar.r[s][op.key] = op
        for r in writes:
            ar = r.arena
            for s in ar.rng(r.lo, r.hi):
                ar.w[s] = op
                ar.r[s] = {}
        if dma:
            if eng == "pool":
                sid = NDMA - NSWDMA + self.sw_rr % NSWDMA
                self.sw_rr += 1
            else:
                sid = self.dma_rr % (NDMA - NSWDMA)
                self.dma_rr += 1
            prev = self.dma_last[sid]
            if prev is not None:
                deps[prev.key] = prev
            self.dma_last[sid] = op
            self.dma_cnt[sid] += 16
            op.sem_id = sid
            op.val = self.dma_cnt[sid]
        op.deps = deps
        self.ops[eng].append(op)
        return op

    def emit(self, nc, eng_sems, dma_sems, block):
        for e in self.ENGS:
            for op in self.ops[e]:
                for d in op.deps.values():
                    if not d.is_dma:
                        d.inc = True
        for e in self.ENGS:
            c = 0
            for op in self.ops[e]:
                if not op.is_dma and op.inc:
                    c += 1
                    op.val = c

        def run(e, h):
            seen = {}
            for op in self.ops[e]:
                waits = {}
                for d in op.deps.values():
                    k = ("d", d.sem_id) if d.is_dma else ("e", d.eng)
                    if waits.get(k, 0) < d.val:
                        waits[k] = d.val
                for k, v in waits.items():
                    if seen.get(k, 0) >= v:
                        continue
                    seen[k] = v
                    sem = dma_sems[k[1]] if k[0] == "d" else eng_sems[k[1]]
                    h.wait_ge(sem, v)
                ins = op.fn(h)
                if op.is_dma:
                    ins.then_inc(dma_sems[op.sem_id], 16)
                elif op.inc:
                    ins.then_inc(eng_sems[e], 1)
            if e == "sp":
                for sid in range(NDMA):
                    if self.dma_cnt[sid] > 0 and seen.get(("d", sid), 0) < self.dma_cnt[sid]:
                        h.wait_ge(dma_sems[sid], self.dma_cnt[sid])

        @block.tensor
        def _(h):
            run("pe", h)

        @block.scalar
        def _(h):
            run("act", h)

        @block.vector
        def _(h):
            run("dve", h)

        @block.gpsimd
        def _(h):
            run("pool", h)

        @block.sync
        def _(h):
            run("sp", h)


def build(S, debug=False):
    NCH = S // CH
    M = NCH // CPB
    NT = S // 128
    nc = bass.Bass("TRN2", target_bir_lowering=False)

    def din(name, shape):
        return nc.dram_tensor(name, list(shape), F32, kind="ExternalInput").ap()

    xb = din("xb", [S, D])
    xo = din("xo", [M, HALO + CH, D])
    posA = din("posA", [NCH, CH])
    posB = din("posB", [M, CH])
    consts = din("consts", [128, 4])
    thr_d = din("thr", [128, 16])
    qkb_d = din("qkbase", [128, CH])
    ident_d = din("ident", [128, 128])
    memb = din("memb", [MEM, D])
    w_in = din("w_in", [D, 3072])
    w_qsw = din("w_qsw", [D, 512])
    w_ksw = din("w_ksw", [D, 512])
    w_pool = din("w_pool", [4, 128, 128])
    w_out = din("w_out", [D, D])
    w_cq = din("w_cq", [D, D])
    w_ckv = din("w_ckv", [D, 2 * D])
    w_co = din("w_co", [D, D])
    w_up = din("w_up", [D, DFF])
    w_down = din("w_down", [DFF, D])
    gcols_d = din("gcols", [128, 32])
    pscale_d = din("pscale", [128, 4])
    gfin_d = din("gfin", [1, D])
    gsub_d = din("gsub", [1, 128])
    lam_d = din("lam", [1, 256])
    out_d = nc.dram_tensor("out", [M * CH, D], F32, kind="ExternalOutput").ap()
    skind = "ExternalOutput" if debug else "Internal"
    KTs = nc.dram_tensor("KTs", [4, 128, S], BF16, kind=skind).ap()
    Vs = nc.dram_tensor("Vs", [4, 128, NT, 129], BF16, kind=skind).ap()
    slab_defs = [
        ("qq", 1024, [(w_in, 0, 512, 512, 0), (w_qsw, 0, 0, 512, 512)]),
        ("ug", 1024, [(w_in, 0, 0, 512, 0), (w_in, 0, 2048, 512, 512)]),
        ("g2", 512, [(w_in, 0, 2560, 512, 0)]),
        ("out", 1024, [(w_out, 0, 0, 1024, 0)]),
        ("cq", 1024, [(w_cq, 0, 0, 1024, 0)]),
        ("co", 1024, [(w_co, 0, 0, 1024, 0)]),
    ]
    for i in range(4):
        slab_defs.append(("up%d" % i, 1024, [(w_up, 0, 1024 * i, 1024, 0)]))
    for i in range(4):
        slab_defs.append(("dn%d" % i, 1024, [(w_down, 1024 * i, 0, 1024, 0)]))
    NSLAB = len(slab_defs)
    wbf = [nc.dram_tensor("wbf_" + sd[0], [128, 8, sd[1]], BF16, kind="Internal").ap()
           for sd in slab_defs]
    dbg = {}
    if debug:
        dbg["x1"] = nc.dram_tensor("dbg_x1", [M * CH, D], F32, kind="ExternalOutput").ap()
        dbg["mixT"] = nc.dram_tensor("dbg_mixT", [M, 128, 8, CH], BF16, kind="ExternalOutput").ap()
        dbg["x2"] = nc.dram_tensor("dbg_x2", [M * CH, D], F32, kind="ExternalOutput").ap()
        dbg["acc"] = nc.dram_tensor("dbg_acc", [4, 128, 8, 129], F32, kind="ExternalOutput").ap()
        dbg["yt"] = nc.dram_tensor("dbg_yt", [4, 128, CH], BF16, kind="ExternalOutput").ap()
        dbg["q"] = nc.dram_tensor("dbg_q", [128, 4, CH], BF16, kind="ExternalOutput").ap()

    SB_BYTES = 175 * 1024
    sched = Sched()
    sb_arena = Arena(SB_BYTES)
    ps_arena = Arena(8 * 2048)
    dram_arenas = {}

    from contextlib import ExitStack
    with ExitStack() as es:
        big = es.enter_context(nc.sbuf_tensor("big", [128, SB_BYTES], U8))
        banks = [es.enter_context(nc.psum_tensor("bank%d" % i, [128, 512], F32)) for i in range(8)]
        eng_sems = {e: es.enter_context(nc.semaphore("sem_" + e)) for e in Sched.ENGS}
        dma_sems = [es.enter_context(nc.semaphore("dsem%d" % i)) for i in range(NDMA)]

        top = [0]

        def sb(shape, dt):
            esz = ESZ[dt]
            n = 1
            for v in shape[1:]:
                n *= v
            nb = (n * esz + 63) // 64 * 64
            lo = top[0]
            top[0] += nb
            assert top[0] <= SB_BYTES, ("SBUF overflow", top[0])
            ap = big[0:shape[0], lo:lo + n * esz].bitcast(dt)
            if len(shape) == 3:
                ap = ap.rearrange("p (a b) -> p a b", a=shape[1])
            elif len(shape) == 4:
                ap = ap.rearrange("p (a b c) -> p a b c", a=shape[1], b=shape[2])
            return Buf(sb_arena, lo, lo + n * esz, ap, esz)

        def ps(bank, c0=0, c1=512, dt=F32):
            esz = ESZ[dt]
            ap = banks[bank][:, :]
            if dt != F32:
                ap = ap.bitcast(dt)
            b_ = Buf(ps_arena, bank * 2048, (bank + 1) * 2048, ap[:, c0:c1], esz)
            b_.c = lambda a0, a1, b_=b_: Buf(ps_arena, b_.lo, b_.hi, b_.ap[:, a0:a1], esz)
            return b_

        def dr(name, lo, hi):
            if name not in dram_arenas:
                dram_arenas[name] = Arena(1 << 40)
            return Buf(dram_arenas[name], lo, hi, None)

        A = sched.add

        def dma(q, out, in_, reads, writes):
            return A(q, lambda h, o=out, i=in_: h.dma_start(out=o, in_=i), reads, writes, dma=True)

        ident = sb([128, 128], BF16)
        ones = sb([128, 128], BF16)
        cst = sb([128, 4], F32)
        thr = sb([128, 16], F32)
        qkb = sb([128, CH], F32)
        gcols = sb([128, 32], F32)
        pscale = sb([128, 4], F32)
        gfin = sb([128, D], F32)
        gsub = sb([128, 128], F32)
        lamt = sb([128, 256], F32)
        lamw = sb([128, 8], F32)
        neglam = sb([128, 1], F32)
        wpool = sb([128, 4, 128], BF16)
        KcT = sb([128, 8, MEM], BF16)
        Vc = sb([128, 2, D], BF16)
        ssn = sb([128, 16], F32)
        lnv = sb([128, 16], F32)
        rstd = sb([128, 16], F32)

        dma("pool", ident.ap, ident_d, [], [ident])
        dma("sp", cst.ap, consts, [], [cst])
        dma("sp", thr.ap, thr_d, [], [thr])
        dma("sp", qkb.ap, qkb_d, [], [qkb])
        dma("sp", gcols.ap, gcols_d, [], [gcols])
        dma("sp", pscale.ap, pscale_d, [], [pscale])
        dma("sp", gfin.ap, gfin_d[0:1, :].broadcast_to([128, D]), [], [gfin])
        dma("sp", gsub.ap, gsub_d[0:1, :].broadcast_to([128, 128]), [], [gsub])
        dma("sp", lamt.ap, lam_d[0:1, :].broadcast_to([128, 256]), [], [lamt])
        dma("pool", wpool.ap, w_pool.rearrange("g c d -> c g d"), [], [wpool])
        A("dve", lambda h: h.memset(ones.ap, 1.0), [], [ones])
        A("dve", lambda h: h.memset(ssn.ap, 1.0), [], [ssn])
        A("dve", lambda h: h.tensor_tensor(out=lamt.ap[:, 0:64], in0=lamt.ap[:, 0:64],
                                           in1=lamt.ap[:, 64:128], op=ALU.mult), [lamt], [lamt])
        A("dve", lambda h: h.tensor_tensor(out=lamt.ap[:, 128:192], in0=lamt.ap[:, 128:192],
                                           in1=lamt.ap[:, 192:256], op=ALU.mult), [lamt], [lamt])
        A("dve", lambda h: h.reduce_sum(out=lamw.ap[:, 0:1], in_=lamt.ap[:, 0:64], axis=AX.X), [lamt], [lamw])
        A("dve", lambda h: h.reduce_sum(out=lamw.ap[:, 1:2], in_=lamt.ap[:, 128:192], axis=AX.X), [lamt], [lamw])
        A("act", lambda h: h.activation(out=lamw.ap[:, 2:4], in_=lamw.ap[:, 0:2], func=AF.Exp), [lamw], [lamw])
        A("dve", lambda h: h.tensor_tensor(out=lamw.ap[:, 4:5], in0=lamw.ap[:, 3:4], in1=lamw.ap[:, 2:3],
                                           op=ALU.subtract), [lamw], [lamw])
        A("dve", lambda h: h.tensor_scalar(out=neglam.ap, in0=lamw.ap[:, 4:5], scalar1=-LAM_INIT, scalar2=None,
                                           op0=ALU.add), [lamw], [neglam])
        A("dve", lambda h: h.tensor_scalar(out=gsub.ap, in0=gsub.ap, scalar1=1.0 - LAM_INIT, scalar2=None,
                                           op0=ALU.mult), [gsub], [gsub])

        wbf_bufs = []
        for s, (nm, ncol, parts) in enumerate(slab_defs):
            b = dr("wbf%d" % s, 0, 1)
            wbf_bufs.append(b)
            for (src, r0, c0, n_, d0) in parts:
                dma("pool", wbf[s][:, :, d0:d0 + n_],
                    src[r0:r0 + 1024, c0:c0 + n_].rearrange("(k p) c -> p k c", p=128), [], [b])

        persist_top = top[0]

        def rms_stats(tiles, n_in, jlist):
            for k, (xt, P) in enumerate(tiles):
                junk = jlist[k % len(jlist)]
                A("act", lambda h, xt=xt, P=P, k=k, junk=junk: h.activation(
                    out=junk.ap[0:P, 0:n_in], in_=xt.ap[0:P, :], func=AF.Square,
                    accum_out=ssn.ap[0:P, k:k + 1]), [xt], [junk, ssn.c(k, k + 1)])
            n = len(tiles)
            A("act", lambda h: h.activation(out=lnv.ap[:, 0:n], in_=ssn.ap[:, 0:n], func=AF.Ln,
                                            bias=epsc.ap[:, 0:1], scale=1.0 / n_in), [ssn, epsc], [lnv])
            A("act", lambda h: h.activation(out=rstd.ap[:, 0:n], in_=lnv.ap[:, 0:n], func=AF.Exp, scale=-0.5),
              [lnv], [rstd])

        epsc = sb([128, 1], F32)
        A("dve", lambda h: h.memset(epsc.ap, EPS), [], [epsc])
        halfpi = sb([128, 1], F32)
        A("dve", lambda h: h.memset(halfpi.ap, math.pi / 2.0), [], [halfpi])
        junk_s = [sb([128, 128], BF16) for _ in range(2)]
        persist_top = top[0]

        def norm_pre(xt, P, k, xn, junk, xn_eng="dve"):
            sk, lk, rk = ssn.c(2 * k, 2 * k + 2), lnv.c(2 * k, 2 * k + 2), rstd.c(2 * k, 2 * k + 2)
            A("act", lambda h: h.activation(out=junk.ap[0:P, 0:D], in_=xt.ap[0:P, :], func=AF.Square,
                                            accum_out=sk.ap[0:P, 0:1]), [xt], [junk, sk])
            A("act", lambda h: h.activation(out=lk.ap[0:P, :], in_=sk.ap[0:P, :], func=AF.Ln,
                                            bias=epsc.ap[0:P, 0:1], scale=1.0 / D), [sk, epsc], [lk])
            A("act", lambda h: h.activation(out=rk.ap[0:P, :], in_=lk.ap[0:P, :], func=AF.Exp, scale=-0.5),
              [lk], [rk])
            if xn_eng == "act":
                A("act", lambda h: h.activation(out=xn.ap[0:P, :], in_=xt.ap[0:P, :], func=AF.Copy,
                                                scale=rk.ap[0:P, 0:1]), [xt, rk], [xn])
            else:
                A("dve", lambda h: h.tensor_scalar(out=xn.ap[0:P, :], in0=xt.ap[0:P, :], scalar1=rk.ap[0:P, 0:1],
                                                   scalar2=None, op0=ALU.mult), [xt, rk], [xn])

        def norm_post(xn, P, bank, gofs, dsts, evac):
            for fc in range(8):
                dst = bank.c(fc * 128, fc * 128 + P)
                A("pe", lambda h, fc=fc, dst=dst: h.transpose(
                    out=dst.ap, in_=xn.ap[0:P, fc * 128:(fc + 1) * 128], identity=ident.ap[0:P, 0:P]),
                  [xn, ident], [dst])
            for fc in range(8):
                src = bank.c(fc * 128, fc * 128 + P)
                dstb = dsts[fc]
                g = gcols.c(gofs + fc, gofs + fc + 1)
                if evac[fc % len(evac)] == "act":
                    A("act", lambda h, src=src, dstb=dstb, g=g: h.activation(
                        out=dstb.ap, in_=src.ap, func=AF.Copy, scale=g.ap), [src, g], [dstb])
                else:
                    A("dve", lambda h, src=src, dstb=dstb, g=g: h.tensor_scalar(
                        out=dstb.ap, in0=src.ap, scalar1=g.ap, scalar2=None, op0=ALU.mult), [src, g], [dstb])

        def hT_dsts(hT, W, col0, P):
            return [Buf(sb_arena, hT.lo + (fc * W + col0) * 2, hT.lo + (fc * W + col0 + P) * 2,
                        hT.ap[:, fc, col0:col0 + P], 2) for fc in range(8)]

        def norm_run(tiles, gofs, xn_bufs, tr_banks, dst_list, jlist, evac=("act", "dve"), xn_eng="dve",
                     interleave=True):
            if not interleave:
                for k, (xt, P) in enumerate(tiles):
                    norm_pre(xt, P, k, xn_bufs[k], jlist[k % len(jlist)], xn_eng)
                for k, (xt, P) in enumerate(tiles):
                    norm_post(xn_bufs[k], P, tr_banks[k % len(tr_banks)], gofs, dst_list[k], evac)
                return
            prev = None
            for k, (xt, P) in enumerate(tiles):
                norm_pre(xt, P, k, xn_bufs[k], jlist[k % len(jlist)], xn_eng)
                if prev is not None:
                    norm_post(*prev)
                prev = (xn_bufs[k], P, tr_banks[k % len(tr_banks)], gofs, dst_list[k], (evac[k % len(evac)],))
            norm_post(*prev)

        def rope_tables(pos_row, tb, defer_sin=False):
            dma("sp", tb["pos"].ap, pos_row.broadcast_to([128, CH]), [], [tb["pos"]])
            A("dve", lambda h: h.tensor_scalar(out=tb["ang"].ap, in0=tb["pos"].ap, scalar1=cst.ap[:, 0:1],
                                                scalar2=None, op0=ALU.mult), [tb["pos"], cst], [tb["ang"]])
            A("dve", lambda h: h.tensor_scalar(out=tb["v"].ap, in0=tb["pos"].ap, scalar1=cst.ap[:, 1:2],
                                                scalar2=None, op0=ALU.mult), [tb["pos"], cst], [tb["v"]])
            A("dve", lambda h: h.tensor_copy(out=tb["vi"].ap, in_=tb["v"].ap), [tb["v"]], [tb["vi"]])
            A("dve", lambda h: h.tensor_copy(out=tb["v"].ap, in_=tb["vi"].ap), [tb["vi"]], [tb["v"]])
            A("dve", lambda h: h.scalar_tensor_tensor(out=tb["y"].ap, in0=tb["v"].ap, scalar=-CW1, in1=tb["ang"].ap,
                                                      op0=ALU.mult, op1=ALU.add), [tb["v"], tb["ang"]], [tb["y"]])
            A("dve", lambda h: h.scalar_tensor_tensor(out=tb["ang"].ap, in0=tb["v"].ap, scalar=-CW2, in1=tb["y"].ap,
                                                      op0=ALU.mult, op1=ALU.add), [tb["v"], tb["y"]], [tb["ang"]])
            A("pool", lambda h: h.tensor_scalar(out=tb["y"].ap, in0=tb["ang"].ap, scalar1=PI_LO, scalar2=-PI_LO,
                                                op0=ALU.min, op1=ALU.max), [tb["ang"]], [tb["y"]])
            def sins():
                A("act", lambda h: h.activation(out=tb["sin"].ap, in_=tb["y"].ap, func=AF.Sin, scale=cst.ap[:, 2:3]),
                  [tb["y"], cst], [tb["sin"]])
                A("act", lambda h: h.activation(out=tb["cos"].ap, in_=tb["v"].ap, func=AF.Sin, scale=-1.0,
                                                bias=halfpi.ap[:, 0:1]), [tb["v"], halfpi], [tb["cos"]])
            A("dve", lambda h: h.scalar_tensor_tensor(out=tb["v"].ap, in0=tb["y"].ap, scalar=-1.0, in1=tb["y"].ap,
                                                      op0=ALU.mult, op1=ALU.max),
              [tb["y"]], [tb["v"]])
            if defer_sin:
                return sins
            sins()

        def mm_group(out, pairs, tile_position=None, start=True, stop=True, skip=False):
            n = len(pairs)
            for k, (l, r) in enumerate(pairs):
                kw = {}
                if tile_position is not None:
                    kw["tile_position"] = tile_position
                if skip:
                    kw["skip_group_check"] = True
                A("pe", lambda h, l=l, r=r, k=k, kw=kw: h.matmul(
                    out.ap, lhsT=l.ap, rhs=r.ap, start=(start and k == 0), stop=(stop and k == n - 1), **kw),
                  [l, r], [out])

        def rope_apply(ps_a, ps_b, tb, tmpa, tmpb, dst):
            A("dve", lambda h: h.tensor_tensor(out=tmpa.ap, in0=ps_a.ap, in1=tb["cos"].ap, op=ALU.mult),
              [ps_a, tb["cos"]], [tmpa])
            A("dve", lambda h: h.tensor_tensor(out=tmpb.ap, in0=ps_b.ap, in1=tb["sin"].ap, op=ALU.mult),
              [ps_b, tb["sin"]], [tmpb])
            A("pool", lambda h: h.tensor_tensor(out=dst.ap, in0=tmpa.ap, in1=tmpb.ap, op=ALU.add),
              [tmpa, tmpb], [dst])

        def mk_tables(keys=("pos", "ang", "v", "vi", "y", "cos", "sin")):
            return {k: sb([128, CH], I32 if k == "vi" else F32) for k in keys}

        top[0] = persist_top
        wckv = sb([128, 8, 2 * D], BF16)
        memx = sb([128, 2, D], F32)
        memxn = [sb([128, D], BF16) for _ in range(2)]
        memT = sb([128, 8, MEM], BF16)
        dma("pool", wckv.ap[:, :, 0:1024], w_ckv[:, 0:1024].rearrange("(k p) c -> p k c", p=128), [], [wckv])
        dma("pool", wckv.ap[:, :, 1024:2048], w_ckv[:, 1024:2048].rearrange("(k p) c -> p k c", p=128), [], [wckv])
        dma("sp", memx.ap, memb.rearrange("(t p) d -> p t d", p=128), [], [memx])
        junksA = [sb([128, D], BF16) for _ in range(2)]
        trm = [ps(0, 0, 1024, BF16), ps(7, 0, 1024, BF16)]
        norm_run([(memx.i(0), 128), (memx.i(1), 128)], 24, memxn, trm,
                 [hT_dsts(memT, MEM, 0, 128), hT_dsts(memT, MEM, 128, 128)], junksA)
        for oc in range(8):
            o = ps(1 + oc % 2, 0, MEM)
            mm_group(o, [(Buf(sb_arena, wckv.lo, wckv.hi, wckv.ap[:, kc, oc * 128:(oc + 1) * 128]), memT.i(kc))
                         for kc in range(8)])
            dst = KcT.i(oc)
            A("act", lambda h, o=o, dst=dst: h.activation(out=dst.ap, in_=o.ap, func=AF.Copy), [o], [dst])
        for mt in range(2):
            for hf in range(2):
                o = ps(3 + (mt * 2 + hf) % 2)
                mm_group(o, [(Buf(sb_arena, memT.lo, memT.hi, memT.ap[:, kc, mt * 128:(mt + 1) * 128]),
                              Buf(sb_arena, wckv.lo, wckv.hi, wckv.ap[:, kc, 1024 + hf * 512:1024 + (hf + 1) * 512]))
                             for kc in range(8)])
                dst = Buf(sb_arena, Vc.lo, Vc.hi, Vc.ap[:, mt, hf * 512:(hf + 1) * 512])
                A("dve", lambda h, o=o, dst=dst: h.tensor_copy(out=dst.ap, in_=o.ap), [o], [dst])

        top[0] = persist_top
        wkv = sb([128, 8, 1536], BF16)
        xa = [sb([128, 4, D], F32) for _ in range(2)]
        xnA = [sb([128, D], BF16) for _ in range(4)]
        hTA = [sb([128, 8, CH], BF16) for _ in range(2)]
        kst = [[sb([128, CH], BF16) for _ in range(4)] for _ in range(2)]
        vst = [sb([128, 4, 4, 129], BF16) for _ in range(2)]
        tmpA = [sb([128, CH], F32) for _ in range(2)]
        tmpB = [sb([128, CH], F32) for _ in range(2)]
        tabs = [mk_tables() for _ in range(2)]
        dma("pool", wkv.ap[:, :, 0:512], w_in[:, 1024:1536].rearrange("(k p) c -> p k c", p=128), [], [wkv])
        dma("pool", wkv.ap[:, :, 512:1024], w_ksw.rearrange("(k p) c -> p k c", p=128), [], [wkv])
        dma("pool", wkv.ap[:, :, 1024:1536], w_in[:, 1536:2048].rearrange("(k p) c -> p k c", p=128), [], [wkv])
        for v in vst:
            A("pool", lambda h, v=v: h.memset(v.ap, 1.0), [], [v])
        junksA = [sb([128, D], BF16) for _ in range(4)]
        trA = [ps(0, 0, 1024, BF16), ps(1, 0, 1024, BF16)]

        def wsl(wb, kc, c0, c1):
            return Buf(sb_arena, wb.lo, wb.hi, wb.ap[:, kc, c0:c1])

        def load_xa(c):
            dma("sp", xa[c % 2].ap, xb[c * CH:(c + 1) * CH, :].rearrange("(t p) d -> p t d", p=128), [], [xa[c % 2]])

        def frontA(c):
            sl = c % 2
            sins = rope_tables(posA[c:c + 1, :], tabs[sl], defer_sin=True)
            norm_run([(xa[sl].i(t), 128) for t in range(4)], 0, xnA, trA,
                     [hT_dsts(hTA[sl], CH, t * 128, 128) for t in range(4)], junksA, evac=("act",), xn_eng="act",
                     interleave=False)
            sins()

        def backA(c):
            sl = c % 2
            tb = tabs[sl]
            hT = hTA[sl]
            for hh in range(4):
                pk = ps(2 + 2 * (hh % 2))
                pks = ps(3 + 2 * (hh % 2))
                mm_group(pk, [(wsl(wkv, kc, hh * 128, (hh + 1) * 128), hT.i(kc)) for kc in range(8)])
                mm_group(pks, [(wsl(wkv, kc, 512 + hh * 128, 512 + (hh + 1) * 128), hT.i(kc)) for kc in range(8)])
                rope_apply(pk, pks, tb, tmpA[hh % 2], tmpB[hh % 2], kst[sl][hh])
                dma("sp", KTs[hh, :, c * CH:(c + 1) * CH], kst[sl][hh].ap, [kst[sl][hh]],
                    [dr("KT%d" % hh, c * CH, (c + 1) * CH)])
            for t in range(4):
                pv = ps(6 + t % 2)
                mm_group(pv, [(Buf(sb_arena, hT.lo, hT.hi, hT.ap[:, kc, t * 128:(t + 1) * 128]),
                               wsl(wkv, kc, 1024, 1536)) for kc in range(8)])
                dst = Buf(sb_arena, vst[sl].lo, vst[sl].hi, vst[sl].ap[:, t, :, 0:128])
                A("dve", lambda h, pv=pv, dst=dst: h.tensor_copy(
                    out=dst.ap, in_=pv.ap.rearrange("p (a b) -> p a b", a=4)), [pv], [dst])
            for hh in range(4):
                dma("sp", Vs[hh, :, 4 * c:4 * c + 4, :], vst[sl].ap[:, :, hh, :], [vst[sl]],
                    [dr("V%d" % hh, 4 * c, 4 * c + 4)])

        STOP = int(os.environ.get("K_STOP", "0"))
        if STOP == 1:
            pass
        elif os.environ.get("K_NOPIPE_A"):
            load_xa(0)
            for c in range(NCH):
                if c + 1 < NCH:
                    load_xa(c + 1)
                frontA(c)
                backA(c)
        else:
            load_xa(0)
            if NCH > 1:
                load_xa(1)
            frontA(0)
            for c in range(NCH):
                if c + 1 < NCH:
                    frontA(c + 1)
                if c + 2 < NCH:
                    load_xa(c + 2)
                backA(c)

        top[0] = persist_top
        xB = [sb([128, 4, D], F32) for _ in range(2)]
        xH = [sb([HALO, D], F32) for _ in range(2)]
        xnB = [sb([128, D], BF16) for _ in range(5)]
        W2 = HALO + CH
        hTB = sb([128, 8, CH], BF16)
        hTh = sb([128, 8, HALO], BF16)
        gT = sb([128, 8, CH], BF16)
        junkP = [sb([128, D], BF16) for _ in range(2)]
        NSL = 2
        slabs = [sb([128, 8, 1024], BF16) for _ in range(NSL)]
        stage_top = top[0]
        uT = sb([128, 4, W2], F32)
        zT = sb([128, 4, CH], BF16)
        qT = sb([128, 4, CH], BF16)
        tabB = mk_tables(("pos", "cos", "sin"))
        pa = sb([128, W2], F32)
        pb = sb([128, W2], F32)
        invc = sb([128, CH], F32)
        accS = sb([128, 8, 129], F32)
        rl = sb([128, 8], F32)
        rl2n = sb([128, 4], F32)
        ot2 = [sb([128, 128], F32) for _ in range(2)]
        oo = [sb([128, 128], F32) for _ in range(4)]
        yts = [sb([128, CH], BF16) for _ in range(2)]
        r1_top = top[0]
        NKV = 2
        ktb = [sb([128, 1024], BF16) for _ in range(NKV)]
        vtb = [sb([128, 8, 129], BF16) for _ in range(NKV)]
        NPT = 3
        ptb = [[sb([128, CH], BF16) for _ in range(2)] for _ in range(NPT)]
        r1_end = top[0]
        top[0] = r1_top
        tmpA2 = [sb([128, CH], F32)]
        tmpB2 = [sb([128, CH], F32)]
        tabB.update(mk_tables(("ang", "v", "vi", "y")))
        assert top[0] <= r1_end, (top[0], r1_end)
        top[0] = r1_end
        mix_top = top[0]
        top[0] = stage_top
        qcT = sb([128, 8, CH], BF16)
        pcT = [sb([128, 2, CH], BF16) for _ in range(2)]
        lnl = sb([128, CH], F32)
        rlc = [sb([128, CH], F32) for _ in range(2)]
        ocT = sb([128, 8, CH], BF16)
        cross_top = top[0]
        top[0] = stage_top
        aT = sb([128, 32, CH], BF16)
        rtmp = [sb([128, CH], BF16) for _ in range(3)]
        mlp_top = top[0]
        top[0] = max(mix_top, cross_top, mlp_top)
        print("SBUF bytes used (phase B):", top[0])

        slab_emitted = [0]
        slab_seq = [(i, s) for i in range(M) for s in range(NSLAB)]

        def slab_get(n):
            while slab_emitted[0] <= min(n + NSL - 1, len(slab_seq) - 1):
                k = slab_emitted[0]
                s = slab_seq[k][1]
                ncol = slab_defs[s][1]
                dstb = slabs[k % NSL]
                dma("sp", dstb.ap[:, :, 0:ncol], wbf[s], [wbf_bufs[s]], [dstb])
                slab_emitted[0] += 1
            return slabs[n % NSL]

        kv_seq = []
        for i in range(M):
            for hh in range(4):
                for kb in range(2 * (i + 1)):
                    kv_seq.append((i, hh, kb))
        kv_emitted = [0]

        def kv_fetch(k):
            if k >= len(kv_seq) or k < kv_emitted[0]:
                return
            assert k == kv_emitted[0]
            i, hh, kb = kv_seq[k]
            dma("sp", ktb[k % NKV].ap, KTs[hh, :, kb * 1024:(kb + 1) * 1024],
                [dr("KT%d" % hh, kb * 1024, (kb + 1) * 1024)], [ktb[k % NKV]])
            dma("sp", vtb[k % NKV].ap, Vs[hh, :, kb * 8:(kb + 1) * 8, :],
                [dr("V%d" % hh, kb * 8, (kb + 1) * 8)], [vtb[k % NKV]])
            kv_emitted[0] += 1

        def load_xB(i):
            dma("sp", xB[i % 2].ap, xo[i, HALO:HALO + CH, :].rearrange("(t p) d -> p t d", p=128), [], [xB[i % 2]])
            dma("sp", xH[i % 2].ap, xo[i, 0:HALO, :], [], [xH[i % 2]])

        def resid_proj(xcur, srcT, wslab, banks_ring):
            n = 0
            for t in range(4):
                for hf in range(2):
                    o = ps(banks_ring[n % len(banks_ring)])
                    n += 1
                    mm_group(o, [(Buf(sb_arena, srcT.lo, srcT.hi, srcT.ap[:, kc, t * 128:(t + 1) * 128]),
                                  wsl(wslab, kc, hf * 512, (hf + 1) * 512)) for kc in range(8)])
                    xs = Buf(sb_arena, xcur.lo + (t * D + hf * 512) * 4, xcur.lo + (t * D + (hf + 1) * 512) * 4,
                             xcur.ap[:, t, hf * 512:(hf + 1) * 512], 4)
                    A("dve", lambda h, o=o, xs=xs: h.tensor_tensor(out=xs.ap, in0=o.ap, in1=xs.ap, op=ALU.add),
                      [o, xs], [xs])

        trB = [ps(7, 0, 1024, BF16), ps(0, 0, 1024, BF16)]
        GB = [3, 4, 5, 6]

        def resid_norm(xcur, srcT, wslab, banks_ring, gofs):
            prev = None
            n = 0
            for t in range(4):
                for hf in range(2):
                    o = ps(banks_ring[n % len(banks_ring)])
                    n += 1
                    mm_group(o, [(Buf(sb_arena, srcT.lo, srcT.hi, srcT.ap[:, kc, t * 128:(t + 1) * 128]),
                                  wsl(wslab, kc, hf * 512, (hf + 1) * 512)) for kc in range(8)])
                    xs = Buf(sb_arena, xcur.lo + (t * D + hf * 512) * 4, xcur.lo + (t * D + (hf + 1) * 512) * 4,
                             xcur.ap[:, t, hf * 512:(hf + 1) * 512], 4)
                    A("dve", lambda h, o=o, xs=xs: h.tensor_tensor(out=xs.ap, in0=o.ap, in1=xs.ap, op=ALU.add),
                      [o, xs], [xs])
                norm_pre(xcur.i(t), 128, t, xnB[t], junkP[t % 2])
                if prev is not None:
                    norm_post(*prev)
                prev = (xnB[t], 128, trB[t % 2], gofs, hT_dsts(hTB, CH, t * 128, 128), (("act", "dve")[t % 2],))
            norm_post(*prev)
        slab_n = 0
        kv_n = 0
        if STOP == 0:
            load_xB(0)
        STOPB = int(os.environ.get("K_STOPB", "0"))
        for i in range(M if STOP == 0 else 0):
            xs_ = xB[i % 2]
            xh_ = xH[i % 2]
            if i + 1 < M:
                load_xB(i + 1)
            rope_tables(posB[i:i + 1, :], tabB)
            norm_run([(xs_.i(t), 128) for t in range(4)] + [(xh_, HALO)], 0, xnB, trB,
                     [hT_dsts(hTB, CH, t * 128, 128) for t in range(4)] + [[hTh.i(fc) for fc in range(8)]], junkP)
            if STOPB == 3:
                continue
            gb = 0
            w_qq = slab_get(slab_n); slab_n += 1
            for hh in range(4):
                pq = ps(GB[gb % 4]); gb += 1
                pqs = ps(GB[gb % 4]); gb += 1
                mm_group(pq, [(wsl(w_qq, kc, hh * 128, (hh + 1) * 128), hTB.i(kc)) for kc in range(8)])
                mm_group(pqs, [(wsl(w_qq, kc, 512 + hh * 128, 512 + (hh + 1) * 128), hTB.i(kc)) for kc in range(8)])
                rope_apply(pq, pqs, tabB, tmpA2[0], tmpB2[0], qT.i(hh))
            w_ug = slab_get(slab_n); slab_n += 1
            for g in range(4):
                o = ps(GB[gb % 4]); gb += 1
                mm_group(o, [(wsl(w_ug, kc, g * 128, (g + 1) * 128), hTB.i(kc)) for kc in range(8)])
                oh = ps(GB[gb % 4], 0, HALO); gb += 1
                mm_group(oh, [(wsl(w_ug, kc, g * 128, (g + 1) * 128), hTh.i(kc)) for kc in range(8)])
                um = Buf(sb_arena, uT.lo + (g * W2 + HALO) * 4, uT.lo + (g + 1) * W2 * 4, uT.ap[:, g, HALO:W2], 4)
                uh = Buf(sb_arena, uT.lo + g * W2 * 4, uT.lo + (g * W2 + HALO) * 4, uT.ap[:, g, 0:HALO], 4)
                A("act", lambda h, o=o, um=um: h.activation(out=um.ap, in_=o.ap, func=AF.Copy), [o], [um])
                A("dve", lambda h, oh=oh, uh=uh: h.tensor_copy(out=uh.ap, in_=oh.ap), [oh], [uh])
            for gc in range(8):
                if gc == 4:
                    w_g2 = slab_get(slab_n); slab_n += 1
                wg_, c0_ = (w_ug, 512 + gc * 128) if gc < 4 else (w_g2, (gc - 4) * 128)
                o = ps(GB[gb % 4]); gb += 1
                mm_group(o, [(wsl(wg_, kc, c0_, c0_ + 128), hTB.i(kc)) for kc in range(8)])
                dst = gT.i(gc)
                A("act", lambda h, o=o, dst=dst: h.activation(out=dst.ap, in_=o.ap, func=AF.Sigmoid), [o], [dst])
            for g, wdw in enumerate((2, 4, 8, 16)):
                U = uT.i(g)
                cur = U
                step = 1
                tmp_cycle = [pa, pb]
                ti = 0
                while step * 2 < wdw:
                    nxt = tmp_cycle[ti % 2]; ti += 1
                    A("pool", lambda h, cur=cur, nxt=nxt, step=step: h.tensor_tensor(
                        out=nxt.ap[:, 2 * step - 1:W2], in0=cur.ap[:, 2 * step - 1:W2],
                        in1=cur.ap[:, step - 1:W2 - step], op=ALU.add), [cur], [nxt])
                    cur = nxt
                    step *= 2
                nxt = tmp_cycle[ti % 2]; ti += 1
                A("pool", lambda h, cur=cur, nxt=nxt, step=step: h.tensor_tensor(
                    out=nxt.ap[:, HALO:W2], in0=cur.ap[:, HALO:W2], in1=cur.ap[:, HALO - step:W2 - step], op=ALU.add),
                  [cur], [nxt])
                A("dve", lambda h, wdw=wdw: h.tensor_scalar(out=invc.ap, in0=tabB["pos"].ap, scalar1=1.0,
                                                            scalar2=float(wdw), op0=ALU.add, op1=ALU.min),
                  [tabB["pos"]], [invc])
                A("act", lambda h: h.activation(out=invc.ap, in_=invc.ap, func=AF.Ln), [invc], [invc])
                A("act", lambda h: h.activation(out=invc.ap, in_=invc.ap, func=AF.Exp, scale=-1.0), [invc], [invc])
                A("dve", lambda h, nxt=nxt: h.tensor_tensor(out=nxt.ap[:, HALO:W2], in0=nxt.ap[:, HALO:W2],
                                                            in1=invc.ap, op=ALU.mult), [nxt, invc], [nxt])
                zg = zT.i(g)
                A("dve", lambda h, nxt=nxt, U=U, zg=zg: h.tensor_tensor(out=zg.ap, in0=nxt.ap[:, HALO:W2],
                                                                    in1=U.ap[:, HALO:W2], op=ALU.subtract),
                  [nxt, U], [zg])
            pending_pool = [0, 1, 2, 3]
            defer_pool = not os.environ.get("K_NODEFER")
            if STOPB == 4:
                continue

            def pool_out(g):
                o = ps(7)
                zg = zT.i(g)
                mm_group(o, [(Buf(sb_arena, wpool.lo, wpool.hi, wpool.ap[:, g, :]), zg)])
                dst = gT.i(g)
                A("dve", lambda h, o=o, dst=dst, g=g: h.scalar_tensor_tensor(
                    out=dst.ap, in0=o.ap, scalar=pscale.ap[:, g:g + 1], in1=dst.ap, op0=ALU.mult, op1=ALU.mult),
                  [o, pscale, dst], [dst])

            if not defer_pool:
                while pending_pool:
                    pool_out(pending_pool.pop(0))
            n_kt = 16 * (i + 1)
            pending_fin = []
            trF = ps(7, 0, 512, BF16)

            def fin_part2(hp):
                for qt in range(4):
                    yq = yts[hp % 2].c(qt * 128, (qt + 1) * 128)
                    dst = trF.c(qt * 128, (qt + 1) * 128)
                    A("pe", lambda h, yq=yq, dst=dst: h.transpose(out=dst.ap, in_=yq.ap, identity=ident.ap),
                      [yq, ident], [dst])
                dst = gT.i(4 + hp)
                A("dve", lambda h, dst=dst: h.tensor_tensor(out=dst.ap, in0=trF.ap, in1=dst.ap, op=ALU.mult),
                  [trF, dst], [dst])

            for hh in range(4):
                accs = []
                for a in range(8):
                    bk, off = a // 3, (a % 3) * 129
                    accs.append(ps(bk, off, off + 129))
                qh = qT.i(hh)
                q_lo = Buf(sb_arena, qh.lo, qh.hi, qh.ap[0:64, :])
                q_hi = Buf(sb_arena, qh.lo, qh.hi, qh.ap[64:128, :])
                sb_ = [[ps(3, 0, 512), ps(4, 0, 512)], [ps(5, 0, 512), ps(6, 0, 512)]]
                blocks = {}

                kv_base = kv_n
                kv_n += 2 * (i + 1)

                def qk(kt):
                    kb, k8 = kt // 8, kt % 8
                    n_ = kv_base + kb
                    kv_fetch(n_)
                    blocks[kb] = (ktb[n_ % NKV], vtb[n_ % NKV])
                    kt_b, _ = blocks[kb]
                    for cpt in range(2):
                        l = Buf(sb_arena, kt_b.lo, kt_b.hi, kt_b.ap[64 * cpt:64 * (cpt + 1), k8 * 128:(k8 + 1) * 128])
                        r = q_lo if cpt == 0 else q_hi
                        mm_group(sb_[kt % 2][cpt], [(l, r)], tile_position=(64 * cpt, 0))

                qk(0)
                for kt in range(n_kt):
                    if kt + 1 < n_kt:
                        qk(kt + 1)
                    if kt == 6 and pending_fin:
                        fin_part2(pending_fin.pop(0))
                    if kt in (8, 9, 10, 11) and pending_pool:
                        pool_out(pending_pool.pop(0))
                    kb, k8 = kt // 8, kt % 8
                    if k8 == 0:
                        nxt_ = kv_base + kb + 1
                        if nxt_ < len(kv_seq) and kv_seq[nxt_][0] == i:
                            kv_fetch(nxt_)
                    _, v_b = blocks[kb]
                    pts = ptb[kt % NPT]
                    for cpt in range(2):
                        s_ = sb_[kt % 2][cpt]
                        p_ = pts[cpt]
                        A("act", lambda h, s_=s_, p_=p_: h.activation(out=p_.ap, in_=s_.ap, func=AF.Exp, scale=0.125),
                          [s_], [p_])
                        if kt >= n_kt - 16:
                            rr = kt - (n_kt - 16)
                            A("dve", lambda h, p_=p_, rr=rr: h.scalar_tensor_tensor(
                                out=p_.ap, in0=qkb.ap, scalar=thr.ap[:, rr:rr + 1], in1=p_.ap,
                                op0=ALU.is_ge, op1=ALU.mult), [qkb, thr, p_], [p_])
                    vk = Buf(sb_arena, v_b.lo, v_b.hi, v_b.ap[:, k8, :])
                    for cpt in range(2):
                        for qt in range(4):
                            a = cpt * 4 + qt
                            l = Buf(sb_arena, pts[cpt].lo, pts[cpt].hi, pts[cpt].ap[:, qt * 128:(qt + 1) * 128])
                            A("pe", lambda h, a=a, l=l, vk=vk, kt=kt: h.matmul(
                                accs[a].ap, lhsT=l.ap, rhs=vk.ap, start=(kt == 0 and a % 3 == 0),
                                stop=(kt == n_kt - 1), skip_group_check=True), [l, vk], [accs[a]])
                for bk in range(3):
                    na = 3 if bk < 2 else 2
                    src = ps(bk, 0, na * 129)
                    dst = Buf(sb_arena, accS.lo + bk * 3 * 129 * 4, accS.lo + (bk * 3 + na) * 129 * 4,
                              accS.ap[:, bk * 3:bk * 3 + na, :], 4)
                    eng = "act" if bk == 1 else "dve"
                    if eng == "act":
                        A("act", lambda h, src=src, dst=dst, na=na: h.activation(
                            out=dst.ap, in_=src.ap.rearrange("p (a b) -> p a b", a=na), func=AF.Copy), [src], [dst])
                    else:
                        A("dve", lambda h, src=src, dst=dst, na=na: h.tensor_copy(
                            out=dst.ap, in_=src.ap.rearrange("p (a b) -> p a b", a=na)), [src], [dst])
                if debug and i == 0:
                    dma("sp", dbg["acc"][hh], accS.ap, [accS], [])
                    if hh == 0:
                        dma("sp", dbg["q"], qT.ap, [qT], [])
                A("dve", lambda h: h.reciprocal(out=rl.ap, in_=accS.ap[:, :, 128]), [accS], [rl])
                A("dve", lambda h: h.tensor_scalar(out=rl2n.ap, in0=rl.ap[:, 4:8], scalar1=neglam.ap[:, 0:1],
                                                   scalar2=None, op0=ALU.mult), [rl, neglam], [rl2n])
                for qt in range(4):
                    t2 = ot2[qt % 2]
                    o_ = oo[qt]
                    A("dve", lambda h, qt=qt, t2=t2: h.tensor_scalar(
                        out=t2.ap, in0=accS.ap[:, 4 + qt, 0:128], scalar1=rl2n.ap[:, qt:qt + 1], scalar2=None,
                        op0=ALU.mult), [accS, rl2n], [t2])
                    A("dve", lambda h, qt=qt, t2=t2, o_=o_: h.scalar_tensor_tensor(
                        out=o_.ap, in0=accS.ap[:, qt, 0:128], scalar=rl.ap[:, qt:qt + 1], in1=t2.ap,
                        op0=ALU.mult, op1=ALU.add), [accS, rl, t2], [o_])
                rms_stats([(oo[qt], 128) for qt in range(4)], 128, junk_s)
                for qt in range(4):
                    o_ = oo[qt]
                    yq = yts[hh % 2].c(qt * 128, (qt + 1) * 128)
                    A("dve", lambda h, qt=qt, o_=o_, yq=yq: h.scalar_tensor_tensor(
                        out=yq.ap, in0=o_.ap, scalar=rstd.ap[:, qt:qt + 1], in1=gsub.ap, op0=ALU.mult, op1=ALU.mult),
                      [o_, rstd.c(qt, qt + 1), gsub], [yq])

                if debug and i == 0:
                    dma("sp", dbg["yt"][hh], yts[hh % 2].ap, [yts[hh % 2]], [])
                pending_fin.append(hh)
            while pending_pool:
                pool_out(pending_pool.pop(0))
            while pending_fin:
                fin_part2(pending_fin.pop(0))
            if STOPB == 5:
                continue
            if debug:
                dma("sp", dbg["mixT"][i], gT.ap, [gT], [])
            w_o = slab_get(slab_n); slab_n += 1
            resid_norm(xs_, gT, w_o, GB, 8)
            if STOPB == 6:
                continue
            if debug:
                dma("sp", dbg["x1"][i * CH:(i + 1) * CH, :].rearrange("(t p) d -> p t d", p=128), xs_.ap, [xs_], [])
            w_q = slab_get(slab_n); slab_n += 1
            gb = 0
            for oc in range(8):
                o = ps(GB[gb % 4]); gb += 1
                mm_group(o, [(wsl(w_q, kc, oc * 128, (oc + 1) * 128), hTB.i(kc)) for kc in range(8)])
                dst = qcT.i(oc)
                if oc % 2 == 0:
                    A("act", lambda h, o=o, dst=dst: h.activation(out=dst.ap, in_=o.ap, func=AF.Copy), [o], [dst])
                else:
                    A("dve", lambda h, o=o, dst=dst: h.tensor_copy(out=dst.ap, in_=o.ap), [o], [dst])
            for hh in range(4):
                pc = pcT[hh % 2]
                for mt in range(2):
                    o = ps(GB[gb % 4]); gb += 1
                    mm_group(o, [(Buf(sb_arena, KcT.lo, KcT.hi, KcT.ap[:, 2 * hh + dc, mt * 128:(mt + 1) * 128]),
                                  qcT.i(2 * hh + dc)) for dc in range(2)])
                    dst = pc.i(mt)
                    A("act", lambda h, o=o, dst=dst: h.activation(out=dst.ap, in_=o.ap, func=AF.Exp, scale=1.0 / 16.0),
                      [o], [dst])
                ol = ps(GB[gb % 4]); gb += 1
                mm_group(ol, [(ones, pc.i(mt)) for mt in range(2)])
                rc = rlc[hh % 2]
                A("act", lambda h, ol=ol: h.activation(out=lnl.ap, in_=ol.ap, func=AF.Ln), [ol], [lnl])
                A("act", lambda h, rc=rc: h.activation(out=rc.ap, in_=lnl.ap, func=AF.Exp, scale=-1.0), [lnl], [rc])
                for dc in range(2):
                    o = ps(GB[gb % 4]); gb += 1
                    mm_group(o, [(Buf(sb_arena, Vc.lo, Vc.hi, Vc.ap[:, mt, (2 * hh + dc) * 128:(2 * hh + dc + 1) * 128]),
                                  pc.i(mt)) for mt in range(2)])
                    dst = ocT.i(2 * hh + dc)
                    A("dve", lambda h, o=o, dst=dst, rc=rc: h.tensor_tensor(out=dst.ap, in0=o.ap, in1=rc.ap, op=ALU.mult),
                      [o, rc], [dst])
            w_c = slab_get(slab_n); slab_n += 1
            resid_norm(xs_, ocT, w_c, GB, 16)
            if STOPB == 7:
                continue
            if debug:
                dma("sp", dbg["x2"][i * CH:(i + 1) * CH, :].rearrange("(t p) d -> p t d", p=128), xs_.ap, [xs_], [])
            gb = 0
            for su in range(4):
                w_u = slab_get(slab_n); slab_n += 1
                for o8 in range(8):
                    oc = su * 8 + o8
                    o = ps(GB[gb % 4]); gb += 1
                    mm_group(o, [(wsl(w_u, kc, o8 * 128, (o8 + 1) * 128), hTB.i(kc)) for kc in range(8)])
                    rt = rtmp[oc % 3]
                    A("act", lambda h, o=o, rt=rt: h.activation(out=rt.ap, in_=o.ap, func=AF.Relu), [o], [rt])
                    dst = aT.i(oc)
                    A("dve", lambda h, o=o, rt=rt, dst=dst: h.tensor_tensor(out=dst.ap, in0=o.ap, in1=rt.ap, op=ALU.mult),
                      [o, rt], [dst])
            if STOPB == 8:
                continue
            dacc = [ps(b_) for b_ in range(8)]
            for sd in range(4):
                w_d = slab_get(slab_n); slab_n += 1
                for t in range(4):
                    for hf in range(2):
                        o = dacc[t * 2 + hf]
                        for k8 in range(8):
                            kc = sd * 8 + k8
                            l = Buf(sb_arena, aT.lo, aT.hi, aT.ap[:, kc, t * 128:(t + 1) * 128])
                            r = wsl(w_d, k8, hf * 512, (hf + 1) * 512)
                            A("pe", lambda h, o=o, l=l, r=r, kc=kc: h.matmul(
                                o.ap, lhsT=l.ap, rhs=r.ap, start=(kc == 0), stop=(kc == 31)), [l, r], [o])
            for t in range(4):
                for hf in range(2):
                    o = dacc[t * 2 + hf]
                    xsl = Buf(sb_arena, xs_.lo + (t * D + hf * 512) * 4, xs_.lo + (t * D + (hf + 1) * 512) * 4,
                              xs_.ap[:, t, hf * 512:(hf + 1) * 512], 4)
                    A("dve", lambda h, o=o, xsl=xsl: h.tensor_tensor(out=xsl.ap, in0=o.ap, in1=xsl.ap, op=ALU.add),
                      [o, xsl], [xsl])
            if STOPB == 9:
                continue
            rms_stats([(xs_.i(t), 128) for t in range(4)], D, junkP)
            if STOPB == 10:
                continue
            for t in range(4):
                xt = xs_.i(t)
                A("dve", lambda h, xt=xt, t=t: h.scalar_tensor_tensor(
                    out=xt.ap, in0=xt.ap, scalar=rstd.ap[:, t:t + 1], in1=gfin.ap, op0=ALU.mult, op1=ALU.mult),
                  [xt, rstd.c(t, t + 1), gfin], [xt])
            if STOPB == 11:
                continue
            dma("sp", out_d[i * CH:(i + 1) * CH, :].rearrange("(t p) d -> p t d", p=128), xs_.ap, [xs_],
                [dr("out", i, i + 1)])

        with nc.Block() as block:
            sched.emit(nc, eng_sems, dma_sems, block)
    return nc


def make_in_maps(S, x, mem, g_mix, w_in, w_pool, pool_scale, lambda_q1, lambda_k1, lambda_q2, lambda_k2,
                 g_subln, w_out, g_cross, g_mem, w_cq, w_ckv, w_co, g_mlp, w_up, w_down, g_final):
    f = lambda a: np.ascontiguousarray(np.asarray(a, dtype=np.float32))
    NCH = S // CH
    M = NCH // CPB
    w_in0 = f(w_in[0])
    perm = np.arange(512).reshape(8, 2, 32)[:, ::-1, :].reshape(512)
    w_qsw = f(w_in0[:, 512:1024][:, perm])
    w_ksw = f(w_in0[:, 1024:1536][:, perm])
    col = lambda g: np.asarray(g, np.float32).reshape(8, 128).T
    gcols = f(np.concatenate([col(g_mix[0]), col(g_cross[0]), col(g_mlp[0]), col(g_mem[0])], axis=1))
    pscale = f(np.asarray(pool_scale[0], np.float32).reshape(4, 128).T)
    lam = f(np.concatenate([lambda_q1[0], lambda_k1[0], lambda_q2[0], lambda_k2[0]]).reshape(1, 256))
    p = np.arange(128)
    inv_freq = (10000.0 ** (-(np.arange(0, 64, 2, dtype=np.float32)) / np.float32(64))).astype(np.float32)
    invf = inv_freq[p % 32]
    sgn = np.where((p % 64) < 32, -1.0, 1.0).astype(np.float32)
    ident = np.eye(128, dtype=np.float32)
    qkbase = f(np.arange(CH)[None, :] - np.arange(128)[:, None])
    posA = f(np.arange(S).reshape(NCH, CH))
    shared = {
        "posA": posA, "qkbase": qkbase, "ident": ident, "w_in": w_in0, "w_qsw": w_qsw, "w_ksw": w_ksw,
        "w_pool": f(w_pool[0]), "w_out": f(w_out[0]), "w_cq": f(w_cq[0]), "w_ckv": f(w_ckv[0]),
        "w_co": f(w_co[0]), "w_up": f(w_up[0]), "w_down": f(w_down[0]), "gcols": gcols, "pscale": pscale,
        "gfin": f(np.asarray(g_final).reshape(1, D)), "gsub": f(np.asarray(g_subln[0]).reshape(1, 128)), "lam": lam,
    }
    x = np.asarray(x, np.float32)
    mem = np.asarray(mem, np.float32)
    in_maps = []
    for c in range(NCORE):
        b, j = c // CPB, c % CPB
        xo = np.zeros((M, HALO + CH, D), np.float32)
        posB = np.zeros((M, CH), np.float32)
        for m in range(M):
            t0 = (CPB * m + j) * CH
            lo = max(t0 - HALO, 0)
            xo[m, HALO - (t0 - lo):] = x[b, lo:t0 + CH]
            posB[m] = np.arange(t0, t0 + CH)
        consts = np.zeros((128, 4), np.float32)
        consts[:, 0] = invf
        consts[:, 1] = invf / np.float32(TWO_PI)
        consts[:, 2] = sgn
        thr = np.zeros((128, 16), np.float32)
        thr[:, :] = (128.0 * np.arange(16) - 512.0 * j)[None, :]
        d = dict(shared)
        d.update({"xb": f(x[b]), "xo": xo, "posB": posB, "consts": consts, "thr": thr, "memb": f(mem[b])})
        in_maps.append(d)
    return in_maps


_NC_CACHE = {}


def run(S, inputs, debug=False):
    key = (S, debug)
    if key not in _NC_CACHE:
        _NC_CACHE[key] = build(S, debug)
    nc = _NC_CACHE[key]
    in_maps = make_in_maps(S, **inputs)
    res = run_bass_kernel_spmd(nc, in_maps, core_ids=list(range(NCORE)))
    return res


def assemble(S, results, key="out"):
    NCH = S // CH
    M = NCH // CPB
    out = np.zeros((2, S, D), np.float32)
    for c in range(NCORE):
        b, j = c // CPB, c % CPB
        o = np.asarray(results[c][key]).reshape(M, CH, D)
        for m in range(M):
            t0 = (CPB * m + j) * CH
            out[b, t0:t0 + CH] = o[m]
    return out


def kernel(**inputs):
    S = int(np.asarray(inputs["x"]).shape[1])
    res = run(S, inputs)
    return assemble(S, res.results)
```

```python
import bisect
import math
import os
import numpy as np
import concourse.bass as bass
import concourse.mybir as mybir
from concourse.bass_utils import run_bass_kernel_spmd

F32 = mybir.dt.float32
BF16 = mybir.dt.bfloat16
I32 = mybir.dt.int32
U8 = mybir.dt.uint8
ALU = mybir.AluOpType
AF = mybir.ActivationFunctionType
AX = mybir.AxisListType

D = 1024
CH = 512
NCORE = 8
CPB = 4
HALO = 16
MEM = 256
DFF = 4096
EPS = 1e-6
LAM_INIT = 0.8 - 0.6 * math.exp(0.0)
TWO_PI = 2.0 * math.pi
CW1 = 6.28125
CW2 = TWO_PI - CW1
PI_LO = 3.1415925
NDMA = 48
NSWDMA = 8
ESZ = {F32: 4, BF16: 2, I32: 4, U8: 1}


class Arena:
    def __init__(self, size):
        self.b = [0, size]
        self.w = [None]
        self.r = [{}]

    def _split(self, x):
        i = bisect.bisect_right(self.b, x) - 1
        if self.b[i] == x:
            return i
        self.b.insert(i + 1, x)
        self.w.insert(i + 1, self.w[i])
        self.r.insert(i + 1, dict(self.r[i]))
        return i + 1

    def rng(self, lo, hi):
        i = self._split(lo)
        j = self._split(hi)
        return range(i, j)


class Buf:
    def __init__(self, arena, lo, hi, ap, esz=1):
        self.arena, self.lo, self.hi, self.ap, self.esz = arena, lo, hi, ap, esz

    def c(self, c0, c1):
        return Buf(self.arena, self.lo + c0 * self.esz, self.lo + c1 * self.esz,
                   self.ap[:, c0:c1], self.esz)

    def i(self, k):
        n = self.ap.shape[2]
        return Buf(self.arena, self.lo + k * n * self.esz, self.lo + (k + 1) * n * self.esz,
                   self.ap[:, k, :], self.esz)


class Op:
    __slots__ = ("eng", "fn", "deps", "idx", "inc", "val", "is_dma", "sem_id", "key")


class Sched:
    ENGS = ("pe", "act", "dve", "pool", "sp")

    def __init__(self):
        self.ops = {e: [] for e in self.ENGS}
        self.dma_rr = 0
        self.sw_rr = 0
        self.dma_last = [None] * NDMA
        self.dma_cnt = [0] * NDMA
        self.nuid = 0

    def add(self, eng, fn, reads=(), writes=(), dma=False):
        op = Op()
        op.eng, op.fn, op.is_dma, op.inc, op.val, op.sem_id = eng, fn, dma, False, 0, -1
        op.idx = len(self.ops[eng])
        if dma:
            self.nuid += 1
            op.key = ("d", self.nuid)
        else:
            op.key = eng
        deps = {}

        def dep(o, raw):
            if o is None or o is op:
                return
            if (not dma) and (not o.is_dma) and o.eng == eng:
                if eng == "pe":
                    return
            p = deps.get(o.key)
            if p is None or o.idx > p.idx:
                deps[o.key] = o

        for r in reads:
            ar = r.arena
            for s in ar.rng(r.lo, r.hi):
                dep(ar.w[s], True)
        for r in writes:
            ar = r.arena
            for s in ar.rng(r.lo, r.hi):
                dep(ar.w[s], False)
                for o in ar.r[s].values():
                    dep(o, False)
        for r in reads:
            ar = r.arena
            for s in ar.rng(r.lo, r.hi):
                ar.r[s][op.key] = op
        for r in writes:
            ar = r.arena
            for s in ar.rng(r.lo, r.hi):
                ar.w[s] = op
                ar.r[s] = {}
        if dma:
            if eng == "pool":
                sid = NDMA - NSWDMA + self.sw_rr % NSWDMA
                self.sw_rr += 1
            else:
                sid = self.dma_rr % (NDMA - NSWDMA)
                self.dma_rr += 1
            prev = self.dma_last[sid]
            if prev is not None:
                deps[prev.key] = prev
            self.dma_last[sid] = op
            self.dma_cnt[sid] += 16
            op.sem_id = sid
            op.val = self.dma_cnt[sid]
        op.deps = deps
        self.ops[eng].append(op)
        return op

    def emit(self, nc, eng_sems, dma_sems, block):
        for e in self.ENGS:
            for op in self.ops[e]:
                for d in op.deps.values():
                    if not d.is_dma:
                        d.inc = True
        for e in self.ENGS:
            c = 0
            for op in self.ops[e]:
                if not op.is_dma and op.inc:
                    c += 1
                    op.val = c

        def run(e, h):
            seen = {}
            for op in self.ops[e]:
                waits = {}
                for d in op.deps.values():
                    k = ("d", d.sem_id) if d.is_dma else ("e", d.eng)
                    if waits.get(k, 0) < d.val:
                        waits[k] = d.val
                for k, v in waits.items():
                    if seen.get(k, 0) >= v:
                        continue
                    seen[k] = v
                    sem = dma_sems[k[1]] if k[0] == "d" else eng_sems[k[1]]
                    h.wait_ge(sem, v)
                ins = op.fn(h)
                if op.is_dma:
                    ins.then_inc(dma_sems[op.sem_id], 16)
                elif op.inc:
                    ins.then_inc(eng_sems[e], 1)
            if e == "sp":
                for sid in range(NDMA):
                    if self.dma_cnt[sid] > 0 and seen.get(("d", sid), 0) < self.dma_cnt[sid]:
                        h.wait_ge(dma_sems[sid], self.dma_cnt[sid])

        @block.tensor
        def _(h):
            run("pe", h)

        @block.scalar
        def _(h):
            run("act", h)

        @block.vector
        def _(h):
            run("dve", h)

        @block.gpsimd
        def _(h):
            run("pool", h)

        @block.sync
        def _(h):
            run("sp", h)


def build(S, debug=False):
    NCH = S // CH
    M = NCH // CPB
    NT = S // 128
    nc = bass.Bass("TRN2", target_bir_lowering=False)

    def din(name, shape):
        return nc.dram_tensor(name, list(shape), F32, kind="ExternalInput").ap()

    xb = din("xb", [S, D])
    xo = din("xo", [M, HALO + CH, D])
    posA = din("posA", [NCH, CH])
    posB = din("posB", [M, CH])
    consts = din("consts", [128, 4])
    thr_d = din("thr", [128, 16])
    qkb_d = din("qkbase", [128, CH])
    ident_d = din("ident", [128, 128])
    memb = din("memb", [MEM, D])
    w_in = din("w_in", [D, 3072])
    w_qsw = din("w_qsw", [D, 512])
    w_ksw = din("w_ksw", [D, 512])
    w_pool = din("w_pool", [4, 128, 128])
    w_out = din("w_out", [D, D])
    w_cq = din("w_cq", [D, D])
    w_ckv = din("w_ckv", [D, 2 * D])
    w_co = din("w_co", [D, D])
    w_up = din("w_up", [D, DFF])
    w_down = din("w_down", [DFF, D])
    gcols_d = din("gcols", [128, 32])
    pscale_d = din("pscale", [128, 4])
    gfin_d = din("gfin", [1, D])
    gsub_d = din("gsub", [1, 128])
    lam_d = din("lam", [1, 256])
    out_d = nc.dram_tensor("out", [M * CH, D], F32, kind="ExternalOutput").ap()
    skind = "ExternalOutput" if debug else "Internal"
    KTs = nc.dram_tensor("KTs", [4, 128, S], BF16, kind=skind).ap()
    Vs = nc.dram_tensor("Vs", [4, 128, NT, 129], BF16, kind=skind).ap()
    slab_defs = [
        ("qq", 1024, [(w_in, 0, 512, 512, 0), (w_qsw, 0, 0, 512, 512)]),
        ("ug", 1024, [(w_in, 0, 0, 512, 0), (w_in, 0, 2048, 512, 512)]),
        ("g2", 512, [(w_in, 0, 2560, 512, 0)]),
        ("out", 1024, [(w_out, 0, 0, 1024, 0)]),
        ("cq", 1024, [(w_cq, 0, 0, 1024, 0)]),
        ("co", 1024, [(w_co, 0, 0, 1024, 0)]),
    ]
    for i in range(4):
        slab_defs.append(("up%d" % i, 1024, [(w_up, 0, 1024 * i, 1024, 0)]))
    for i in range(4):
        slab_defs.append(("dn%d" % i, 1024, [(w_down, 1024 * i, 0, 1024, 0)]))
    NSLAB = len(slab_defs)
    wbf = [nc.dram_tensor("wbf_" + sd[0], [128, 8, sd[1]], BF16, kind="Internal").ap()
           for sd in slab_defs]
    dbg = {}
    if debug:
        dbg["x1"] = nc.dram_tensor("dbg_x1", [M * CH, D], F32, kind="ExternalOutput").ap()
        dbg["mixT"] = nc.dram_tensor("dbg_mixT", [M, 128, 8, CH], BF16, kind="ExternalOutput").ap()
        dbg["x2"] = nc.dram_tensor("dbg_x2", [M * CH, D], F32, kind="ExternalOutput").ap()
        dbg["acc"] = nc.dram_tensor("dbg_acc", [4, 128, 8, 129], F32, kind="ExternalOutput").ap()
        dbg["yt"] = nc.dram_tensor("dbg_yt", [4, 128, CH], BF16, kind="ExternalOutput").ap()
        dbg["q"] = nc.dram_tensor("dbg_q", [128, 4, CH], BF16, kind="ExternalOutput").ap()

    SB_BYTES = 175 * 1024
    sched = Sched()
    sb_arena = Arena(SB_BYTES)
    ps_arena = Arena(8 * 2048)
    dram_arenas = {}

    from contextlib import ExitStack
    with ExitStack() as es:
        big = es.enter_context(nc.sbuf_tensor("big", [128, SB_BYTES], U8))
        banks = [es.enter_context(nc.psum_tensor("bank%d" % i, [128, 512], F32)) for i in range(8)]
        eng_sems = {e: es.enter_context(nc.semaphore("sem_" + e)) for e in Sched.ENGS}
        dma_sems = [es.enter_context(nc.semaphore("dsem%d" % i)) for i in range(NDMA)]

        top = [0]

        def sb(shape, dt):
            esz = ESZ[dt]
            n = 1
            for v in shape[1:]:
                n *= v
            nb = (n * esz + 63) // 64 * 64
            lo = top[0]
            top[0] += nb
            assert top[0] <= SB_BYTES, ("SBUF overflow", top[0])
            ap = big[0:shape[0], lo:lo + n * esz].bitcast(dt)
            if len(shape) == 3:
                ap = ap.rearrange("p (a b) -> p a b", a=shape[1])
            elif len(shape) == 4:
                ap = ap.rearrange("p (a b c) -> p a b c", a=shape[1], b=shape[2])
            return Buf(sb_arena, lo, lo + n * esz, ap, esz)

        def ps(bank, c0=0, c1=512, dt=F32):
            esz = ESZ[dt]
            ap = banks[bank][:, :]
            if dt != F32:
                ap = ap.bitcast(dt)
            b_ = Buf(ps_arena, bank * 2048, (bank + 1) * 2048, ap[:, c0:c1], esz)
            b_.c = lambda a0, a1, b_=b_: Buf(ps_arena, b_.lo, b_.hi, b_.ap[:, a0:a1], esz)
            return b_

        def dr(name, lo, hi):
            if name not in dram_arenas:
                dram_arenas[name] = Arena(1 << 40)
            return Buf(dram_arenas[name], lo, hi, None)

        A = sched.add

        def dma(q, out, in_, reads, writes):
            return A(q, lambda h, o=out, i=in_: h.dma_start(out=o, in_=i), reads, writes, dma=True)

        ident = sb([128, 128], BF16)
        ones = sb([128, 128], BF16)
        cst = sb([128, 4], F32)
        thr = sb([128, 16], F32)
        qkb = sb([128, CH], F32)
        gcols = sb([128, 32], F32)
        pscale = sb([128, 4], F32)
        gfin = sb([128, D], F32)
        gsub = sb([128, 128], F32)
        lamt = sb([128, 256], F32)
        lamw = sb([128, 8], F32)
        neglam = sb([128, 1], F32)
        wpool = sb([128, 4, 128], BF16)
        KcT = sb([128, 8, MEM], BF16)
        Vc = sb([128, 2, D], BF16)
        ssn = sb([128, 16], F32)
        lnv = sb([128, 16], F32)
        rstd = sb([128, 16], F32)

        dma("pool", ident.ap, ident_d, [], [ident])
        dma("sp", cst.ap, consts, [], [cst])
        dma("sp", thr.ap, thr_d, [], [thr])
        dma("sp", qkb.ap, qkb_d, [], [qkb])
        dma("sp", gcols.ap, gcols_d, [], [gcols])
        dma("sp", pscale.ap, pscale_d, [], [pscale])
        dma("sp", gfin.ap, gfin_d[0:1, :].broadcast_to([128, D]), [], [gfin])
        dma("sp", gsub.ap, gsub_d[0:1, :].broadcast_to([128, 128]), [], [gsub])
        dma("sp", lamt.ap, lam_d[0:1, :].broadcast_to([128, 256]), [], [lamt])
        dma("pool", wpool.ap, w_pool.rearrange("g c d -> c g d"), [], [wpool])
        A("dve", lambda h: h.memset(ones.ap, 1.0), [], [ones])
        A("dve", lambda h: h.memset(ssn.ap, 1.0), [], [ssn])
        A("dve", lambda h: h.tensor_tensor(out=lamt.ap[:, 0:64], in0=lamt.ap[:, 0:64],
                                           in1=lamt.ap[:, 64:128], op=ALU.mult), [lamt], [lamt])
        A("dve", lambda h: h.tensor_tensor(out=lamt.ap[:, 128:192], in0=lamt.ap[:, 128:192],
                                           in1=lamt.ap[:, 192:256], op=ALU.mult), [lamt], [lamt])
        A("dve", lambda h: h.reduce_sum(out=lamw.ap[:, 0:1], in_=lamt.ap[:, 0:64], axis=AX.X), [lamt], [lamw])
        A("dve", lambda h: h.reduce_sum(out=lamw.ap[:, 1:2], in_=lamt.ap[:, 128:192], axis=AX.X), [lamt], [lamw])
        A("act", lambda h: h.activation(out=lamw.ap[:, 2:4], in_=lamw.ap[:, 0:2], func=AF.Exp), [lamw], [lamw])
        A("dve", lambda h: h.tensor_tensor(out=lamw.ap[:, 4:5], in0=lamw.ap[:, 3:4], in1=lamw.ap[:, 2:3],
                                           op=ALU.subtract), [lamw], [lamw])
        A("dve", lambda h: h.tensor_scalar(out=neglam.ap, in0=lamw.ap[:, 4:5], scalar1=-LAM_INIT, scalar2=None,
                                           op0=ALU.add), [lamw], [neglam])
        A("dve", lambda h: h.tensor_scalar(out=gsub.ap, in0=gsub.ap, scalar1=1.0 - LAM_INIT, scalar2=None,
                                           op0=ALU.mult), [gsub], [gsub])

        persist_top = top[0]

        def rms_stats(tiles, n_in, jlist):
            for k, (xt, P) in enumerate(tiles):
                junk = jlist[k % len(jlist)]
                A("act", lambda h, xt=xt, P=P, k=k, junk=junk: h.activation(
                    out=junk.ap[0:P, 0:n_in], in_=xt.ap[0:P, :], func=AF.Square,
                    accum_out=ssn.ap[0:P, k:k + 1]), [xt], [junk, ssn.c(k, k + 1)])
            n = len(tiles)
            A("act", lambda h: h.activation(out=lnv.ap[:, 0:n], in_=ssn.ap[:, 0:n], func=AF.Ln,
                                            bias=epsc.ap[:, 0:1], scale=1.0 / n_in), [ssn, epsc], [lnv])
            A("act", lambda h: h.activation(out=rstd.ap[:, 0:n], in_=lnv.ap[:, 0:n], func=AF.Exp, scale=-0.5),
              [lnv], [rstd])

        epsc = sb([128, 1], F32)
        A("dve", lambda h: h.memset(epsc.ap, EPS), [], [epsc])
        halfpi = sb([128, 1], F32)
        A("dve", lambda h: h.memset(halfpi.ap, math.pi / 2.0), [], [halfpi])
        junk_s = [sb([128, 128], BF16) for _ in range(2)]
        persist_top = top[0]

        def norm_pre(xt, P, k, xn, junk, xn_eng="dve"):
            sk, lk, rk = ssn.c(2 * k, 2 * k + 2), lnv.c(2 * k, 2 * k + 2), rstd.c(2 * k, 2 * k + 2)
            A("act", lambda h: h.activation(out=junk.ap[0:P, 0:D], in_=xt.ap[0:P, :], func=AF.Square,
                                            accum_out=sk.ap[0:P, 0:1]), [xt], [junk, sk])
            A("act", lambda h: h.activation(out=lk.ap[0:P, :], in_=sk.ap[0:P, :], func=AF.Ln,
                                            bias=epsc.ap[0:P, 0:1], scale=1.0 / D), [sk, epsc], [lk])
            A("act", lambda h: h.activation(out=rk.ap[0:P, :], in_=lk.ap[0:P, :], func=AF.Exp, scale=-0.5),
              [lk], [rk])
            if xn_eng == "act":
                A("act", lambda h: h.activation(out=xn.ap[0:P, :], in_=xt.ap[0:P, :], func=AF.Copy,
                                                scale=rk.ap[0:P, 0:1]), [xt, rk], [xn])
            else:
                A("dve", lambda h: h.tensor_scalar(out=xn.ap[0:P, :], in0=xt.ap[0:P, :], scalar1=rk.ap[0:P, 0:1],
                                                   scalar2=None, op0=ALU.mult), [xt, rk], [xn])

        def norm_post(xn, P, bank, gofs, dsts, evac):
            for fc in range(8):
                dst = bank.c(fc * 128, fc * 128 + P)
                A("pe", lambda h, fc=fc, dst=dst: h.transpose(
                    out=dst.ap, in_=xn.ap[0:P, fc * 128:(fc + 1) * 128], identity=ident.ap[0:P, 0:P]),
                  [xn, ident], [dst])
            if gofs is None:
                hT_, col0_ = dsts
                src = bank.c(0, 1024)
                wr = hT_dsts(hT_, hT_.ap.shape[2], col0_, P)
                e_ = evac[0]
                if e_ == "act":
                    A("act", lambda h: h.activation(out=hT_.ap[:, :, col0_:col0_ + P],
                                                    in_=src.ap.rearrange("p (a b) -> p a b", a=8)[:, :, 0:P],
                                                    func=AF.Copy), [src], wr)
                else:
                    A("dve", lambda h: h.tensor_copy(out=hT_.ap[:, :, col0_:col0_ + P],
                                                     in_=src.ap.rearrange("p (a b) -> p a b", a=8)[:, :, 0:P]),
                      [src], wr)
                return
            for fc in range(8):
                src = bank.c(fc * 128, fc * 128 + P)
                dstb = dsts[fc]
                g = gcols.c(gofs + fc, gofs + fc + 1)
                if evac[fc % len(evac)] == "act":
                    A("act", lambda h, src=src, dstb=dstb, g=g: h.activation(
                        out=dstb.ap, in_=src.ap, func=AF.Copy, scale=g.ap), [src, g], [dstb])
                else:
                    A("dve", lambda h, src=src, dstb=dstb, g=g: h.tensor_scalar(
                        out=dstb.ap, in0=src.ap, scalar1=g.ap, scalar2=None, op0=ALU.mult), [src, g], [dstb])

        def hT_dsts(hT, W, col0, P):
            return [Buf(sb_arena, hT.lo + (fc * W + col0) * 2, hT.lo + (fc * W + col0 + P) * 2,
                        hT.ap[:, fc, col0:col0 + P], 2) for fc in range(8)]

        def norm_run(tiles, gofs, xn_bufs, tr_banks, dst_list, jlist, evac=("act", "dve"), xn_eng="dve",
                     interleave=True):
            if not interleave:
                for k, (xt, P) in enumerate(tiles):
                    norm_pre(xt, P, k, xn_bufs[k], jlist[k % len(jlist)], xn_eng)
                for k, (xt, P) in enumerate(tiles):
                    norm_post(xn_bufs[k], P, tr_banks[k % len(tr_banks)], gofs, dst_list[k], evac)
                return
            prev = None
            for k, (xt, P) in enumerate(tiles):
                norm_pre(xt, P, k, xn_bufs[k], jlist[k % len(jlist)], xn_eng)
                if prev is not None:
                    norm_post(*prev)
                prev = (xn_bufs[k], P, tr_banks[k % len(tr_banks)], gofs, dst_list[k], (evac[k % len(evac)],))
            norm_post(*prev)

        def rope_tables(pos_row, tb, defer_sin=False):
            dma("sp", tb["pos"].ap, pos_row.broadcast_to([128, CH]), [], [tb["pos"]])
            A("dve", lambda h: h.tensor_scalar(out=tb["ang"].ap, in0=tb["pos"].ap, scalar1=cst.ap[:, 0:1],
                                                scalar2=None, op0=ALU.mult), [tb["pos"], cst], [tb["ang"]])
            A("dve", lambda h: h.tensor_scalar(out=tb["v"].ap, in0=tb["pos"].ap, scalar1=cst.ap[:, 1:2],
                                                scalar2=None, op0=ALU.mult), [tb["pos"], cst], [tb["v"]])
            A("dve", lambda h: h.tensor_copy(out=tb["vi"].ap, in_=tb["v"].ap), [tb["v"]], [tb["vi"]])
            A("dve", lambda h: h.tensor_copy(out=tb["v"].ap, in_=tb["vi"].ap), [tb["vi"]], [tb["v"]])
            A("dve", lambda h: h.scalar_tensor_tensor(out=tb["y"].ap, in0=tb["v"].ap, scalar=-CW1, in1=tb["ang"].ap,
                                                      op0=ALU.mult, op1=ALU.add), [tb["v"], tb["ang"]], [tb["y"]])
            A("dve", lambda h: h.scalar_tensor_tensor(out=tb["ang"].ap, in0=tb["v"].ap, scalar=-CW2, in1=tb["y"].ap,
                                                      op0=ALU.mult, op1=ALU.add), [tb["v"], tb["y"]], [tb["ang"]])
            A("pool", lambda h: h.tensor_scalar(out=tb["y"].ap, in0=tb["ang"].ap, scalar1=PI_LO, scalar2=-PI_LO,
                                                op0=ALU.min, op1=ALU.max), [tb["ang"]], [tb["y"]])
            def sins():
                A("act", lambda h: h.activation(out=tb["sin"].ap, in_=tb["y"].ap, func=AF.Sin, scale=cst.ap[:, 2:3]),
                  [tb["y"], cst], [tb["sin"]])
                A("act", lambda h: h.activation(out=tb["cos"].ap, in_=tb["v"].ap, func=AF.Sin, scale=-1.0,
                                                bias=halfpi.ap[:, 0:1]), [tb["v"], halfpi], [tb["cos"]])
            A("dve", lambda h: h.scalar_tensor_tensor(out=tb["v"].ap, in0=tb["y"].ap, scalar=-1.0, in1=tb["y"].ap,
                                                      op0=ALU.mult, op1=ALU.max),
              [tb["y"]], [tb["v"]])
            if defer_sin:
                return sins
            sins()

        def mm_group(out, pairs, tile_position=None, start=True, stop=True, skip=False):
            n = len(pairs)
            for k, (l, r) in enumerate(pairs):
                kw = {}
                if tile_position is not None:
                    kw["tile_position"] = tile_position
                if skip:
                    kw["skip_group_check"] = True
                A("pe", lambda h, l=l, r=r, k=k, kw=kw: h.matmul(
                    out.ap, lhsT=l.ap, rhs=r.ap, start=(start and k == 0), stop=(stop and k == n - 1), **kw),
                  [l, r], [out])

        def rope_apply(ps_a, ps_b, tb, tmpa, tmpb, dst):
            A("dve", lambda h: h.tensor_tensor(out=tmpa.ap, in0=ps_a.ap, in1=tb["cos"].ap, op=ALU.mult),
              [ps_a, tb["cos"]], [tmpa])
            A("dve", lambda h: h.tensor_tensor(out=tmpb.ap, in0=ps_b.ap, in1=tb["sin"].ap, op=ALU.mult),
              [ps_b, tb["sin"]], [tmpb])
            A("pool", lambda h: h.tensor_tensor(out=dst.ap, in0=tmpa.ap, in1=tmpb.ap, op=ALU.add),
              [tmpa, tmpb], [dst])

        def mk_tables(keys=("pos", "ang", "v", "vi", "y", "cos", "sin")):
            return {k: sb([128, CH], I32 if k == "vi" else F32) for k in keys}

        wbf_bufs = []

        def emit_weight_conversion():
            for s, (nm, ncol, parts) in enumerate(slab_defs):
                b = dr("wbf%d" % s, 0, 1)
                wbf_bufs.append(b)
                for (src, r0, c0, n_, d0) in parts:
                    dma("pool", wbf[s][:, :, d0:d0 + n_],
                        src[r0:r0 + 1024, c0:c0 + n_].rearrange("(k p) c -> p k c", p=128), [], [b])


        top[0] = persist_top
        wkv = sb([128, 8, 1536], BF16)
        wkv_top = top[0]
        dma("pool", wkv.ap[:, :, 0:512], w_in[:, 1024:1536].rearrange("(k p) c -> p k c", p=128), [], [wkv])
        dma("pool", wkv.ap[:, :, 512:1024], w_ksw.rearrange("(k p) c -> p k c", p=128), [], [wkv])
        dma("pool", wkv.ap[:, :, 1024:1536], w_in[:, 1536:2048].rearrange("(k p) c -> p k c", p=128), [], [wkv])
        for kc in range(8):
            wk_ = Buf(sb_arena, wkv.lo + kc * 1536 * 2, wkv.lo + (kc + 1) * 1536 * 2, wkv.ap[:, kc, :], 2)
            A("dve", lambda h, wk_=wk_, kc=kc: h.tensor_scalar(out=wk_.ap, in0=wk_.ap, scalar1=gcols.ap[:, kc:kc + 1],
                                                             scalar2=None, op0=ALU.mult), [wk_, gcols], [wk_])
        wckv = sb([128, 8, 2 * D], BF16)
        memx = sb([128, 2, D], F32)
        memxn = [sb([128, D], BF16) for _ in range(2)]
        memT = sb([128, 8, MEM], BF16)
        dma("pool", wckv.ap[:, :, 0:1024], w_ckv[:, 0:1024].rearrange("(k p) c -> p k c", p=128), [], [wckv])
        dma("pool", wckv.ap[:, :, 1024:2048], w_ckv[:, 1024:2048].rearrange("(k p) c -> p k c", p=128), [], [wckv])
        dma("sp", memx.ap, memb.rearrange("(t p) d -> p t d", p=128), [], [memx])
        emit_weight_conversion()
        junksA = [sb([128, D], BF16) for _ in range(2)]
        trm = [ps(0, 0, 1024, BF16), ps(7, 0, 1024, BF16)]
        norm_run([(memx.i(0), 128), (memx.i(1), 128)], 24, memxn, trm,
                 [hT_dsts(memT, MEM, 0, 128), hT_dsts(memT, MEM, 128, 128)], junksA)
        for oc in range(8):
            o = ps(1 + oc % 2, 0, MEM)
            mm_group(o, [(Buf(sb_arena, wckv.lo, wckv.hi, wckv.ap[:, kc, oc * 128:(oc + 1) * 128]), memT.i(kc))
                         for kc in range(8)])
            dst = KcT.i(oc)
            A("act", lambda h, o=o, dst=dst: h.activation(out=dst.ap, in_=o.ap, func=AF.Copy), [o], [dst])
        for mt in range(2):
            for hf in range(2):
                o = ps(3 + (mt * 2 + hf) % 2)
                mm_group(o, [(Buf(sb_arena, memT.lo, memT.hi, memT.ap[:, kc, mt * 128:(mt + 1) * 128]),
                              Buf(sb_arena, wckv.lo, wckv.hi, wckv.ap[:, kc, 1024 + hf * 512:1024 + (hf + 1) * 512]))
                             for kc in range(8)])
                dst = Buf(sb_arena, Vc.lo, Vc.hi, Vc.ap[:, mt, hf * 512:(hf + 1) * 512])
                A("dve", lambda h, o=o, dst=dst: h.tensor_copy(out=dst.ap, in_=o.ap), [o], [dst])

        top[0] = wkv_top
        xa = [sb([128, 4, D], F32) for _ in range(2)]
        xnA = [sb([128, D], BF16) for _ in range(4)]
        hTA = [sb([128, 8, CH], BF16) for _ in range(2)]
        kst = [[sb([128, CH], BF16) for _ in range(4)] for _ in range(2)]
        vst = [sb([128, 4, 4, 129], BF16) for _ in range(2)]
        tmpA = [sb([128, CH], F32) for _ in range(2)]
        tmpB = [sb([128, CH], F32) for _ in range(2)]
        tabs = [mk_tables() for _ in range(2)]
        for v in vst:
            A("pool", lambda h, v=v: h.memset(v.ap, 1.0), [], [v])
        junksA = [sb([128, D], BF16) for _ in range(4)]
        trA = [ps(0, 0, 1024, BF16), ps(1, 0, 1024, BF16)]

        def wsl(wb, kc, c0, c1):
            return Buf(sb_arena, wb.lo, wb.hi, wb.ap[:, kc, c0:c1])

        def load_xa(c):
            dma("sp", xa[c % 2].ap, xb[c * CH:(c + 1) * CH, :].rearrange("(t p) d -> p t d", p=128), [], [xa[c % 2]])

        def frontA(c):
            sl = c % 2
            sins = rope_tables(posA[c:c + 1, :], tabs[sl], defer_sin=True)
            norm_run([(xa[sl].i(t), 128) for t in range(4)], None, xnA, trA,
                     [(hTA[sl], t * 128) for t in range(4)], junksA, evac=("act",), xn_eng="act",
                     interleave=False)
            sins()

        def backA(c):
            sl = c % 2
            tb = tabs[sl]
            hT = hTA[sl]
            for hh in range(4):
                pk = ps(2 + 2 * (hh % 2))
                pks = ps(3 + 2 * (hh % 2))
                mm_group(pk, [(wsl(wkv, kc, hh * 128, (hh + 1) * 128), hT.i(kc)) for kc in range(8)])
                mm_group(pks, [(wsl(wkv, kc, 512 + hh * 128, 512 + (hh + 1) * 128), hT.i(kc)) for kc in range(8)])
                rope_apply(pk, pks, tb, tmpA[hh % 2], tmpB[hh % 2], kst[sl][hh])
                dma("sp", KTs[hh, :, c * CH:(c + 1) * CH], kst[sl][hh].ap, [kst[sl][hh]],
                    [dr("KT%d" % hh, c * CH, (c + 1) * CH)])
            for t in range(4):
                pv = ps(6 + t % 2)
                mm_group(pv, [(Buf(sb_arena, hT.lo, hT.hi, hT.ap[:, kc, t * 128:(t + 1) * 128]),
                               wsl(wkv, kc, 1024, 1536)) for kc in range(8)])
                dst = Buf(sb_arena, vst[sl].lo, vst[sl].hi, vst[sl].ap[:, t, :, 0:128])
                A("dve", lambda h, pv=pv, dst=dst: h.tensor_copy(
                    out=dst.ap, in_=pv.ap.rearrange("p (a b) -> p a b", a=4)), [pv], [dst])
            for hh in range(4):
                dma("sp", Vs[hh, :, 4 * c:4 * c + 4, :], vst[sl].ap[:, :, hh, :], [vst[sl]],
                    [dr("V%d" % hh, 4 * c, 4 * c + 4)])

        STOP = int(os.environ.get("K_STOP", "0"))
        if STOP == 1:
            pass
        elif os.environ.get("K_NOPIPE_A"):
            load_xa(0)
            for c in range(NCH):
                if c + 1 < NCH:
                    load_xa(c + 1)
                frontA(c)
                backA(c)
        else:
            load_xa(0)
            if NCH > 1:
                load_xa(1)
            frontA(0)
            for c in range(NCH):
                if c + 1 < NCH:
                    frontA(c + 1)
                if c + 2 < NCH:
                    load_xa(c + 2)
                backA(c)

        top[0] = persist_top
        xB = [sb([128, 4, D], F32) for _ in range(2)]
        xH = [sb([HALO, D], F32) for _ in range(2)]
        xnB = [sb([128, D], BF16) for _ in range(5)]
        W2 = HALO + CH
        hTB = sb([128, 8, CH], BF16)
        hTh = sb([128, 8, HALO], BF16)
        gT = sb([128, 8, CH], BF16)
        junkP = [sb([128, D], BF16) for _ in range(2)]
        NSL = 2
        slabs = [sb([128, 8, 1024], BF16) for _ in range(NSL)]
        stage_top = top[0]
        uT = sb([128, 4, W2], F32)
        zT = sb([128, 4, CH], BF16)
        qT = sb([128, 4, CH], BF16)
        tabB = mk_tables(("pos", "cos", "sin"))
        pa = sb([128, W2], F32)
        pb = sb([128, W2], F32)
        invc = sb([128, CH], F32)
        accS = sb([128, 8, 129], F32)
        rl = sb([128, 8], F32)
        rl2n = sb([128, 4], F32)
        ot2 = [sb([128, 128], F32) for _ in range(2)]
        oo = [sb([128, 128], F32) for _ in range(4)]
        yts = [sb([128, CH], BF16) for _ in range(2)]
        r1_top = top[0]
        NKV = 2
        ktb = [sb([128, 1024], BF16) for _ in range(NKV)]
        vtb = [sb([128, 8, 129], BF16) for _ in range(NKV)]
        NPT = 3
        ptb = [[sb([128, CH], BF16) for _ in range(2)] for _ in range(NPT)]
        r1_end = top[0]
        top[0] = r1_top
        tmpA2 = [sb([128, CH], F32)]
        tmpB2 = [sb([128, CH], F32)]
        tabB.update(mk_tables(("ang", "v", "vi", "y")))
        assert top[0] <= r1_end, (top[0], r1_end)
        top[0] = r1_end
        mix_top = top[0]
        top[0] = stage_top
        qcT = sb([128, 8, CH], BF16)
        pcT = [sb([128, 2, CH], BF16) for _ in range(2)]
        lnl = sb([128, CH], F32)
        rlc = [sb([128, CH], F32) for _ in range(2)]
        ocT = sb([128, 8, CH], BF16)
        cross_top = top[0]
        top[0] = stage_top
        aT = sb([128, 32, CH], BF16)
        rtmp = [sb([128, CH], BF16) for _ in range(3)]
        mlp_top = top[0]
        top[0] = max(mix_top, cross_top, mlp_top)
        print("SBUF bytes used (phase B):", top[0])

        slab_emitted = [0]
        slab_seq = [(i, s) for i in range(M) for s in range(NSLAB)]

        def slab_get(n):
            while slab_emitted[0] <= min(n + NSL - 1, len(slab_seq) - 1):
                k = slab_emitted[0]
                s = slab_seq[k][1]
                ncol = slab_defs[s][1]
                dstb = slabs[k % NSL]
                dma("sp", dstb.ap[:, :, 0:ncol], wbf[s], [wbf_bufs[s]], [dstb])
                slab_emitted[0] += 1
            return slabs[n % NSL]

        kv_seq = []
        for i in range(M):
            for hh in range(4):
                for kb in range(2 * (i + 1)):
                    kv_seq.append((i, hh, kb))
        kv_emitted = [0]

        def kv_fetch(k):
            if k >= len(kv_seq) or k < kv_emitted[0]:
                return
            assert k == kv_emitted[0]
            i, hh, kb = kv_seq[k]
            dma("sp", ktb[k % NKV].ap, KTs[hh, :, kb * 1024:(kb + 1) * 1024],
                [dr("KT%d" % hh, kb * 1024, (kb + 1) * 1024)], [ktb[k % NKV]])
            dma("sp", vtb[k % NKV].ap, Vs[hh, :, kb * 8:(kb + 1) * 8, :],
                [dr("V%d" % hh, kb * 8, (kb + 1) * 8)], [vtb[k % NKV]])
            kv_emitted[0] += 1

        def load_xB(i):
            dma("sp", xB[i % 2].ap, xo[i, HALO:HALO + CH, :].rearrange("(t p) d -> p t d", p=128), [], [xB[i % 2]])
            dma("sp", xH[i % 2].ap, xo[i, 0:HALO, :], [], [xH[i % 2]])

        def resid_proj(xcur, srcT, wslab, banks_ring):
            n = 0
            for t in range(4):
                for hf in range(2):
                    o = ps(banks_ring[n % len(banks_ring)])
                    n += 1
                    mm_group(o, [(Buf(sb_arena, srcT.lo, srcT.hi, srcT.ap[:, kc, t * 128:(t + 1) * 128]),
                                  wsl(wslab, kc, hf * 512, (hf + 1) * 512)) for kc in range(8)])
                    xs = Buf(sb_arena, xcur.lo + (t * D + hf * 512) * 4, xcur.lo + (t * D + (hf + 1) * 512) * 4,
                             xcur.ap[:, t, hf * 512:(hf + 1) * 512], 4)
                    A("dve", lambda h, o=o, xs=xs: h.tensor_tensor(out=xs.ap, in0=o.ap, in1=xs.ap, op=ALU.add),
                      [o, xs], [xs])

        trB = [ps(7, 0, 1024, BF16), ps(0, 0, 1024, BF16)]
        GB = [3, 4, 5, 6]

        def resid_norm(xcur, srcT, wslab, banks_ring, gofs):
            prev = None
            n = 0
            for t in range(4):
                for hf in range(2):
                    o = ps(banks_ring[n % len(banks_ring)])
                    n += 1
                    mm_group(o, [(Buf(sb_arena, srcT.lo, srcT.hi, srcT.ap[:, kc, t * 128:(t + 1) * 128]),
                                  wsl(wslab, kc, hf * 512, (hf + 1) * 512)) for kc in range(8)])
                    xs = Buf(sb_arena, xcur.lo + (t * D + hf * 512) * 4, xcur.lo + (t * D + (hf + 1) * 512) * 4,
                             xcur.ap[:, t, hf * 512:(hf + 1) * 512], 4)
                    A("dve", lambda h, o=o, xs=xs: h.tensor_tensor(out=xs.ap, in0=o.ap, in1=xs.ap, op=ALU.add),
                      [o, xs], [xs])
                norm_pre(xcur.i(t), 128, t, xnB[t], junkP[t % 2])
                if prev is not None:
                    norm_post(*prev)
                prev = (xnB[t], 128, trB[t % 2], gofs, hT_dsts(hTB, CH, t * 128, 128), (("act", "dve")[t % 2],))
            norm_post(*prev)
        slab_n = 0
        kv_n = 0
        if STOP == 0:
            load_xB(0)
        STOPB = int(os.environ.get("K_STOPB", "0"))
        for i in range(M if STOP == 0 else 0):
            xs_ = xB[i % 2]
            xh_ = xH[i % 2]
            if i + 1 < M:
                load_xB(i + 1)
            rope_tables(posB[i:i + 1, :], tabB)
            norm_run([(xs_.i(t), 128) for t in range(4)] + [(xh_, HALO)], 0, xnB, trB,
                     [hT_dsts(hTB, CH, t * 128, 128) for t in range(4)] + [[hTh.i(fc) for fc in range(8)]], junkP)
            if STOPB == 3:
                continue
            gb = 0
            w_qq = slab_get(slab_n); slab_n += 1
            for hh in range(4):
                pq = ps(GB[gb % 4]); gb += 1
                pqs = ps(GB[gb % 4]); gb += 1
                mm_group(pq, [(wsl(w_qq, kc, hh * 128, (hh + 1) * 128), hTB.i(kc)) for kc in range(8)])
                mm_group(pqs, [(wsl(w_qq, kc, 512 + hh * 128, 512 + (hh + 1) * 128), hTB.i(kc)) for kc in range(8)])
                rope_apply(pq, pqs, tabB, tmpA2[0], tmpB2[0], qT.i(hh))
            w_ug = slab_get(slab_n); slab_n += 1
            for g in range(4):
                o = ps(GB[gb % 4]); gb += 1
                mm_group(o, [(wsl(w_ug, kc, g * 128, (g + 1) * 128), hTB.i(kc)) for kc in range(8)])
                oh = ps(GB[gb % 4], 0, HALO); gb += 1
                mm_group(oh, [(wsl(w_ug, kc, g * 128, (g + 1) * 128), hTh.i(kc)) for kc in range(8)])
                um = Buf(sb_arena, uT.lo + (g * W2 + HALO) * 4, uT.lo + (g + 1) * W2 * 4, uT.ap[:, g, HALO:W2], 4)
                uh = Buf(sb_arena, uT.lo + g * W2 * 4, uT.lo + (g * W2 + HALO) * 4, uT.ap[:, g, 0:HALO], 4)
                A("act", lambda h, o=o, um=um: h.activation(out=um.ap, in_=o.ap, func=AF.Copy), [o], [um])
                A("dve", lambda h, oh=oh, uh=uh: h.tensor_copy(out=uh.ap, in_=oh.ap), [oh], [uh])
            for gc in range(8):
                if gc == 4:
                    w_g2 = slab_get(slab_n); slab_n += 1
                wg_, c0_ = (w_ug, 512 + gc * 128) if gc < 4 else (w_g2, (gc - 4) * 128)
                o = ps(GB[gb % 4]); gb += 1
                mm_group(o, [(wsl(wg_, kc, c0_, c0_ + 128), hTB.i(kc)) for kc in range(8)])
                dst = gT.i(gc)
                A("act", lambda h, o=o, dst=dst: h.activation(out=dst.ap, in_=o.ap, func=AF.Sigmoid), [o], [dst])
            for g, wdw in enumerate((2, 4, 8, 16)):
                U = uT.i(g)
                cur = U
                step = 1
                tmp_cycle = [pa, pb]
                ti = 0
                while step * 2 < wdw:
                    nxt = tmp_cycle[ti % 2]; ti += 1
                    A("pool", lambda h, cur=cur, nxt=nxt, step=step: h.tensor_tensor(
                        out=nxt.ap[:, 2 * step - 1:W2], in0=cur.ap[:, 2 * step - 1:W2],
                        in1=cur.ap[:, step - 1:W2 - step], op=ALU.add), [cur], [nxt])
                    cur = nxt
                    step *= 2
                nxt = tmp_cycle[ti % 2]; ti += 1
                A("pool", lambda h, cur=cur, nxt=nxt, step=step: h.tensor_tensor(
                    out=nxt.ap[:, HALO:W2], in0=cur.ap[:, HALO:W2], in1=cur.ap[:, HALO - step:W2 - step], op=ALU.add),
                  [cur], [nxt])
                A("dve", lambda h, wdw=wdw: h.tensor_scalar(out=invc.ap, in0=tabB["pos"].ap, scalar1=1.0,
                                                            scalar2=float(wdw), op0=ALU.add, op1=ALU.min),
                  [tabB["pos"]], [invc])
                A("act", lambda h: h.activation(out=invc.ap, in_=invc.ap, func=AF.Ln), [invc], [invc])
                A("act", lambda h: h.activation(out=invc.ap, in_=invc.ap, func=AF.Exp, scale=-1.0), [invc], [invc])
                A("dve", lambda h, nxt=nxt: h.tensor_tensor(out=nxt.ap[:, HALO:W2], in0=nxt.ap[:, HALO:W2],
                                                            in1=invc.ap, op=ALU.mult), [nxt, invc], [nxt])
                zg = zT.i(g)
                A("dve", lambda h, nxt=nxt, U=U, zg=zg: h.tensor_tensor(out=zg.ap, in0=nxt.ap[:, HALO:W2],
                                                                    in1=U.ap[:, HALO:W2], op=ALU.subtract),
                  [nxt, U], [zg])
            pending_pool = [0, 1, 2, 3]
            defer_pool = not os.environ.get("K_NODEFER")
            if STOPB == 4:
                continue

            def pool_out(g):
                o = ps(7)
                zg = zT.i(g)
                mm_group(o, [(Buf(sb_arena, wpool.lo, wpool.hi, wpool.ap[:, g, :]), zg)])
                dst = gT.i(g)
                A("dve", lambda h, o=o, dst=dst, g=g: h.scalar_tensor_tensor(
                    out=dst.ap, in0=o.ap, scalar=pscale.ap[:, g:g + 1], in1=dst.ap, op0=ALU.mult, op1=ALU.mult),
                  [o, pscale, dst], [dst])

            if not defer_pool:
                while pending_pool:
                    pool_out(pending_pool.pop(0))
            n_kt = 16 * (i + 1)
            pending_fin = []
            trF = ps(7, 0, 512, BF16)

            def fin_part2(hp):
                for qt in range(4):
                    yq = yts[hp % 2].c(qt * 128, (qt + 1) * 128)
                    dst = trF.c(qt * 128, (qt + 1) * 128)
                    A("pe", lambda h, yq=yq, dst=dst: h.transpose(out=dst.ap, in_=yq.ap, identity=ident.ap),
                      [yq, ident], [dst])
                dst = gT.i(4 + hp)
                A("dve", lambda h, dst=dst: h.tensor_tensor(out=dst.ap, in0=trF.ap, in1=dst.ap, op=ALU.mult),
                  [trF, dst], [dst])

            for hh in range(4):
                accs = []
                for a in range(8):
                    bk, off = a // 3, (a % 3) * 129
                    accs.append(ps(bk, off, off + 129))
                qh = qT.i(hh)
                q_lo = Buf(sb_arena, qh.lo, qh.hi, qh.ap[0:64, :])
                q_hi = Buf(sb_arena, qh.lo, qh.hi, qh.ap[64:128, :])
                sb_ = [[ps(3, 0, 512), ps(4, 0, 512)], [ps(5, 0, 512), ps(6, 0, 512)]]
                blocks = {}

                kv_base = kv_n
                kv_n += 2 * (i + 1)

                def qk(kt):
                    kb, k8 = kt // 8, kt % 8
                    n_ = kv_base + kb
                    kv_fetch(n_)
                    blocks[kb] = (ktb[n_ % NKV], vtb[n_ % NKV])
                    kt_b, _ = blocks[kb]
                    for cpt in range(2):
                        l = Buf(sb_arena, kt_b.lo, kt_b.hi, kt_b.ap[64 * cpt:64 * (cpt + 1), k8 * 128:(k8 + 1) * 128])
                        r = q_lo if cpt == 0 else q_hi
                        mm_group(sb_[kt % 2][cpt], [(l, r)], tile_position=(64 * cpt, 0))

                qk(0)
                for kt in range(n_kt):
                    if kt + 1 < n_kt:
                        qk(kt + 1)
                    if kt == 6 and pending_fin:
                        fin_part2(pending_fin.pop(0))
                    if kt in (8, 9, 10, 11) and pending_pool:
                        pool_out(pending_pool.pop(0))
                    kb, k8 = kt // 8, kt % 8
                    if k8 == 0:
                        nxt_ = kv_base + kb + 1
                        if nxt_ < len(kv_seq) and kv_seq[nxt_][0] == i:
                            kv_fetch(nxt_)
                    _, v_b = blocks[kb]
                    pts = ptb[kt % NPT]
                    for cpt in range(2):
                        s_ = sb_[kt % 2][cpt]
                        p_ = pts[cpt]
                        A("act", lambda h, s_=s_, p_=p_: h.activation(out=p_.ap, in_=s_.ap, func=AF.Exp, scale=0.125),
                          [s_], [p_])
                        if kt >= n_kt - 16:
                            rr = kt - (n_kt - 16)
                            A("dve", lambda h, p_=p_, rr=rr: h.scalar_tensor_tensor(
                                out=p_.ap, in0=qkb.ap, scalar=thr.ap[:, rr:rr + 1], in1=p_.ap,
                                op0=ALU.is_ge, op1=ALU.mult), [qkb, thr, p_], [p_])
                    vk = Buf(sb_arena, v_b.lo, v_b.hi, v_b.ap[:, k8, :])
                    for cpt in range(2):
                        for qt in range(4):
                            a = cpt * 4 + qt
                            l = Buf(sb_arena, pts[cpt].lo, pts[cpt].hi, pts[cpt].ap[:, qt * 128:(qt + 1) * 128])
                            A("pe", lambda h, a=a, l=l, vk=vk, kt=kt: h.matmul(
                                accs[a].ap, lhsT=l.ap, rhs=vk.ap, start=(kt == 0 and a % 3 == 0),
                                stop=(kt == n_kt - 1), skip_group_check=True), [l, vk], [accs[a]])
                for bk in range(3):
                    na = 3 if bk < 2 else 2
                    src = ps(bk, 0, na * 129)
                    dst = Buf(sb_arena, accS.lo + bk * 3 * 129 * 4, accS.lo + (bk * 3 + na) * 129 * 4,
                              accS.ap[:, bk * 3:bk * 3 + na, :], 4)
                    eng = "act" if bk == 1 else "dve"
                    if eng == "act":
                        A("act", lambda h, src=src, dst=dst, na=na: h.activation(
                            out=dst.ap, in_=src.ap.rearrange("p (a b) -> p a b", a=na), func=AF.Copy), [src], [dst])
                    else:
                        A("dve", lambda h, src=src, dst=dst, na=na: h.tensor_copy(
                            out=dst.ap, in_=src.ap.rearrange("p (a b) -> p a b", a=na)), [src], [dst])
                if debug and i == 0:
                    dma("sp", dbg["acc"][hh], accS.ap, [accS], [])
                    if hh == 0:
                        dma("sp", dbg["q"], qT.ap, [qT], [])
                A("dve", lambda h: h.reciprocal(out=rl.ap, in_=accS.ap[:, :, 128]), [accS], [rl])
                A("dve", lambda h: h.tensor_scalar(out=rl2n.ap, in0=rl.ap[:, 4:8], scalar1=neglam.ap[:, 0:1],
                                                   scalar2=None, op0=ALU.mult), [rl, neglam], [rl2n])
                for qt in range(4):
                    t2 = ot2[qt % 2]
                    o_ = oo[qt]
                    A("dve", lambda h, qt=qt, t2=t2: h.tensor_scalar(
                        out=t2.ap, in0=accS.ap[:, 4 + qt, 0:128], scalar1=rl2n.ap[:, qt:qt + 1], scalar2=None,
                        op0=ALU.mult), [accS, rl2n], [t2])
                    A("dve", lambda h, qt=qt, t2=t2, o_=o_: h.scalar_tensor_tensor(
                        out=o_.ap, in0=accS.ap[:, qt, 0:128], scalar=rl.ap[:, qt:qt + 1], in1=t2.ap,
                        op0=ALU.mult, op1=ALU.add), [accS, rl, t2], [o_])
                rms_stats([(oo[qt], 128) for qt in range(4)], 128, junk_s)
                for qt in range(4):
                    o_ = oo[qt]
                    yq = yts[hh % 2].c(qt * 128, (qt + 1) * 128)
                    A("dve", lambda h, qt=qt, o_=o_, yq=yq: h.scalar_tensor_tensor(
                        out=yq.ap, in0=o_.ap, scalar=rstd.ap[:, qt:qt + 1], in1=gsub.ap, op0=ALU.mult, op1=ALU.mult),
                      [o_, rstd.c(qt, qt + 1), gsub], [yq])

                if debug and i == 0:
                    dma("sp", dbg["yt"][hh], yts[hh % 2].ap, [yts[hh % 2]], [])
                pending_fin.append(hh)
            while pending_pool:
                pool_out(pending_pool.pop(0))
            while pending_fin:
                fin_part2(pending_fin.pop(0))
            if STOPB == 5:
                continue
            if debug:
                dma("sp", dbg["mixT"][i], gT.ap, [gT], [])
            w_o = slab_get(slab_n); slab_n += 1
            resid_norm(xs_, gT, w_o, GB, 8)
            if STOPB == 6:
                continue
            if debug:
                dma("sp", dbg["x1"][i * CH:(i + 1) * CH, :].rearrange("(t p) d -> p t d", p=128), xs_.ap, [xs_], [])
            w_q = slab_get(slab_n); slab_n += 1
            gb = 0
            for oc in range(8):
                o = ps(GB[gb % 4]); gb += 1
                mm_group(o, [(wsl(w_q, kc, oc * 128, (oc + 1) * 128), hTB.i(kc)) for kc in range(8)])
                dst = qcT.i(oc)
                if oc % 2 == 0:
                    A("act", lambda h, o=o, dst=dst: h.activation(out=dst.ap, in_=o.ap, func=AF.Copy), [o], [dst])
                else:
                    A("dve", lambda h, o=o, dst=dst: h.tensor_copy(out=dst.ap, in_=o.ap), [o], [dst])
            for hh in range(4):
                pc = pcT[hh % 2]
                for mt in range(2):
                    o = ps(GB[gb % 4]); gb += 1
                    mm_group(o, [(Buf(sb_arena, KcT.lo, KcT.hi, KcT.ap[:, 2 * hh + dc, mt * 128:(mt + 1) * 128]),
                                  qcT.i(2 * hh + dc)) for dc in range(2)])
                    dst = pc.i(mt)
                    A("act", lambda h, o=o, dst=dst: h.activation(out=dst.ap, in_=o.ap, func=AF.Exp, scale=1.0 / 16.0),
                      [o], [dst])
                ol = ps(GB[gb % 4]); gb += 1
                mm_group(ol, [(ones, pc.i(mt)) for mt in range(2)])
                rc = rlc[hh % 2]
                A("act", lambda h, ol=ol: h.activation(out=lnl.ap, in_=ol.ap, func=AF.Ln), [ol], [lnl])
                A("act", lambda h, rc=rc: h.activation(out=rc.ap, in_=lnl.ap, func=AF.Exp, scale=-1.0), [lnl], [rc])
                for dc in range(2):
                    o = ps(GB[gb % 4]); gb += 1
                    mm_group(o, [(Buf(sb_arena, Vc.lo, Vc.hi, Vc.ap[:, mt, (2 * hh + dc) * 128:(2 * hh + dc + 1) * 128]),
                                  pc.i(mt)) for mt in range(2)])
                    dst = ocT.i(2 * hh + dc)
                    A("dve", lambda h, o=o, dst=dst, rc=rc: h.tensor_tensor(out=dst.ap, in0=o.ap, in1=rc.ap, op=ALU.mult),
                      [o, rc], [dst])
            w_c = slab_get(slab_n); slab_n += 1
            resid_norm(xs_, ocT, w_c, GB, 16)
            if STOPB == 7:
                continue
            if debug:
                dma("sp", dbg["x2"][i * CH:(i + 1) * CH, :].rearrange("(t p) d -> p t d", p=128), xs_.ap, [xs_], [])
            gb = 0
            for su in range(4):
                w_u = slab_get(slab_n); slab_n += 1
                for o8 in range(8):
                    oc = su * 8 + o8
                    o = ps(GB[gb % 4]); gb += 1
                    mm_group(o, [(wsl(w_u, kc, o8 * 128, (o8 + 1) * 128), hTB.i(kc)) for kc in range(8)])
                    rt = rtmp[oc % 3]
                    A("act", lambda h, o=o, rt=rt: h.activation(out=rt.ap, in_=o.ap, func=AF.Relu), [o], [rt])
                    dst = aT.i(oc)
                    A("dve", lambda h, o=o, rt=rt, dst=dst: h.tensor_tensor(out=dst.ap, in0=o.ap, in1=rt.ap, op=ALU.mult),
                      [o, rt], [dst])
            if STOPB == 8:
                continue
            dacc = [ps(b_) for b_ in range(8)]
            for sd in range(4):
                w_d = slab_get(slab_n); slab_n += 1
                for t in range(4):
                    for hf in range(2):
                        o = dacc[t * 2 + hf]
                        for k8 in range(8):
                            kc = sd * 8 + k8
                            l = Buf(sb_arena, aT.lo, aT.hi, aT.ap[:, kc, t * 128:(t + 1) * 128])
                            r = wsl(w_d, k8, hf * 512, (hf + 1) * 512)
                            A("pe", lambda h, o=o, l=l, r=r, kc=kc: h.matmul(
                                o.ap, lhsT=l.ap, rhs=r.ap, start=(kc == 0), stop=(kc == 31)), [l, r], [o])
            for t in range(4):
                for hf in range(2):
                    o = dacc[t * 2 + hf]
                    xsl = Buf(sb_arena, xs_.lo + (t * D + hf * 512) * 4, xs_.lo + (t * D + (hf + 1) * 512) * 4,
                              xs_.ap[:, t, hf * 512:(hf + 1) * 512], 4)
                    A("dve", lambda h, o=o, xsl=xsl: h.tensor_tensor(out=xsl.ap, in0=o.ap, in1=xsl.ap, op=ALU.add),
                      [o, xsl], [xsl])
            if STOPB == 9:
                continue
            rms_stats([(xs_.i(t), 128) for t in range(4)], D, junkP)
            if STOPB == 10:
                continue
            for t in range(4):
                xt = xs_.i(t)
                A("dve", lambda h, xt=xt, t=t: h.scalar_tensor_tensor(
                    out=xt.ap, in0=xt.ap, scalar=rstd.ap[:, t:t + 1], in1=gfin.ap, op0=ALU.mult, op1=ALU.mult),
                  [xt, rstd.c(t, t + 1), gfin], [xt])
            if STOPB == 11:
                continue
            dma("sp", out_d[i * CH:(i + 1) * CH, :].rearrange("(t p) d -> p t d", p=128), xs_.ap, [xs_],
                [dr("out", i, i + 1)])

        with nc.Block() as block:
            sched.emit(nc, eng_sems, dma_sems, block)
    return nc


def make_in_maps(S, x, mem, g_mix, w_in, w_pool, pool_scale, lambda_q1, lambda_k1, lambda_q2, lambda_k2,
                 g_subln, w_out, g_cross, g_mem, w_cq, w_ckv, w_co, g_mlp, w_up, w_down, g_final):
    f = lambda a: np.ascontiguousarray(np.asarray(a, dtype=np.float32))
    NCH = S // CH
    M = NCH // CPB
    w_in0 = f(w_in[0])
    perm = np.arange(512).reshape(8, 2, 32)[:, ::-1, :].reshape(512)
    w_qsw = f(w_in0[:, 512:1024][:, perm])
    w_ksw = f(w_in0[:, 1024:1536][:, perm])
    col = lambda g: np.asarray(g, np.float32).reshape(8, 128).T
    gcols = f(np.concatenate([col(g_mix[0]), col(g_cross[0]), col(g_mlp[0]), col(g_mem[0])], axis=1))
    pscale = f(np.asarray(pool_scale[0], np.float32).reshape(4, 128).T)
    lam = f(np.concatenate([lambda_q1[0], lambda_k1[0], lambda_q2[0], lambda_k2[0]]).reshape(1, 256))
    p = np.arange(128)
    inv_freq = (10000.0 ** (-(np.arange(0, 64, 2, dtype=np.float32)) / np.float32(64))).astype(np.float32)
    invf = inv_freq[p % 32]
    sgn = np.where((p % 64) < 32, -1.0, 1.0).astype(np.float32)
    ident = np.eye(128, dtype=np.float32)
    qkbase = f(np.arange(CH)[None, :] - np.arange(128)[:, None])
    posA = f(np.arange(S).reshape(NCH, CH))
    shared = {
        "posA": posA, "qkbase": qkbase, "ident": ident, "w_in": w_in0, "w_qsw": w_qsw, "w_ksw": w_ksw,
        "w_pool": f(w_pool[0]), "w_out": f(w_out[0]), "w_cq": f(w_cq[0]), "w_ckv": f(w_ckv[0]),
        "w_co": f(w_co[0]), "w_up": f(w_up[0]), "w_down": f(w_down[0]), "gcols": gcols, "pscale": pscale,
        "gfin": f(np.asarray(g_final).reshape(1, D)), "gsub": f(np.asarray(g_subln[0]).reshape(1, 128)), "lam": lam,
    }
    x = np.asarray(x, np.float32)
    mem = np.asarray(mem, np.float32)
    in_maps = []
    for c in range(NCORE):
        b, j = c // CPB, c % CPB
        xo = np.zeros((M, HALO + CH, D), np.float32)
        posB = np.zeros((M, CH), np.float32)
        for m in range(M):
            t0 = (CPB * m + j) * CH
            lo = max(t0 - HALO, 0)
            xo[m, HALO - (t0 - lo):] = x[b, lo:t0 + CH]
            posB[m] = np.arange(t0, t0 + CH)
        consts = np.zeros((128, 4), np.float32)
        consts[:, 0] = invf
        consts[:, 1] = invf / np.float32(TWO_PI)
        consts[:, 2] = sgn
        thr = np.zeros((128, 16), np.float32)
        thr[:, :] = (128.0 * np.arange(16) - 512.0 * j)[None, :]
        d = dict(shared)
        d.update({"xb": f(x[b]), "xo": xo, "posB": posB, "consts": consts, "thr": thr, "memb": f(mem[b])})
        in_maps.append(d)
    return in_maps


_NC_CACHE = {}


def run(S, inputs, debug=False):
    key = (S, debug)
    if key not in _NC_CACHE:
        _NC_CACHE[key] = build(S, debug)
    nc = _NC_CACHE[key]
    in_maps = make_in_maps(S, **inputs)
    res = run_bass_kernel_spmd(nc, in_maps, core_ids=list(range(NCORE)))
    return res


def assemble(S, results, key="out"):
    NCH = S // CH
    M = NCH // CPB
    out = np.zeros((2, S, D), np.float32)
    for c in range(NCORE):
        b, j = c // CPB, c % CPB
        o = np.asarray(results[c][key]).reshape(M, CH, D)
        for m in range(M):
            t0 = (CPB * m + j) * CH
            out[b, t0:t0 + CH] = o[m]
    return out


def kernel(**inputs):
    S = int(np.asarray(inputs["x"]).shape[1])
    res = run(S, inputs)
    return assemble(S, res.results)
```
